# Optimizing a Trainium2 kernel written in Bass

```python
import jax, jax.numpy as jnp
from jax import lax
import numpy as np

D_MODEL = 1024
BATCH = 16
SEQ = 2048
DEPTH = 1

D_FF = 2816
MLSTM_HEADS = 4
MLSTM_QK_DIM = 64
MLSTM_V_DIM = 128
MLSTM_CHUNK = 64
GATE_SOFTCAP = 15.0
ATTN_Q_HEADS = 8
ATTN_KV_HEADS = 2
ATTN_HEAD_DIM = 64
WINDOW = 128
ROPE_DIM = ATTN_HEAD_DIM // 4
ROPE_THETA = 500000.0
NORM_EPS = 1e-6

MLSTM_QK_W = MLSTM_HEADS * MLSTM_QK_DIM
MLSTM_V_W = MLSTM_HEADS * MLSTM_V_DIM
ATTN_Q_W = ATTN_Q_HEADS * ATTN_HEAD_DIM
ATTN_KV_W = ATTN_KV_HEADS * ATTN_HEAD_DIM
IN_WIDTHS = (MLSTM_QK_W, MLSTM_QK_W, MLSTM_V_W, MLSTM_V_W, MLSTM_HEADS, MLSTM_HEADS,
             ATTN_Q_W, ATTN_KV_W, ATTN_KV_W, D_MODEL, D_MODEL)
IN_WIDTH = sum(IN_WIDTHS)

kernel_name = 'hybrid_mlstm_swa_sinks_macaron'


def _rmsnorm(x, g):
    xf = x.astype(jnp.float32)
    y = xf * lax.rsqrt(jnp.mean(xf * xf, axis=-1, keepdims=True) + NORM_EPS)
    return (y * g.astype(jnp.float32)).astype(x.dtype)


def _swiglu(h, w_gate, w_up, w_down):
    return (jax.nn.silu(h @ w_gate) * (h @ w_up)) @ w_down


def _split_cols(a, widths):
    out = []
    start = 0
    for w in widths:
        out.append(a[..., start:start + w])
        start += w
    return out


def _softcap(a):
    return GATE_SOFTCAP * jnp.tanh(a / GATE_SOFTCAP)


def _partial_rope(x, positions):
    half = ROPE_DIM // 2
    inv = ROPE_THETA ** (-jnp.arange(half, dtype=jnp.float32) * 2.0 / ROPE_DIM)
    ang = positions.astype(jnp.float32)[:, None, :, None] * inv
    cos, sin = jnp.cos(ang), jnp.sin(ang)
    xr = x[..., :ROPE_DIM].astype(jnp.float32)
    x1, x2 = xr[..., :half], xr[..., half:]
    rot = jnp.concatenate([x1 * cos - x2 * sin, x2 * cos + x1 * sin], axis=-1).astype(x.dtype)
    return jnp.concatenate([rot, x[..., ROPE_DIM:]], axis=-1)


def _mlstm(q, k, v, i_pre, f_pre):
    B, H, S, dk = q.shape
    dv = v.shape[-1]
    L = MLSTM_CHUNK
    NC = S // L
    q = q * (dk ** -0.5)
    logf = jax.nn.log_sigmoid(f_pre)
    rs = lambda a: a.reshape((B, H, NC, L) + a.shape[3:])
    qc, kc, vc, ic = rs(q), rs(k), rs(v), rs(i_pre)
    b = jnp.cumsum(rs(logf), axis=-1)
    b_last = b[..., -1]
    a = b_last[..., None] - b + ic

    def step(carry, inp):
        C, n, m = carry
        k_c, v_c, a_c, bl = inp
        m_new = jnp.maximum(bl + m, jnp.max(a_c, axis=-1))
        decay = jnp.exp(bl + m - m_new)
        w = jnp.exp(a_c - m_new[..., None])
        C_new = decay[..., None, None] * C + jnp.einsum('bhl,bhlv,bhlk->bhvk', w, v_c, k_c)
        n_new = decay[..., None] * n + jnp.einsum('bhl,bhlk->bhk', w, k_c)
        return (C_new, n_new, m_new), (C, n, m)

    init = (jnp.zeros((B, H, dv, dk), jnp.float32), jnp.zeros((B, H, dk), jnp.float32),
            jnp.zeros((B, H), jnp.float32))
    mv = lambda t: jnp.moveaxis(t, 2, 0)
    _, (C_prev, n_prev, m_prev) = lax.scan(step, init, (mv(kc), mv(vc), mv(a), mv(b_last)))
    C_prev = jnp.moveaxis(C_prev, 0, 2)
    n_prev = jnp.moveaxis(n_prev, 0, 2)
    m_prev = jnp.moveaxis(m_prev, 0, 2)

    causal = jnp.tril(jnp.ones((L, L), dtype=bool))
    logD = b[..., :, None] - b[..., None, :] + ic[..., None, :]
    logD = jnp.where(causal, logD, -jnp.inf)
    inter_log = b + m_prev[..., None]
    m = jnp.maximum(inter_log, jnp.max(logD, axis=-1))
    sqk = jnp.einsum('bhcjk,bhcsk->bhcjs', qc, kc) * jnp.exp(logD - m[..., None])
    inter_scale = jnp.exp(inter_log - m)
    num = (jnp.einsum('bhcjs,bhcsv->bhcjv', sqk, vc)
           + inter_scale[..., None] * jnp.einsum('bhcjk,bhcvk->bhcjv', qc, C_prev))
    den = jnp.sum(sqk, axis=-1) + inter_scale * jnp.einsum('bhcjk,bhck->bhcj', qc, n_prev)
    h = num / jnp.maximum(jnp.abs(den), jnp.exp(-m))[..., None]
    return h.reshape(B, H, S, dv)


def _swa_sinks(q, k, v, sinks):
    B, Hq, S, hd = q.shape
    Hkv = k.shape[1]
    G = Hq // Hkv
    W = WINDOW
    NB = S // W
    qb = q.reshape(B, Hkv, G, NB, W, hd) * (hd ** -0.5)
    pad = lambda t: jnp.pad(t, ((0, 0), (0, 0), (W, 0), (0, 0))).reshape(B, Hkv, NB + 1, W, hd)
    kp, vp = pad(k), pad(v)
    kb = jnp.concatenate([kp[:, :, :-1], kp[:, :, 1:]], axis=3)
    vb = jnp.concatenate([vp[:, :, :-1], vp[:, :, 1:]], axis=3)
    s = jnp.einsum('bkgnqd,bknsd->bkgnqs', qb, kb).astype(jnp.float32)
    qi = jnp.arange(W)[:, None]
    si = jnp.arange(2 * W)[None, :]
    band = (si > qi) & (si <= qi + W)
    valid = (jnp.arange(NB)[:, None, None] > 0) | (si[None] >= W)
    s = jnp.where(band[None] & valid, s, -jnp.inf)
    sink = jnp.broadcast_to(sinks.astype(jnp.float32).reshape(1, Hkv, G, 1, 1, 1),
                            s.shape[:-1] + (1,))
    p = jax.nn.softmax(jnp.concatenate([s, sink], axis=-1), axis=-1)[..., :-1]
    o = jnp.einsum('bkgnqs,bknsd->bkgnqd', p.astype(vb.dtype), vb)
    return o.reshape(B, Hq, S, hd)


def _mixer(h, positions, w_in, b_i, b_f, out_norm_g, sinks, w_br_m, w_br_a, w_out):
    B, S, _ = h.shape
    proj = h @ w_in
    q_m, k_m, v_m, o_m, i_m, f_m, q_a, k_a, v_a, g_m, g_a = _split_cols(proj, IN_WIDTHS)
    heads = lambda t, n: t.reshape(B, S, n, -1).transpose(0, 2, 1, 3)
    f32 = jnp.float32
    i_pre = _softcap((i_m + b_i).astype(f32)).transpose(0, 2, 1)
    f_pre = _softcap((f_m + b_f).astype(f32)).transpose(0, 2, 1)
    hm = _mlstm(heads(q_m, MLSTM_HEADS).astype(f32), heads(k_m, MLSTM_HEADS).astype(f32),
                heads(v_m, MLSTM_HEADS).astype(f32), i_pre, f_pre)
    hm = hm * lax.rsqrt(jnp.mean(hm * hm, axis=-1, keepdims=True) + NORM_EPS)
    hm = hm * out_norm_g.astype(f32).reshape(MLSTM_HEADS, 1, MLSTM_V_DIM)
    hm = hm.transpose(0, 2, 1, 3).reshape(B, S, MLSTM_V_W) * jax.nn.sigmoid(o_m.astype(f32))
    y_m = hm.astype(h.dtype) @ w_br_m
    qa = _partial_rope(heads(q_a, ATTN_Q_HEADS), positions)
    ka = _partial_rope(heads(k_a, ATTN_KV_HEADS), positions)
    va = heads(v_a, ATTN_KV_HEADS)
    oa = _swa_sinks(qa, ka, va, sinks).transpose(0, 2, 1, 3).reshape(B, S, ATTN_Q_W)
    y_a = oa @ w_br_a
    merged = jax.nn.sigmoid(g_m) * y_m + jax.nn.sigmoid(g_a) * y_a
    return merged @ w_out


def setup_inputs(seed: int = 0) -> dict:
    key = jax.random.key(seed)
    ks = jax.random.split(key, 24)
    nrm = lambda k, shape, fan_in: jax.random.normal(k, shape, jnp.float32) * (fan_in ** -0.5)
    gain = lambda k, n: 1.0 + 0.02 * jax.random.normal(k, (DEPTH, n), jnp.float32)
    x = jax.random.normal(ks[0], (BATCH, SEQ, D_MODEL), jnp.float32)
    start = jax.random.randint(ks[1], (BATCH, 1), 0, 4096, dtype=jnp.int32)
    positions = start + jnp.arange(SEQ, dtype=jnp.int32)[None, :]
    return {
        'x': x,
        'positions': positions,
        'ffn1_norm_g': gain(ks[2], D_MODEL),
        'ffn1_w_gate': nrm(ks[3], (DEPTH, D_MODEL, D_FF), D_MODEL),
        'ffn1_w_up': nrm(ks[4], (DEPTH, D_MODEL, D_FF), D_MODEL),
        'ffn1_w_down': nrm(ks[5], (DEPTH, D_FF, D_MODEL), D_FF),
        'mix_norm_g': gain(ks[6], D_MODEL),
        'w_in': nrm(ks[7], (DEPTH, D_MODEL, IN_WIDTH), D_MODEL),
        'mlstm_b_i': 0.1 * jax.random.normal(ks[8], (DEPTH, MLSTM_HEADS), jnp.float32),
        'mlstm_b_f': 3.0 + 0.5 * jax.random.normal(ks[9], (DEPTH, MLSTM_HEADS), jnp.float32),
        'mlstm_out_norm_g': gain(ks[10], MLSTM_V_W),
        'attn_sinks': 0.5 * jax.random.normal(ks[11], (DEPTH, ATTN_Q_HEADS), jnp.float32),
        'w_branch_mlstm': nrm(ks[12], (DEPTH, MLSTM_V_W, D_MODEL), MLSTM_V_W),
        'w_branch_attn': nrm(ks[13], (DEPTH, ATTN_Q_W, D_MODEL), ATTN_Q_W),
        'w_out': nrm(ks[14], (DEPTH, D_MODEL, D_MODEL), D_MODEL),
        'ffn2_norm_g': gain(ks[15], D_MODEL),
        'ffn2_w_gate': nrm(ks[16], (DEPTH, D_MODEL, D_FF), D_MODEL),
        'ffn2_w_up': nrm(ks[17], (DEPTH, D_MODEL, D_FF), D_MODEL),
        'ffn2_w_down': nrm(ks[18], (DEPTH, D_FF, D_MODEL), D_FF),
        'final_norm_g': 1.0 + 0.02 * jax.random.normal(ks[19], (D_MODEL,), jnp.float32),
    }


def reference(x, positions, ffn1_norm_g, ffn1_w_gate, ffn1_w_up, ffn1_w_down, mix_norm_g,
              w_in, mlstm_b_i, mlstm_b_f, mlstm_out_norm_g, attn_sinks, w_branch_mlstm,
              w_branch_attn, w_out, ffn2_norm_g, ffn2_w_gate, ffn2_w_up, ffn2_w_down,
              final_norm_g):
    for l in range(DEPTH):
        h = _rmsnorm(x, ffn1_norm_g[l])
        x = x + 0.5 * _swiglu(h, ffn1_w_gate[l], ffn1_w_up[l], ffn1_w_down[l])
        h = _rmsnorm(x, mix_norm_g[l])
        x = x + _mixer(h, positions, w_in[l], mlstm_b_i[l], mlstm_b_f[l], mlstm_out_norm_g[l],
                       attn_sinks[l], w_branch_mlstm[l], w_branch_attn[l], w_out[l])
        h = _rmsnorm(x, ffn2_norm_g[l])
        x = x + 0.5 * _swiglu(h, ffn2_w_gate[l], ffn2_w_up[l], ffn2_w_down[l])
    return _rmsnorm(x, final_norm_g)
```

```python
import numpy as np
import concourse.bass as bass
import concourse.mybir as mybir
from contextlib import ExitStack

F32 = mybir.dt.float32; BF16 = mybir.dt.bfloat16; I32 = mybir.dt.int32
AF = mybir.ActivationFunctionType; ALU = mybir.AluOpType; AX = mybir.AxisListType

class Prog:
    ENGS = ('pe', 'act', 'dve', 'pool', 'sp')
    def __init__(self, nc):
        self.nc = nc
        self.ops = []
    def add(self, eng, fn, r=(), w=(), dma=None):
        self.ops.append([eng, fn, tuple(r), tuple(w), dma])
    def barrier(self, tag):
        for e in self.ENGS:
            self.add(e, lambda eng: eng.nop(), r=(), w=[('bar', tag, e)])
        for e in self.ENGS:
            self.add(e, lambda eng: eng.nop(), r=[('bar', tag, e2) for e2 in self.ENGS if e2 != e], w=())
    def build(self, stack):
        nc = self.nc
        ops = self.ops
        n = len(ops)
        last_w = {}
        readers = {}
        deps = [None] * n
        needs_sig = [False] * n
        for i, (eng, fn, r, w, dma) in enumerate(ops):
            d = {}
            def add_dep(j, kind):
                if j == i:
                    return
                e2, _, _, _, dma2 = ops[j]
                if e2 == eng and dma2 is None and dma is None:
                    if kind != 'raw' or eng == 'pe':
                        return
                d[j] = True
            rk = list(r)
            wk = list(w)
            if dma is not None:
                wk.append(('dmasem', dma))
            for k in rk:
                if k in last_w:
                    add_dep(last_w[k], 'raw')
            for k in wk:
                if k in last_w:
                    add_dep(last_w[k], 'waw')
                for j in readers.get(k, ()):
                    add_dep(j, 'war')
            for k in rk:
                readers.setdefault(k, []).append(i)
            for k in wk:
                last_w[k] = i
                readers[k] = []
            deps[i] = list(d.keys())
            for j in deps[i]:
                needs_sig[j] = True
        sig = [None] * n
        cnt = {}
        semnames = set()
        for i, (eng, fn, r, w, dma) in enumerate(ops):
            if not needs_sig[i] and dma is None:
                continue
            name = ('dma_' + dma) if dma is not None else ('eng_' + eng)
            inc = 16 if dma is not None else 1
            cnt[name] = cnt.get(name, 0) + inc
            sig[i] = (name, cnt[name], inc)
            semnames.add(name)
        sems = {}
        for name in sorted(semnames):
            sems[name] = stack.enter_context(nc.semaphore(name))
        self.sem_counts = cnt
        per_eng = {e: [] for e in self.ENGS}
        waited = {e: {} for e in self.ENGS}
        nwaits = 0
        for i, (eng, fn, r, w, dma) in enumerate(ops):
            need = {}
            for j in deps[i]:
                name, val, _ = sig[j]
                if need.get(name, 0) < val:
                    need[name] = val
            ws = []
            for name, val in need.items():
                if waited[eng].get(name, 0) < val:
                    waited[eng][name] = val
                    ws.append((sems[name], val))
                    nwaits += 1
            s = None
            if sig[i] is not None:
                s = (sems[sig[i][0]], sig[i][2])
            per_eng[eng].append((ws, fn, s))
        self.nwaits = nwaits
        block = stack.enter_context(nc.Block())
        def mk(lst):
            def body(eng):
                for ws, fn, s in lst:
                    for sem, val in ws:
                        eng.wait_ge(sem, val)
                    if fn is None:
                        assert s is None
                        continue
                    ins = fn(eng)
                    if s is not None:
                        ins.then_inc(s[0], s[1])
            return body
        block.tensor(mk(per_eng['pe']))
        block.scalar(mk(per_eng['act']))
        block.vector(mk(per_eng['dve']))
        block.gpsimd(mk(per_eng['pool']))
        block.sync(mk(per_eng['sp']))


import numpy as np
import ml_dtypes
import concourse.bass as bass
import concourse.mybir as mybir
from contextlib import ExitStack
from concourse.bass_utils import run_bass_kernel_spmd

F32 = mybir.dt.float32; BF16 = mybir.dt.bfloat16; I32 = mybir.dt.int32
AF = mybir.ActivationFunctionType; ALU = mybir.AluOpType; AX = mybir.AxisListType
EPS = 1e-6

FULL = dict(DM=1024, DFF=2816, S=2048, NSEQ=2, ST=1024, NCORES=8)

WNAMES = ['ffn1_w_gate', 'ffn1_w_up', 'ffn1_w_down', 'w_in', 'w_branch_mlstm', 'w_branch_attn', 'w_out',
          'ffn2_w_gate', 'ffn2_w_up', 'ffn2_w_down']


def build_program(cfg, phases=('ffn1', 'mix', 'ffn2')):
    DM, DFF, S, NSEQ, ST = cfg['DM'], cfg['DFF'], cfg['S'], cfg['NSEQ'], cfg['ST']
    KC = DM // 128
    FC = DFF // 128
    TT = ST // 128
    MTOK = min(512, ST)
    NM = ST // MTOK
    TPM = MTOK // 128
    NTOK = NSEQ * S
    NST = NTOK // ST
    STPS = S // ST
    NTILES = NTOK // 128
    INW = 2312 + 2 * DM
    HW = min(512, DM)
    NH = DM // HW

    nc = bass.Bass("TRN2", target_bir_lowering=False)
    st = ExitStack()
    P = Prog(nc)

    def din(name, shape, dt=F32):
        return nc.dram_tensor(name, list(shape), dt, kind="ExternalInput").ap()
    x_d = din('x', [NTOK, DM])
    pos_d = din('posT', [128, NTILES], I32)
    W = {}
    W['ffn1_w_gate'] = din('ffn1_w_gate', [DM, DFF]); W['ffn1_w_up'] = din('ffn1_w_up', [DM, DFF]); W['ffn1_w_down'] = din('ffn1_w_down', [DFF, DM])
    W['ffn2_w_gate'] = din('ffn2_w_gate', [DM, DFF]); W['ffn2_w_up'] = din('ffn2_w_up', [DM, DFF]); W['ffn2_w_down'] = din('ffn2_w_down', [DFF, DM])
    W['w_in'] = din('w_in', [DM, INW]); W['w_branch_mlstm'] = din('w_branch_mlstm', [512, DM]); W['w_branch_attn'] = din('w_branch_attn', [512, DM])
    W['w_out'] = din('w_out', [DM, DM])
    g1_d = din('ffn1_norm_gT', [128, KC]); gm_d = din('mix_norm_gT', [128, KC]); g2_d = din('ffn2_norm_gT', [128, KC])
    fg_d = din('final_norm_g', [DM])
    bi_d = din('mlstm_b_i', [4, 1]); bf_d = din('mlstm_b_f', [4, 1])
    og_d = din('mlstm_out_norm_g', [512]); sink_d = din('attn_sinks', [8])
    identb_d = din('ident_bf', [128, 128], BF16); identf_d = din('ident_f', [128, 128])
    sel_d = din('sel', [4, 512]); maskneg_d = din('maskneg', [128, 128])
    mcur_d = din('mask_cur', [128, 128], BF16); mprev_d = din('mask_prev', [128, 128], BF16)
    invf_d = din('inv_freq', [8])
    out_d = nc.dram_tensor('out', [NTOK, DM], F32, kind="ExternalOutput").ap()

    def sb(name, shape, dt=F32):
        return st.enter_context(nc.sbuf_tensor(name, list(shape), dt))
    X = sb('X', [128, TT, DM])
    HT = sb('HT', [128, KC, ST], BF16)
    SLAB_E = 12352
    SLAB = [sb('SLAB%d' % i, [128, SLAB_E], BF16) for i in range(2)]
    Y = [sb('Y%d' % i, [128, DM]) for i in range(2)]
    FG = sb('FG', [128, DM])
    GT = {1: sb('G1', [128, KC]), 'm': sb('GM', [128, KC]), 2: sb('G2', [128, KC])}
    IDB = sb('IDB', [128, 128], BF16)
    SS = sb('SS', [128, TT]); RS = sb('RS', [128, TT])
    JUNK = sb('JUNK', [128, DM], BF16)
    HB = [sb('HB%d' % i, [128, DM], BF16) for i in range(2)]
    HM = sb('HM', [128, 4, ST], BF16)
    OA = sb('OA', [128, 4, ST], BF16)
    IDF = sb('IDF', [128, 128])
    SEL = sb('SEL', [4, 512])
    MASKNEG = sb('MASKNEG', [128, 128])
    MCUR = sb('MCUR', [128, 128], BF16); MPREV = sb('MPREV', [128, 128], BF16)
    OG = sb('OG', [128, 512])
    ESINK = sb('ESINK', [128, 8])
    BIS = sb('BIS', [4, 1]); BFS = sb('BFS', [4, 1])
    COS = sb('COS', [128, NTILES, 8]); SIN = sb('SIN', [128, NTILES, 8])
    SST = sb('SST', [64, 4, 129]); SBF = sb('SBF', [64, 4, 129], BF16)
    RCOL = sb('RCOL', [128, 4]); BC = sb('BC', [4, 1]); MC = sb('MC', [4, 1])
    V1A = [sb('V1A%d' % i, [128, 2, 65], BF16) for i in range(2)]
    KTA = [sb('KTA%d' % i, [64, 2, 128], BF16) for i in range(2)]
    DUMMY = sb('DUMMY', [128, 8])
    AE = 25600
    ARENA = sb('ARENA', [128, AE], BF16)
    arena_names = set()
    aoff = [0]
    def areset():
        aoff[0] = 0
    def av(name, parts, free, dt=F32):
        n = int(np.prod(free))
        ne = n * (2 if dt == F32 else 1)
        ne = (ne + 15) // 16 * 16
        a = aoff[0]; aoff[0] += ne
        assert aoff[0] <= AE, (name, aoff[0])
        v = ARENA[0:parts, a:a + n * (2 if dt == F32 else 1)]
        if dt == F32:
            v = v.bitcast(F32)
        if len(free) == 2:
            v = v.rearrange("p (a b) -> p a b", a=free[0])
        arena_names.add(name)
        return v
    _add = P.add
    def padd(eng, fn, r=(), w=(), dma=None):
        r = list(r)
        for k in list(r) + list(w):
            b = k[0] if isinstance(k, tuple) else k
            if b in arena_names:
                r.append('A'); break
        _add(eng, fn, r, w, dma)
    P.add = padd
    bar_ctr = [0]
    def abarrier():
        P.add('dve', lambda e: e.memset(DUMMY[:], 0.0), w=['A', 'DUMMY'])
        areset()
    def ffn_views():
        abarrier()
        SGv = [av('SG', 128, [MTOK]) for i in range(2)]
        ACv = [av('ACTT', 128, [4, MTOK], BF16) for i in range(2)]
        return SGv, ACv

    PS = [st.enter_context(nc.psum_tensor('PS%d' % i, [128, 512], F32)) for i in range(6)]
    PSB = [st.enter_context(nc.psum_tensor('PSB%d' % i, [128, 1024], BF16)) for i in range(2)]

    cl = 0
    def cload(dst, src, key):
        nonlocal cl
        P.add('sp', lambda e: e.dma_start(out=dst, in_=src), w=[key], dma='c%d' % cl)
        cl += 1
    cload(IDB[:], identb_d, 'IDB')
    cload(FG[:], fg_d.partition_broadcast(128), 'FG')
    cload(GT[1][:], g1_d, 'GT1'); cload(GT['m'][:], gm_d, 'GTm'); cload(GT[2][:], g2_d, 'GT2')

    slab_ctr = [0]
    def load_slab(parts):
        i = slab_ctr[0]; slab_ctr[0] += 1
        buf = SLAB[i % 2]
        for pi, (mk_dst, src) in enumerate(parts):
            dst = mk_dst(buf)
            wk = [('slab', i % 2, q) for q in range(3)] if pi == 0 else [('slab', i % 2, pi)]
            P.add('pool', lambda e, dst=dst, src=src: e.dma_start(out=dst, in_=src),
                  w=wk, dma='slab%d_%d' % (i % 2, pi))
        return buf, i % 2

    def load_x(T):
        for t in range(TT):
            r0 = T * ST + t * 128
            P.add('sp', lambda e, t=t, r0=r0: e.dma_start(out=X[:, t, :], in_=x_d[r0:r0 + 128, :]),
                  w=[('X', t)], dma='x%d' % (t % 4))

    def rstd_batch():
        for t in range(TT):
            P.add('act', lambda e, t=t: e.activation(out=JUNK[:], in_=X[:, t, :], func=AF.Square, accum_out=SS[:, t:t + 1]),
                  r=[('X', t)], w=['JUNK', ('SS', t)])
        P.add('dve', lambda e: e.tensor_scalar(RS[:], SS[:], 1.0 / DM, EPS, op0=ALU.mult, op1=ALU.add),
              r=[('SS', t) for t in range(TT)], w=['RS'])
        P.add('act', lambda e: e.activation(out=RS[:], in_=RS[:], func=AF.Ln), r=['RS'], w=['RS'])
        P.add('act', lambda e: e.activation(out=RS[:], in_=RS[:], func=AF.Exp, scale=-0.5), r=['RS'], w=['RS'])

    def norm_to_HT(gkey):
        rstd_batch()
        G = GT[gkey]
        for t in range(TT):
            hb = HB[t % 2]
            P.add('act', lambda e, t=t, hb=hb: e.activation(out=hb[:], in_=X[:, t, :], func=AF.Copy, scale=RS[:, t:t + 1]),
                  r=[('X', t), 'RS'], w=[('HB', t % 2)])
            tp = PSB[0]
            for k in range(KC):
                P.add('pe', lambda e, k=k, hb=hb, tp=tp: e.transpose(tp[:, k * 128:(k + 1) * 128], hb[:, k * 128:(k + 1) * 128], IDB[:]),
                      r=[('HB', t % 2), 'IDB'], w=[('PSB', 0)])
            P.add('dve', lambda e, t=t, tp=tp: e.tensor_tensor(
                out=HT[:, :, t * 128:(t + 1) * 128],
                in0=tp[:, 0:KC * 128].rearrange("p (k c) -> p k c", k=KC),
                in1=G[:].unsqueeze(2).to_broadcast([128, KC, 128]), op=ALU.mult),
                r=[('PSB', 0), 'GT%s' % gkey], w=[('HT', t)])

    cnt = dict(gu=0, o=0, act=0)
    def ffn(f):
        wg, wu, wd = W['ffn%d_w_gate' % f], W['ffn%d_w_up' % f], W['ffn%d_w_down' % f]
        norm_to_HT(f)
        SG, ACTT = ffn_views()
        groups = []
        c = 0
        while c < FC:
            n = min(4, FC - c)
            groups.append((c, n)); c += n
        for (c0, n) in groups:
            WGO, WUO, WDO = 0, KC * 512, 2 * KC * 512
            parts = [
                (lambda b, n=n: b[:, WGO:WGO + KC * n * 128].rearrange("p (k c) -> p k c", k=KC),
                 wg[:, c0 * 128:(c0 + n) * 128].rearrange("(k p) c -> p k c", p=128)),
                (lambda b, n=n: b[:, WUO:WUO + KC * n * 128].rearrange("p (k c) -> p k c", k=KC),
                 wu[:, c0 * 128:(c0 + n) * 128].rearrange("(k p) c -> p k c", p=128)),
                (lambda b, n=n: b[:, WDO:WDO + n * DM].rearrange("p (c d) -> p c d", c=n),
                 wd[c0 * 128:(c0 + n) * 128, :].rearrange("(c p) d -> p c d", p=128)),
            ]
            buf, bi = load_slab(parts)
            Wg = buf[:, WGO:WGO + KC * n * 128].rearrange("p (k c) -> p k c", k=KC)
            Wu = buf[:, WUO:WUO + KC * n * 128].rearrange("p (k c) -> p k c", k=KC)
            Wd = buf[:, WDO:WDO + n * DM].rearrange("p (c d) -> p c d", c=n)
            for m in range(NM):
                ab = cnt['act'] % 2; cnt['act'] += 1
                at = ACTT[ab]
                htk = [('HT', m * TPM + j) for j in range(TPM)]
                for ci in range(n):
                    gb = cnt['gu'] % 2; cnt['gu'] += 1
                    Gp, Up = PS[2 * gb], PS[2 * gb + 1]
                    for (Wx, Pp, pi, bk) in ((Wg, Gp, 0, 2 * gb), (Wu, Up, 1, 2 * gb + 1)):
                        for k in range(KC):
                            P.add('pe', lambda e, Wx=Wx, Pp=Pp, k=k, ci=ci, m=m: e.matmul(
                                Pp[:, 0:MTOK], Wx[:, k, ci * 128:(ci + 1) * 128], HT[:, k, m * MTOK:(m + 1) * MTOK],
                                start=(k == 0), stop=(k == KC - 1)),
                                r=[('slab', bi, pi)] + htk, w=[('PS', bk)])
                    sg = SG[gb]
                    P.add('act', lambda e, sg=sg, Gp=Gp: e.activation(out=sg[:], in_=Gp[:, 0:MTOK], func=AF.Silu),
                          r=[('PS', 2 * gb)], w=[('SG', gb)])
                    P.add('dve', lambda e, sg=sg, Up=Up, at=at, ci=ci: e.tensor_tensor(out=at[:, ci, :], in0=Up[:, 0:MTOK], in1=sg[:], op=ALU.mult),
                          r=[('PS', 2 * gb + 1), ('SG', gb)], w=[('ACTT', ab, ci)])
                for j in range(TPM):
                    t = m * TPM + j
                    for h in range(NH):
                        ob = cnt['o'] % 2; cnt['o'] += 1
                        Op = PS[4 + ob]
                        for ci in range(n):
                            P.add('pe', lambda e, Op=Op, at=at, ci=ci, j=j, h=h, n=n, Wd=Wd: e.matmul(
                                Op[:, 0:HW], at[:, ci, j * 128:(j + 1) * 128], Wd[:, ci, h * HW:(h + 1) * HW],
                                start=(ci == 0), stop=(ci == n - 1)),
                                r=[('ACTT', ab, ci), ('slab', bi, 2)], w=[('PS', 4 + ob)])
                        P.add('dve', lambda e, Op=Op, t=t, h=h: e.scalar_tensor_tensor(
                            out=X[:, t, h * HW:(h + 1) * HW], in0=Op[:, 0:HW], scalar=0.5, in1=X[:, t, h * HW:(h + 1) * HW],
                            op0=ALU.mult, op1=ALU.add),
                            r=[('PS', 4 + ob), ('X', t)], w=[('X', t)])

    def final_out(T):
        rstd_batch()
        for t in range(TT):
            y = Y[t % 2]
            r0 = T * ST + t * 128
            P.add('dve', lambda e, t=t, y=y: e.scalar_tensor_tensor(out=y[:], in0=X[:, t, :], scalar=RS[:, t:t + 1], in1=FG[:],
                                                                    op0=ALU.mult, op1=ALU.mult),
                  r=[('X', t), 'RS', 'FG'], w=[('Y', t % 2)])
            P.add('sp', lambda e, y=y, r0=r0: e.dma_start(out=out_d[r0:r0 + 128, :], in_=y[:]),
                  r=[('Y', t % 2)], dma='o%d' % (t % 2))

    import math
    cload(IDF[:], identf_d, 'IDF'); cload(SEL[:], sel_d, 'SEL'); cload(MASKNEG[:], maskneg_d, 'MASKNEG')
    cload(MCUR[:], mcur_d, 'MCUR'); cload(MPREV[:], mprev_d, 'MPREV')
    cload(OG[:], og_d.partition_broadcast(128), 'OG')
    cload(ESINK[:], sink_d.partition_broadcast(128), 'ESINK')
    cload(BIS[:], bi_d, 'BIS'); cload(BFS[:], bf_d, 'BFS')
    POSI = sb('POSI', [128, NTILES], I32); POSF = sb('POSF', [128, NTILES]); INVF = sb('INVF', [128, 8])
    cload(POSI[:], pos_d, 'POSI'); cload(INVF[:], invf_d.partition_broadcast(128), 'INVF')
    P.add('dve', lambda e: e.tensor_scalar(BIS[:], BIS[:], 1.0 / 15.0, None, op0=ALU.mult), r=['BIS'], w=['BIS'])
    P.add('dve', lambda e: e.tensor_scalar(BFS[:], BFS[:], 1.0 / 15.0, None, op0=ALU.mult), r=['BFS'], w=['BFS'])
    P.add('act', lambda e: e.activation(out=ESINK[:], in_=ESINK[:], func=AF.Exp), r=['ESINK'], w=['ESINK'])
    for i in range(2):
        P.add('dve', lambda e, i=i: e.memset(V1A[i][:], 1.0), w=[('V1A', i)])
        P.add('dve', lambda e, i=i: e.memset(KTA[i][:], 0.0), w=[('KTA', i)])
    NA = NTILES * 8
    ANG = sb('ANG', [128, NTILES, 8]); RR_ = sb('RRa', [128, NA]); KI = sb('KI', [128, NA], I32); KF = sb('KF', [128, NA]); MM_ = sb('MMa', [128, NA])
    P.add('dve', lambda e: e.tensor_copy(POSF[:], POSI[:]), r=['POSI'], w=['POSF'])
    P.add('dve', lambda e: e.tensor_tensor(out=ANG[:], in0=POSF[:].unsqueeze(2).to_broadcast([128, NTILES, 8]),
                                           in1=INVF[:].unsqueeze(1).to_broadcast([128, NTILES, 8]), op=ALU.mult), r=['POSF', 'INVF'], w=['ANG'])
    TWO_PI = 2.0 * math.pi
    C1 = 6.28125; C2 = TWO_PI - C1
    def sin_of(dst, shift, tag):
        angf = ANG[:].rearrange("p a b -> p (a b)")
        P.add('dve', lambda e: e.tensor_scalar(RR_[:], angf, shift, None, op0=ALU.add), r=['ANG'], w=['RRa'])
        P.add('dve', lambda e: e.tensor_scalar(KF[:], RR_[:], 1.0 / TWO_PI, None, op0=ALU.mult), r=['RRa'], w=['KF'])
        P.add('dve', lambda e: e.tensor_copy(KI[:], KF[:]), r=['KF'], w=['KI'])
        P.add('dve', lambda e: e.tensor_copy(KF[:], KI[:]), r=['KI'], w=['KF'])
        P.add('dve', lambda e: e.scalar_tensor_tensor(out=RR_[:], in0=KF[:], scalar=-C1, in1=RR_[:], op0=ALU.mult, op1=ALU.add), r=['KF', 'RRa'], w=['RRa'])
        P.add('dve', lambda e: e.scalar_tensor_tensor(out=RR_[:], in0=KF[:], scalar=-C2, in1=RR_[:], op0=ALU.mult, op1=ALU.add), r=['KF', 'RRa'], w=['RRa'])
        P.add('dve', lambda e: e.tensor_scalar(MM_[:], RR_[:], math.pi, -TWO_PI, op0=ALU.is_gt, op1=ALU.mult), r=['RRa'], w=['MMa'])
        P.add('dve', lambda e: e.tensor_tensor(out=RR_[:], in0=RR_[:], in1=MM_[:], op=ALU.add), r=['RRa', 'MMa'], w=['RRa'])
        P.add('dve', lambda e: e.tensor_scalar(MM_[:], RR_[:], -math.pi, TWO_PI, op0=ALU.is_lt, op1=ALU.mult), r=['RRa'], w=['MMa'])
        P.add('dve', lambda e: e.tensor_tensor(out=RR_[:], in0=RR_[:], in1=MM_[:], op=ALU.add), r=['RRa', 'MMa'], w=['RRa'])
        P.add('dve', lambda e: e.tensor_scalar(RR_[:], RR_[:], math.pi, -math.pi, op0=ALU.min, op1=ALU.max), r=['RRa'], w=['RRa'])
        P.add('act', lambda e: e.activation(out=dst[:].rearrange("p a b -> p (a b)"), in_=RR_[:], func=AF.Sin), r=['RRa'], w=[tag])
    sin_of(SIN, 0.0, 'SIN')
    sin_of(COS, math.pi / 2.0, 'COS')

    W_IN = W['w_in']
    pcnt = dict(qk=0, g=0, o=0)

    def mixer(T):
        seq_first = (T % STPS == 0)
        norm_to_HT('m')
        abarrier()
        QT = av('QT', 64, [4, MTOK], BF16); KT = av('KT', 64, [4, MTOK], BF16)
        TI = av('TI', 4, [MTOK]); TF = av('TF', 4, [MTOK]); NEGB = av('NEGB', 4, [MTOK]); GG = av('GG', 4, [MTOK])
        MMs = av('MMs', 4, [MTOK]); NM_ = av('NMs', 4, [MTOK]); ONES4 = av('ONES4', 4, [MTOK]); ZEROS4 = av('ZEROS4', 4, [MTOK])
        GCOL = av('GCOL', 128, [4]); ENMs = [av('ENM', 128, [4]) for _ in range(2)]; NRs = [av('NR', 128, [4, 129]) for _ in range(2)]; RNEW = av('RNEW', 128, [4]); WCOL = av('WCOL', 128, [4])
        DEC = av('DEC', 64, [4]); DEN = av('DEN', 128, [4]); NEGD = av('NEGD', 128, [4]); RRm = av('RRm', 128, [4]); SS2 = av('SS2', 128, [4]); RS2 = av('RS2', 128, [4])
        ARG = av('ARG', 128, [4, 128]); DTm = av('DTm', 128, [4, 128]); PD = av('PD', 128, [4, 128], BF16)
        ARG2 = av('ARG2', 64, [4, 128]); EE = av('EE', 64, [4, 128]); QP = av('QP', 64, [4, 128], BF16)
        KP = av('KP', 128, [4, 64], BF16); V1 = av('V1', 128, [4, 129], BF16)
        SIGOs = [av('SIGO', 128, [512]) for _ in range(2)]; HMN = av('HMN', 128, [4, 128]); T1 = av('T1', 128, [4, 128]); HMB = av('HMB', 128, [512], BF16)
        JK2 = av('JK2', 128, [128])
        P.add('dve', lambda e: e.memset(ONES4, 1.0), w=['ONES4'])
        P.add('dve', lambda e: e.memset(ZEROS4, 0.0), w=['ZEROS4'])
        P.add('dve', lambda e: e.memset(V1, 1.0), w=['V1'])
        if seq_first:
            P.add('dve', lambda e: e.memset(SST[:], 0.0), w=['SST'])
            P.add('dve', lambda e: e.memset(SBF[:], 0.0), w=['SBF'])
            P.add('dve', lambda e: e.memset(RCOL[:], 0.0), w=['RCOL'])
            P.add('dve', lambda e: e.memset(BC[:], 0.0), w=['BC'])
            P.add('dve', lambda e: e.memset(MC[:], 0.0), w=['MC'])
        pending_back = [None]
        NW1 = 1544
        parts = [(lambda b: b[:, 0:KC * NW1].rearrange("p (k c) -> p k c", k=KC), W_IN[:, 0:NW1].rearrange("(k p) c -> p k c", p=128))]
        buf, bi = load_slab(parts)
        W1 = buf[:, 0:KC * NW1].rearrange("p (k c) -> p k c", k=KC)
        WK = [('slab', bi, 0)]
        for m in range(NM):
            mc0 = m * MTOK
            htk = [('HT', m * TPM + j) for j in range(TPM)]
            for h in range(4):
                for which in range(2):
                    coff = which * 256 + h * 64
                    b = pcnt['qk'] % 4; pcnt['qk'] += 1
                    for k in range(KC):
                        P.add('pe', lambda e, b=b, k=k, coff=coff, mc0=mc0: e.matmul(PS[b][0:64, 0:MTOK], W1[:, k, coff:coff + 64], HT[:, k, mc0:mc0 + MTOK],
                                                                             start=(k == 0), stop=(k == KC - 1)), r=WK + htk, w=[('PS', b)])
                    if which == 0:
                        P.add('act', lambda e, b=b, h=h: e.activation(out=QT[:, h, :], in_=PS[b][0:64, 0:MTOK], func=AF.Copy, scale=0.125),
                              r=[('PS', b)], w=['QT'])
                    else:
                        P.add('dve', lambda e, b=b, h=h: e.tensor_copy(KT[:, h, :], PS[b][0:64, 0:MTOK]), r=[('PS', b)], w=['KT'])
            for (b, coff) in ((4, 1536), (5, 1540)):
                for k in range(KC):
                    P.add('pe', lambda e, b=b, k=k, coff=coff, mc0=mc0: e.matmul(PS[b][0:4, 0:MTOK], W1[:, k, coff:coff + 4], HT[:, k, mc0:mc0 + MTOK],
                                                                         start=(k == 0), stop=(k == KC - 1)), r=WK + htk, w=[('PS', b)])
            P.add('act', lambda e: e.activation(out=TI, in_=PS[4][0:4, 0:MTOK], func=AF.Tanh, scale=1.0 / 15.0, bias=BIS[:]), r=[('PS', 4), 'BIS'], w=['TI'])
            P.add('act', lambda e: e.activation(out=TF, in_=PS[5][0:4, 0:MTOK], func=AF.Tanh, scale=1.0 / 15.0, bias=BFS[:]), r=[('PS', 5), 'BFS'], w=['TF'])
            P.add('act', lambda e: e.activation(out=TF, in_=TF, func=AF.Exp, scale=-15.0), r=['TF'], w=['TF'])
            P.add('act', lambda e: e.activation(out=TF, in_=TF, func=AF.Ln, bias=1.0), r=['TF'], w=['TF'])
            P.add('dve', lambda e: e.tensor_tensor_scan(NEGB, ONES4, TF, BC[:], ALU.mult, ALU.add), r=['ONES4', 'TF', 'BC'], w=['NEGB'])
            P.add('dve', lambda e: e.tensor_copy(BC[:], NEGB[:, MTOK - 1:MTOK]), r=['NEGB'], w=['BC'])
            P.add('dve', lambda e: e.scalar_tensor_tensor(out=GG, in0=TI, scalar=15.0, in1=NEGB, op0=ALU.mult, op1=ALU.add), r=['TI', 'NEGB'], w=['GG'])
            P.add('dve', lambda e: e.tensor_tensor_scan(MMs, GG, ZEROS4, MC[:], ALU.max, ALU.max), r=['GG', 'ZEROS4', 'MC'], w=['MMs'])
            P.add('dve', lambda e: e.tensor_copy(MC[:], MMs[:, MTOK - 1:MTOK]), r=['MMs'], w=['MC'])
            P.add('dve', lambda e: e.tensor_tensor(out=NM_, in0=NEGB, in1=MMs, op=ALU.subtract), r=['NEGB', 'MMs'], w=['NMs'])
            for j in range(TPM):
                t = m * TPM + j
                cj = slice(j * 128, (j + 1) * 128)
                tcols = slice(t * 128, (t + 1) * 128)
                par = t % 2
                ENM = ENMs[par]; SIGO = SIGOs[par]; NR = NRs[par]
                ek = ('ENM', par); sk = ('SIGO', par); nk = ('NR', par)
                P.add('pe', lambda e, cj=cj: e.transpose(PS[4][:, 0:4], GG[:, cj], IDF[0:4, 0:4]), r=['GG', 'IDF'], w=[('PS', 4)])
                P.add('pe', lambda e, cj=cj: e.transpose(PS[4][:, 4:8], NM_[:, cj], IDF[0:4, 0:4]), r=['NMs', 'IDF'], w=[('PS', 4)])
                P.add('dve', lambda e: e.tensor_copy(GCOL, PS[4][:, 0:4]), r=[('PS', 4)], w=['GCOL'])
                P.add('act', lambda e, ENM=ENM: e.activation(out=ENM, in_=PS[4][:, 4:8], func=AF.Exp), r=[('PS', 4)], w=[ek])
                for h in range(4):
                    P.add('pe', lambda e, h=h, cj=cj: e.matmul(PS[3][:, h * 128:(h + 1) * 128], SEL[0:4, h * 128:(h + 1) * 128], MMs[:, cj], start=True, stop=True),
                          r=['SEL', 'MMs'], w=[('PS', 3)])
                PS3v = PS[3][:, 0:512].rearrange("p (a b) -> p a b", a=4)
                P.add('dve', lambda e: e.tensor_copy(RNEW, PS3v[:, :, 127]), r=[('PS', 3)], w=['RNEW'])
                P.add('dve', lambda e: e.scalar_tensor_tensor(out=ARG, in0=PS3v, scalar=-1.0, in1=MASKNEG[:].unsqueeze(1).to_broadcast([128, 4, 128]),
                                                              op0=ALU.mult, op1=ALU.add), r=[('PS', 3), 'MASKNEG'], w=['ARG'])
                P.add('dve', lambda e: e.tensor_tensor(out=ARG, in0=ARG, in1=GCOL.unsqueeze(2).to_broadcast([128, 4, 128]), op=ALU.add), r=['ARG', 'GCOL'], w=['ARG'])
                P.add('act', lambda e: e.activation(out=DTm, in_=ARG, func=AF.Exp), r=['ARG'], w=['DTm'])
                P.add('dve', lambda e: e.scalar_tensor_tensor(out=ARG2, in0=PS3v[0:64], scalar=-1.0, in1=RCOL[0:64, :].unsqueeze(2).to_broadcast([64, 4, 128]),
                                                              op0=ALU.mult, op1=ALU.add), r=[('PS', 3), 'RCOL'], w=['ARG2'])
                P.add('act', lambda e: e.activation(out=EE, in_=ARG2, func=AF.Exp), r=['ARG2'], w=['EE'])
                P.add('dve', lambda e, cj=cj: e.tensor_tensor(out=QP, in0=QT[:, :, cj], in1=EE, op=ALU.mult), r=['QT', 'EE'], w=['QP'])
                for (b, c0, wd_) in ((0, 256, 256), (1, 512, 512), (2, 1024, 512)):
                    for k in range(KC):
                        P.add('pe', lambda e, b=b, c0=c0, wd_=wd_, k=k, tcols=tcols: e.matmul(PS[b][:, 0:wd_], HT[:, k, tcols], W1[:, k, c0:c0 + wd_],
                                                                                          start=(k == 0), stop=(k == KC - 1)), r=WK + [('HT', t)], w=[('PS', b)])
                P.add('dve', lambda e: e.tensor_tensor(out=WCOL, in0=GCOL, in1=RNEW, op=ALU.subtract), r=['GCOL', 'RNEW'], w=['WCOL'])
                P.add('act', lambda e: e.activation(out=WCOL, in_=WCOL, func=AF.Exp), r=['WCOL'], w=['WCOL'])
                P.add('dve', lambda e: e.tensor_tensor(out=KP, in0=PS[0][:, 0:256].rearrange("p (a b) -> p a b", a=4),
                                                       in1=WCOL.unsqueeze(2).to_broadcast([128, 4, 64]), op=ALU.mult), r=[('PS', 0), 'WCOL'], w=['KP'])
                P.add('act', lambda e: e.activation(out=V1[:, :, 0:128], in_=PS[1][:, 0:512].rearrange("p (a b) -> p a b", a=4), func=AF.Copy),
                      r=[('PS', 1)], w=['V1'])
                P.add('act', lambda e, SIGO=SIGO: e.activation(out=SIGO, in_=PS[2][:, 0:512], func=AF.Exp, scale=-1.0), r=[('PS', 2)], w=[sk])
                for h in range(4):
                    P.add('pe', lambda e, h=h, cj=cj: e.matmul(PS[5][:, h * 128:(h + 1) * 128], KT[:, h, cj], QT[:, h, cj], start=True, stop=True),
                          r=['KT', 'QT'], w=[('PS', 5)])
                P.add('dve', lambda e: e.tensor_tensor(out=PD, in0=PS[5][:, 0:512].rearrange("p (a b) -> p a b", a=4), in1=DTm, op=ALU.mult),
                      r=[('PS', 5), 'DTm'], w=['PD'])
                for h in range(4):
                    b = h // 2; o0 = (h % 2) * 129
                    P.add('pe', lambda e, h=h, b=b, o0=o0: e.matmul(PS[b][:, o0:o0 + 129], PD[:, h, :], V1[:, h, :], start=True, stop=False),
                          r=['PD', 'V1'], w=[('PS', b)])
                    P.add('pe', lambda e, h=h, b=b, o0=o0: e.matmul(PS[b][:, o0:o0 + 129], QP[:, h, :], SBF[:, h, :], start=False, stop=True),
                          r=['QP', 'SBF'], w=[('PS', b)])
                for h in range(4):
                    b = (2, 4)[h // 2]; o0 = (h % 2) * 129
                    P.add('pe', lambda e, h=h, b=b, o0=o0: e.matmul(PS[b][0:64, o0:o0 + 129], KP[:, h, :], V1[:, h, :], start=True, stop=True),
                          r=['KP', 'V1'], w=[('PS', b)])
                P.add('dve', lambda e: e.tensor_tensor(out=DEC, in0=RCOL[0:64, :], in1=RNEW[0:64, :], op=ALU.subtract), r=['RCOL', 'RNEW'], w=['DEC'])
                P.add('act', lambda e: e.activation(out=DEC, in_=DEC, func=AF.Exp), r=['DEC'], w=['DEC'])
                for h in range(4):
                    b = (2, 4)[h // 2]; o0 = (h % 2) * 129
                    P.add('dve', lambda e, h=h, b=b, o0=o0: e.scalar_tensor_tensor(out=SST[:, h, :], in0=SST[:, h, :], scalar=DEC[:, h:h + 1], in1=PS[b][0:64, o0:o0 + 129],
                                                                                   op0=ALU.mult, op1=ALU.add), r=['SST', 'DEC', ('PS', b), 'SBF'], w=['SST'])
                P.add('act', lambda e: e.activation(out=SBF[:], in_=SST[:], func=AF.Copy), r=['SST'], w=['SBF'])
                P.add('dve', lambda e: e.tensor_copy(RCOL[:], RNEW), r=['RNEW'], w=['RCOL'])
                for b in range(2):
                    pv = PS[b][:, 0:258].rearrange("p (a b) -> p a b", a=2)
                    P.add('dve', lambda e, b=b, pv=pv, NR=NR: e.tensor_copy(NR[:, 2 * b:2 * b + 2, :], pv), r=[('PS', b)], w=[nk])

                def back(t=t, tcols=tcols, ENM=ENM, SIGO=SIGO, NR=NR, ek=ek, sk=sk, nk=nk):
                    P.add('dve', lambda e: e.tensor_copy(DEN, NR[:, :, 128]), r=[nk], w=['DEN'])
                    P.add('dve', lambda e: e.tensor_scalar(NEGD, DEN, -1.0, None, op0=ALU.mult), r=['DEN'], w=['NEGD'])
                    P.add('dve', lambda e: e.tensor_tensor(out=DEN, in0=DEN, in1=NEGD, op=ALU.max), r=['DEN', 'NEGD'], w=['DEN'])
                    P.add('dve', lambda e: e.tensor_tensor(out=DEN, in0=DEN, in1=ENM, op=ALU.max), r=['DEN', ek], w=['DEN'])
                    P.add('dve', lambda e: e.reciprocal(RRm, DEN), r=['DEN'], w=['RRm'])
                    P.add('dve', lambda e: e.tensor_tensor(out=HMN, in0=NR[:, :, 0:128], in1=RRm.unsqueeze(2).to_broadcast([128, 4, 128]), op=ALU.mult),
                          r=[nk, 'RRm'], w=['HMN'])
                    for h in range(4):
                        P.add('act', lambda e, h=h: e.activation(out=JK2, in_=HMN[:, h, :], func=AF.Square, accum_out=SS2[:, h:h + 1]), r=['HMN'], w=['JK2', 'SS2'])
                    P.add('dve', lambda e: e.tensor_scalar(RS2, SS2, 1.0 / 128.0, EPS, op0=ALU.mult, op1=ALU.add), r=['SS2'], w=['RS2'])
                    P.add('act', lambda e: e.activation(out=RS2, in_=RS2, func=AF.Ln), r=['RS2'], w=['RS2'])
                    P.add('act', lambda e: e.activation(out=RS2, in_=RS2, func=AF.Exp, scale=-0.5), r=['RS2'], w=['RS2'])
                    P.add('dve', lambda e: e.tensor_tensor(out=T1, in0=HMN, in1=RS2.unsqueeze(2).to_broadcast([128, 4, 128]), op=ALU.mult), r=['HMN', 'RS2'], w=['T1'])
                    P.add('dve', lambda e: e.tensor_scalar(SIGO, SIGO, 1.0, None, op0=ALU.add), r=[sk], w=[sk])
                    P.add('dve', lambda e: e.reciprocal(SIGO, SIGO), r=[sk], w=[sk])
                    P.add('dve', lambda e: e.tensor_tensor(out=SIGO, in0=SIGO, in1=OG[:], op=ALU.mult), r=[sk, 'OG'], w=[sk])
                    P.add('dve', lambda e: e.tensor_tensor(out=HMB, in0=T1.rearrange("p a b -> p (a b)"), in1=SIGO, op=ALU.mult), r=['T1', sk], w=['HMB'])
                    for kc in range(4):
                        P.add('pe', lambda e, kc=kc: e.transpose(PSB[1][:, kc * 128:(kc + 1) * 128], HMB[:, kc * 128:(kc + 1) * 128], IDB[:]), r=['HMB', 'IDB'], w=[('PSB', 1)])
                    P.add('dve', lambda e: e.tensor_copy(HM[:, :, tcols], PSB[1][:, 0:512].rearrange("p (a b) -> p a b", a=4)), r=[('PSB', 1)], w=[('HM', t)])
                if pending_back[0] is not None:
                    pending_back[0]()
                pending_back[0] = back
        if pending_back[0] is not None:
            pending_back[0]()
            pending_back[0] = None

        abarrier()
        QKVS = av('QKVS', 128, [768]); QKB = av('QKB', 128, [10, 64], BF16)
        TA = av('TA', 128, [10, 8]); TB = av('TB', 128, [10, 8])
        QTA = av('QTA', 64, [8, 128], BF16)
        EB = [[av('EB', 128, [4, 128], BF16) for _ in range(2)] for _ in range(2)]
        PT = [[av('PT', 128, [4, 128], BF16) for _ in range(2)] for _ in range(2)]
        OAB = av('OAB', 128, [8, 64], BF16); DENA = av('DENA', 128, [8]); RRA = av('RRA', 128, [8])
        NW2 = 768
        parts = [(lambda b: b[:, 0:KC * NW2].rearrange("p (k c) -> p k c", k=KC), W_IN[:, 1544:1544 + NW2].rearrange("(k p) c -> p k c", p=128))]
        buf, bi = load_slab(parts)
        W2 = buf[:, 0:KC * NW2].rearrange("p (k c) -> p k c", k=KC)
        WK = [('slab', bi, 0)]
        XR = QKVS[:, 0:640].rearrange("p (a b) -> p a b", a=10)
        for t in range(TT):
            gt = T * TT + t
            lt = (T % STPS) * TT + t
            par = gt % 2
            tcols = slice(t * 128, (t + 1) * 128)
            for (b, c0, wd_) in ((0, 0, 512), (1, 512, 256)):
                for k in range(KC):
                    P.add('pe', lambda e, b=b, c0=c0, wd_=wd_, k=k, tcols=tcols: e.matmul(PS[b][:, 0:wd_], HT[:, k, tcols], W2[:, k, c0:c0 + wd_],
                                                                                      start=(k == 0), stop=(k == KC - 1)), r=WK + [('HT', t)], w=[('PS', b)])
            P.add('act', lambda e: e.activation(out=QKVS[:, 0:512], in_=PS[0][:, 0:512], func=AF.Copy, scale=0.125), r=[('PS', 0)], w=['QKVS'])
            P.add('act', lambda e: e.activation(out=QKVS[:, 512:768], in_=PS[1][:, 0:256], func=AF.Copy), r=[('PS', 1)], w=['QKVS'])
            cosb = COS[:, gt, :].unsqueeze(1).to_broadcast([128, 10, 8]); sinb = SIN[:, gt, :].unsqueeze(1).to_broadcast([128, 10, 8])
            x1 = XR[:, :, 0:8]; x2 = XR[:, :, 8:16]
            P.add('dve', lambda e, x1=x1, cosb=cosb: e.tensor_tensor(out=TA, in0=x1, in1=cosb, op=ALU.mult), r=['QKVS', 'COS'], w=['TA'])
            P.add('dve', lambda e, x2=x2, sinb=sinb: e.tensor_tensor(out=TB, in0=x2, in1=sinb, op=ALU.mult), r=['QKVS', 'SIN'], w=['TB'])
            P.add('dve', lambda e: e.tensor_tensor(out=QKB[:, :, 0:8], in0=TA, in1=TB, op=ALU.subtract), r=['TA', 'TB'], w=['QKB'])
            P.add('dve', lambda e, x2=x2, cosb=cosb: e.tensor_tensor(out=TA, in0=x2, in1=cosb, op=ALU.mult), r=['QKVS', 'COS', 'QKB'], w=['TA'])
            P.add('dve', lambda e, x1=x1, sinb=sinb: e.tensor_tensor(out=TB, in0=x1, in1=sinb, op=ALU.mult), r=['QKVS', 'SIN', 'QKB'], w=['TB'])
            P.add('dve', lambda e: e.tensor_tensor(out=QKB[:, :, 8:16], in0=TA, in1=TB, op=ALU.add), r=['TA', 'TB'], w=['QKB'])
            P.add('dve', lambda e: e.tensor_copy(QKB[:, :, 16:64], XR[:, :, 16:64]), r=['QKVS'], w=['QKB'])
            P.add('act', lambda e, par=par: e.activation(out=V1A[par][:, :, 0:64], in_=QKVS[:, 640:768].rearrange("p (a b) -> p a b", a=2), func=AF.Copy),
                  r=['QKVS'], w=[('V1A', par)])
            for h in range(8):
                P.add('pe', lambda e, h=h: e.transpose(PSB[0][0:64, h * 128:(h + 1) * 128], QKB[:, h, :], IDB[:]), r=['QKB', 'IDB'], w=[('PSB', 0)])
            for kv in range(2):
                P.add('pe', lambda e, kv=kv: e.transpose(PSB[1][0:64, kv * 128:(kv + 1) * 128], QKB[:, 8 + kv, :], IDB[:]), r=['QKB', 'IDB'], w=[('PSB', 1)])
            P.add('dve', lambda e: e.tensor_copy(QTA, PSB[0][0:64, 0:1024].rearrange("p (a b) -> p a b", a=8)), r=[('PSB', 0)], w=['QTA'])
            P.add('dve', lambda e, par=par: e.tensor_copy(KTA[par][:], PSB[1][0:64, 0:256].rearrange("p (a b) -> p a b", a=2)), r=[('PSB', 1)], w=[('KTA', par)])
            has_prev = lt > 0
            for kv in range(2):
                for pc in range(2):
                    if pc == 1 and not has_prev:
                        continue
                    b = 2 + 2 * kv + pc
                    src = par if pc == 0 else 1 - par
                    P.add('pe', lambda e, b=b, kv=kv, src=src: e.matmul(PS[b][:, 0:512], KTA[src][:, kv, :], QTA[:, 4 * kv:4 * kv + 4, :], start=True, stop=True),
                          r=[('KTA', src), 'QTA'], w=[('PS', b)])
                    eb = EB[kv][pc]; pt = PT[kv][pc]
                    mk = MCUR if pc == 0 else MPREV
                    P.add('act', lambda e, b=b, eb=eb: e.activation(out=eb, in_=PS[b][:, 0:512].rearrange("p (a b) -> p a b", a=4), func=AF.Exp), r=[('PS', b)], w=['EB'])
                    P.add('dve', lambda e, eb=eb, pt=pt, mk=mk: e.tensor_tensor(out=pt, in0=eb, in1=mk[:].unsqueeze(1).to_broadcast([128, 4, 128]), op=ALU.mult),
                          r=['EB', 'MCUR', 'MPREV'], w=['PT'])
            for h in range(8):
                kv = h // 4; hh = h % 4; b = h // 4; o0 = hh * 65
                if has_prev:
                    P.add('pe', lambda e, b=b, o0=o0, kv=kv, hh=hh, par=par: e.matmul(PS[b][:, o0:o0 + 65], PT[kv][1][:, hh, :], V1A[1 - par][:, kv, :], start=True, stop=False),
                          r=['PT', ('V1A', 1 - par)], w=[('PS', b)])
                P.add('pe', lambda e, b=b, o0=o0, kv=kv, hh=hh, par=par, has_prev=has_prev: e.matmul(PS[b][:, o0:o0 + 65], PT[kv][0][:, hh, :], V1A[par][:, kv, :],
                                                                                              start=(not has_prev), stop=True),
                      r=['PT', ('V1A', par)], w=[('PS', b)])
            for b in range(2):
                pv = PS[b][:, 0:260].rearrange("p (a b) -> p a b", a=4)
                P.add('dve', lambda e, b=b, pv=pv: e.tensor_tensor(out=DENA[:, 4 * b:4 * b + 4], in0=pv[:, :, 64], in1=ESINK[:, 4 * b:4 * b + 4], op=ALU.add),
                      r=[('PS', b), 'ESINK'], w=['DENA'])
            P.add('dve', lambda e: e.reciprocal(RRA, DENA), r=['DENA'], w=['RRA'])
            for b in range(2):
                pv = PS[b][:, 0:260].rearrange("p (a b) -> p a b", a=4)
                P.add('dve', lambda e, b=b, pv=pv: e.tensor_tensor(out=OAB[:, 4 * b:4 * b + 4, :], in0=pv[:, :, 0:64],
                                                                   in1=RRA[:, 4 * b:4 * b + 4].unsqueeze(2).to_broadcast([128, 4, 64]), op=ALU.mult),
                      r=[('PS', b), 'RRA'], w=['OAB'])
            OABf = OAB.rearrange("p a b -> p (a b)")
            for kc in range(4):
                P.add('pe', lambda e, kc=kc, OABf=OABf: e.transpose(PSB[1][:, kc * 128:(kc + 1) * 128], OABf[:, kc * 128:(kc + 1) * 128], IDB[:]), r=['OAB', 'IDB'], w=[('PSB', 1)])
            P.add('dve', lambda e, tcols=tcols: e.tensor_copy(OA[:, :, tcols], PSB[1][:, 0:512].rearrange("p (a b) -> p a b", a=4)), r=[('PSB', 1)], w=[('OA', t)])

        abarrier()
        MG = av('MG', 128, [KC, ST], BF16)
        SGG = [av('SGG', 128, [MTOK]) for _ in range(2)]
        TMPG = [av('TMPG', 128, [MTOK]) for _ in range(2)]
        for (br, gc0, wbr, SRC, skey) in (('m', 2312, W['w_branch_mlstm'], HM, 'HM'), ('a', 2312 + DM, W['w_branch_attn'], OA, 'OA')):
            GO, BO = 0, KC * DM
            parts = [(lambda b: b[:, GO:GO + KC * DM].rearrange("p (k c) -> p k c", k=KC), W_IN[:, gc0:gc0 + DM].rearrange("(k p) c -> p k c", p=128)),
                     (lambda b: b[:, BO:BO + 4 * DM].rearrange("p (k c) -> p k c", k=4), wbr.rearrange("(k p) c -> p k c", p=128))]
            buf, bi = load_slab(parts)
            W3 = buf[:, GO:GO + KC * DM].rearrange("p (k c) -> p k c", k=KC)
            WB = buf[:, BO:BO + 4 * DM].rearrange("p (k c) -> p k c", k=4)
            for m in range(NM):
                mc0 = m * MTOK
                htk = [('HT', m * TPM + j) for j in range(TPM)]
                srk = [(skey, m * TPM + j) for j in range(TPM)]
                for d in range(KC):
                    gb = pcnt['g'] % 2; pcnt['g'] += 1
                    Gp, Yp = PS[2 * gb], PS[2 * gb + 1]
                    for k in range(KC):
                        P.add('pe', lambda e, Gp=Gp, k=k, d=d, mc0=mc0, W3=W3: e.matmul(Gp[:, 0:MTOK], W3[:, k, d * 128:(d + 1) * 128], HT[:, k, mc0:mc0 + MTOK],
                                                                                start=(k == 0), stop=(k == KC - 1)), r=[('slab', bi, 0)] + htk, w=[('PS', 2 * gb)])
                    for k in range(4):
                        P.add('pe', lambda e, Yp=Yp, k=k, d=d, mc0=mc0, WB=WB, SRC=SRC: e.matmul(Yp[:, 0:MTOK], WB[:, k, d * 128:(d + 1) * 128], SRC[:, k, mc0:mc0 + MTOK],
                                                                                         start=(k == 0), stop=(k == 3)), r=[('slab', bi, 1)] + srk, w=[('PS', 2 * gb + 1)])
                    sgg = SGG[gb]
                    P.add('act', lambda e, sgg=sgg, Gp=Gp: e.activation(out=sgg, in_=Gp[:, 0:MTOK], func=AF.Sigmoid), r=[('PS', 2 * gb)], w=[('SGG', gb)])
                    if br == 'm':
                        P.add('dve', lambda e, sgg=sgg, Yp=Yp, d=d, mc0=mc0: e.tensor_tensor(out=MG[:, d, mc0:mc0 + MTOK], in0=Yp[:, 0:MTOK], in1=sgg, op=ALU.mult),
                              r=[('PS', 2 * gb + 1), ('SGG', gb)], w=[('MG', d, m)])
                    else:
                        tg = TMPG[gb]
                        P.add('dve', lambda e, sgg=sgg, Yp=Yp, tg=tg: e.tensor_tensor(out=tg, in0=Yp[:, 0:MTOK], in1=sgg, op=ALU.mult),
                              r=[('PS', 2 * gb + 1), ('SGG', gb)], w=[('TMPG', gb)])
                        P.add('dve', lambda e, tg=tg, d=d, mc0=mc0: e.tensor_tensor(out=MG[:, d, mc0:mc0 + MTOK], in0=MG[:, d, mc0:mc0 + MTOK], in1=tg, op=ALU.add),
                              r=[('TMPG', gb), ('MG', d, m)], w=[('MG', d, m)])
        parts = [(lambda b: b[:, 0:KC * DM].rearrange("p (k c) -> p k c", k=KC), W['w_out'].rearrange("(k p) c -> p k c", p=128))]
        buf, bi = load_slab(parts)
        WO = buf[:, 0:KC * DM].rearrange("p (k c) -> p k c", k=KC)
        for t in range(TT):
            m = t // TPM
            tcols = slice(t * 128, (t + 1) * 128)
            for h in range(NH):
                ob = pcnt['o'] % 2; pcnt['o'] += 1
                Op = PS[4 + ob]
                for k in range(KC):
                    P.add('pe', lambda e, Op=Op, k=k, h=h, tcols=tcols: e.matmul(Op[:, 0:HW], MG[:, k, tcols], WO[:, k, h * HW:(h + 1) * HW], start=(k == 0), stop=(k == KC - 1)),
                          r=[('slab', bi, 0)] + [('MG', d, m) for d in range(KC)], w=[('PS', 4 + ob)])
                P.add('dve', lambda e, Op=Op, t=t, h=h: e.tensor_tensor(out=X[:, t, h * HW:(h + 1) * HW], in0=Op[:, 0:HW], in1=X[:, t, h * HW:(h + 1) * HW], op=ALU.add),
                      r=[('PS', 4 + ob), ('X', t)], w=[('X', t)])


    for T in range(NST):
        load_x(T)
        if 'ffn1' in phases:
            ffn(1)
        if 'mix' in phases:
            mixer(T)
        if 'ffn2' in phases:
            ffn(2)
        final_out(T)
    P.add('sp', None, r=[('dmasem', 'o0'), ('dmasem', 'o1')])
    P.build(st)
    st.close()
    return nc, P


def make_consts():
    s = np.arange(128)[:, None]; j = np.arange(128)[None, :]
    c = {}
    c['ident_bf'] = np.eye(128).astype(ml_dtypes.bfloat16)
    c['ident_f'] = np.eye(128).astype(np.float32)
    sel = np.zeros((4, 4, 128), np.float32)
    for h in range(4):
        sel[h, h, :] = 1.0
    c['sel'] = sel.reshape(4, 512)
    c['maskneg'] = np.where(s <= j, 0.0, -1e30).astype(np.float32)
    c['mask_cur'] = (s <= j).astype(np.float32).astype(ml_dtypes.bfloat16)
    c['mask_prev'] = (s > j).astype(np.float32).astype(ml_dtypes.bfloat16)
    c['inv_freq'] = (500000.0 ** (-np.arange(8, dtype=np.float32) * 2.0 / 16)).astype(np.float32)
    return c


def make_in_maps(cfg, inputs):
    DM, S, NSEQ, NC_ = cfg['DM'], cfg['S'], cfg['NSEQ'], cfg['NCORES']
    KC = DM // 128
    consts = make_consts()
    shared = dict(consts)
    f32 = lambda a: np.ascontiguousarray(np.asarray(a, dtype=np.float32))
    for n in WNAMES:
        shared[n] = f32(inputs[n][0])
    for n, k in (('ffn1_norm_g', 'ffn1_norm_gT'), ('mix_norm_g', 'mix_norm_gT'), ('ffn2_norm_g', 'ffn2_norm_gT')):
        shared[k] = f32(np.asarray(inputs[n][0]).reshape(KC, 128).T)
    shared['final_norm_g'] = f32(inputs['final_norm_g'])
    shared['mlstm_b_i'] = f32(np.asarray(inputs['mlstm_b_i'][0]).reshape(4, 1))
    shared['mlstm_b_f'] = f32(np.asarray(inputs['mlstm_b_f'][0]).reshape(4, 1))
    shared['mlstm_out_norm_g'] = f32(inputs['mlstm_out_norm_g'][0])
    shared['attn_sinks'] = f32(inputs['attn_sinks'][0])
    x = np.asarray(inputs['x'], dtype=np.float32)
    pos = np.asarray(inputs['positions'], dtype=np.int32)
    maps = []
    for c in range(NC_):
        m = dict(shared)
        m['x'] = np.ascontiguousarray(x[c * NSEQ:(c + 1) * NSEQ].reshape(NSEQ * S, DM))
        m['posT'] = np.ascontiguousarray(pos[c * NSEQ:(c + 1) * NSEQ].reshape(-1, 128).T)
        maps.append(m)
    return maps


def run(cfg, inputs, phases=('ffn1', 'mix', 'ffn2'), sim=False, trace=False):
    nc, P = build_program(cfg, phases)
    maps = make_in_maps(cfg, inputs)
    res = run_bass_kernel_spmd(nc, maps, core_ids=list(range(cfg['NCORES']))).results
    out = np.concatenate([r['out'] for r in res], axis=0)
    B = cfg['NSEQ'] * cfg['NCORES']
    return out.reshape(B, cfg['S'], cfg['DM'])


def kernel(**inputs):
    cfg = FULL
    out = run(cfg, inputs)
    return np.ascontiguousarray(out.astype(np.float32))
```

```python
import numpy as np
import concourse.bass as bass
import concourse.mybir as mybir
from contextlib import ExitStack

F32 = mybir.dt.float32; BF16 = mybir.dt.bfloat16; I32 = mybir.dt.int32
AF = mybir.ActivationFunctionType; ALU = mybir.AluOpType; AX = mybir.AxisListType

class Prog:
    ENGS = ('pe', 'act', 'dve', 'pool', 'sp')
    def __init__(self, nc):
        self.nc = nc
        self.ops = []
    def add(self, eng, fn, r=(), w=(), dma=None):
        self.ops.append([eng, fn, tuple(r), tuple(w), dma])
    def barrier(self, tag):
        for e in self.ENGS:
            self.add(e, lambda eng: eng.nop(), r=(), w=[('bar', tag, e)])
        for e in self.ENGS:
            self.add(e, lambda eng: eng.nop(), r=[('bar', tag, e2) for e2 in self.ENGS if e2 != e], w=())
    def build(self, stack):
        nc = self.nc
        ops = self.ops
        n = len(ops)
        last_w = {}
        readers = {}
        deps = [None] * n
        needs_sig = [False] * n
        for i, (eng, fn, r, w, dma) in enumerate(ops):
            d = {}
            def add_dep(j, kind):
                if j == i:
                    return
                e2, _, _, _, dma2 = ops[j]
                if e2 == eng and dma2 is None and dma is None:
                    if kind != 'raw' or eng == 'pe':
                        return
                d[j] = True
            rk = list(r)
            wk = list(w)
            if dma is not None:
                wk.append(('dmasem', dma))
            for k in rk:
                if k in last_w:
                    add_dep(last_w[k], 'raw')
            for k in wk:
                if k in last_w:
                    add_dep(last_w[k], 'waw')
                for j in readers.get(k, ()):
                    add_dep(j, 'war')
            for k in rk:
                readers.setdefault(k, []).append(i)
            for k in wk:
                last_w[k] = i
                readers[k] = []
            deps[i] = list(d.keys())
            for j in deps[i]:
                needs_sig[j] = True
        sig = [None] * n
        cnt = {}
        semnames = set()
        for i, (eng, fn, r, w, dma) in enumerate(ops):
            if not needs_sig[i] and dma is None:
                continue
            name = ('dma_' + dma) if dma is not None else ('eng_' + eng)
            inc = 16 if dma is not None else 1
            cnt[name] = cnt.get(name, 0) + inc
            sig[i] = (name, cnt[name], inc)
            semnames.add(name)
        sems = {}
        for name in sorted(semnames):
            sems[name] = stack.enter_context(nc.semaphore(name))
        self.sem_counts = cnt
        per_eng = {e: [] for e in self.ENGS}
        waited = {e: {} for e in self.ENGS}
        nwaits = 0
        for i, (eng, fn, r, w, dma) in enumerate(ops):
            need = {}
            for j in deps[i]:
                name, val, _ = sig[j]
                if need.get(name, 0) < val:
                    need[name] = val
            ws = []
            for name, val in need.items():
                if waited[eng].get(name, 0) < val:
                    waited[eng][name] = val
                    ws.append((sems[name], val))
                    nwaits += 1
            s = None
            if sig[i] is not None:
                s = (sems[sig[i][0]], sig[i][2])
            per_eng[eng].append((ws, fn, s))
        self.nwaits = nwaits
        block = stack.enter_context(nc.Block())
        def mk(lst):
            def body(eng):
                for ws, fn, s in lst:
                    for sem, val in ws:
                        eng.wait_ge(sem, val)
                    if fn is None:
                        assert s is None
                        continue
                    ins = fn(eng)
                    if s is not None:
                        ins.then_inc(s[0], s[1])
            return body
        block.tensor(mk(per_eng['pe']))
        block.scalar(mk(per_eng['act']))
        block.vector(mk(per_eng['dve']))
        block.gpsimd(mk(per_eng['pool']))
        block.sync(mk(per_eng['sp']))


import numpy as np
import ml_dtypes
import concourse.bass as bass
import concourse.mybir as mybir
from contextlib import ExitStack
from concourse.bass_utils import run_bass_kernel_spmd

F32 = mybir.dt.float32; BF16 = mybir.dt.bfloat16; I32 = mybir.dt.int32
AF = mybir.ActivationFunctionType; ALU = mybir.AluOpType; AX = mybir.AxisListType
EPS = 1e-6

FULL = dict(DM=1024, DFF=2816, S=2048, NSEQ=2, ST=1024, NCORES=8)

WNAMES = ['ffn1_w_gate', 'ffn1_w_up', 'ffn1_w_down', 'w_in', 'w_branch_mlstm', 'w_branch_attn', 'w_out',
          'ffn2_w_gate', 'ffn2_w_up', 'ffn2_w_down']


def build_program(cfg, phases=('ffn1', 'mix', 'ffn2')):
    DM, DFF, S, NSEQ, ST = cfg['DM'], cfg['DFF'], cfg['S'], cfg['NSEQ'], cfg['ST']
    KC = DM // 128
    FC = DFF // 128
    TT = ST // 128
    MTOK = min(512, ST)
    NM = ST // MTOK
    TPM = MTOK // 128
    NTOK = NSEQ * S
    NST = NTOK // ST
    STPS = S // ST
    NTILES = NTOK // 128
    INW = 2312 + 2 * DM
    HW = min(512, DM)
    NH = DM // HW

    nc = bass.Bass("TRN2", target_bir_lowering=False)
    st = ExitStack()
    P = Prog(nc)

    def din(name, shape, dt=F32):
        return nc.dram_tensor(name, list(shape), dt, kind="ExternalInput").ap()
    x_d = din('x', [NTOK, DM])
    pos_d = din('posT', [128, NTILES], I32)
    W = {}
    W['ffn1_w_gate'] = din('ffn1_w_gate', [DM, DFF]); W['ffn1_w_up'] = din('ffn1_w_up', [DM, DFF]); W['ffn1_w_down'] = din('ffn1_w_down', [DFF, DM])
    W['ffn2_w_gate'] = din('ffn2_w_gate', [DM, DFF]); W['ffn2_w_up'] = din('ffn2_w_up', [DM, DFF]); W['ffn2_w_down'] = din('ffn2_w_down', [DFF, DM])
    W['w_in'] = din('w_in', [DM, INW]); W['w_branch_mlstm'] = din('w_branch_mlstm', [512, DM]); W['w_branch_attn'] = din('w_branch_attn', [512, DM])
    W['w_out'] = din('w_out', [DM, DM])
    g1_d = din('ffn1_norm_gT', [128, KC]); gm_d = din('mix_norm_gT', [128, KC]); g2_d = din('ffn2_norm_gT', [128, KC])
    fg_d = din('final_norm_g', [DM])
    bi_d = din('mlstm_b_i', [4, 1]); bf_d = din('mlstm_b_f', [4, 1])
    og_d = din('mlstm_out_norm_g', [512]); sink_d = din('attn_sinks', [8])
    identb_d = din('ident_bf', [128, 128], BF16); identf_d = din('ident_f', [128, 128])
    sel_d = din('sel', [4, 512]); maskneg_d = din('maskneg', [128, 128])
    mcur_d = din('mask_cur', [128, 128], BF16); mprev_d = din('mask_prev', [128, 128], BF16)
    invf_d = din('inv_freq', [8])
    out_d = nc.dram_tensor('out', [NTOK, DM], F32, kind="ExternalOutput").ap()

    def sb(name, shape, dt=F32):
        return st.enter_context(nc.sbuf_tensor(name, list(shape), dt))
    X = sb('X', [128, TT, DM])
    HT = sb('HT', [128, KC, ST], BF16)
    SLAB_E = 12352
    SLAB = [sb('SLAB%d' % i, [128, SLAB_E], BF16) for i in range(2)]
    Y = [sb('Y%d' % i, [128, DM]) for i in range(2)]
    FG = sb('FG', [128, DM])
    GT = {1: sb('G1', [128, KC]), 'm': sb('GM', [128, KC]), 2: sb('G2', [128, KC])}
    IDB = sb('IDB', [128, 128], BF16)
    SS = sb('SS', [128, TT]); RS = sb('RS', [128, TT])
    JUNK = sb('JUNK', [128, DM], BF16)
    HB = [sb('HB%d' % i, [128, DM], BF16) for i in range(2)]
    HM = sb('HM', [128, 4, ST], BF16)
    OA = sb('OA', [128, 4, ST], BF16)
    IDF = sb('IDF', [128, 128])
    SEL = sb('SEL', [4, 512])
    MASKNEG = sb('MASKNEG', [128, 128])
    MCUR = sb('MCUR', [128, 128], BF16); MPREV = sb('MPREV', [128, 128], BF16)
    OG = sb('OG', [128, 512])
    ESINK = sb('ESINK', [128, 8])
    BIS = sb('BIS', [4, 1]); BFS = sb('BFS', [4, 1])
    COS = sb('COS', [128, NTILES, 8]); SIN = sb('SIN', [128, NTILES, 8])
    SST = sb('SST', [64, 4, 129]); SBF = sb('SBF', [64, 4, 129], BF16)
    RCOL = sb('RCOL', [128, 4]); BC = sb('BC', [4, 1]); MC = sb('MC', [4, 1])
    V1A = [sb('V1A%d' % i, [128, 2, 65], BF16) for i in range(2)]
    KTA = [sb('KTA%d' % i, [64, 2, 128], BF16) for i in range(2)]
    DUMMY = sb('DUMMY', [128, 8])
    AE = 25600
    ARENA = sb('ARENA', [128, AE], BF16)
    arena_names = set()
    aoff = [0]
    def areset():
        aoff[0] = 0
    def av(name, parts, free, dt=F32):
        n = int(np.prod(free))
        ne = n * (2 if dt == F32 else 1)
        ne = (ne + 15) // 16 * 16
        a = aoff[0]; aoff[0] += ne
        assert aoff[0] <= AE, (name, aoff[0])
        v = ARENA[0:parts, a:a + n * (2 if dt == F32 else 1)]
        if dt == F32:
            v = v.bitcast(F32)
        if len(free) == 2:
            v = v.rearrange("p (a b) -> p a b", a=free[0])
        arena_names.add(name)
        return v
    _add = P.add
    def padd(eng, fn, r=(), w=(), dma=None):
        r = list(r)
        for k in list(r) + list(w):
            b = k[0] if isinstance(k, tuple) else k
            if b in arena_names:
                r.append('A'); break
        _add(eng, fn, r, w, dma)
    P.add = padd
    bar_ctr = [0]
    def abarrier():
        P.add('dve', lambda e: e.memset(DUMMY[:], 0.0), w=['A', 'DUMMY'])
        areset()
    def ffn_views():
        abarrier()
        SGv = [av('SG', 128, [MTOK]) for i in range(2)]
        ACv = [av('ACTT', 128, [4, MTOK], BF16) for i in range(2)]
        return SGv, ACv

    PS = [st.enter_context(nc.psum_tensor('PS%d' % i, [128, 512], F32)) for i in range(6)]
    PSB = [st.enter_context(nc.psum_tensor('PSB%d' % i, [128, 1024], BF16)) for i in range(2)]

    cl = 0
    def cload(dst, src, key):
        nonlocal cl
        P.add('sp', lambda e: e.dma_start(out=dst, in_=src), w=[key], dma='c%d' % cl)
        cl += 1
    cload(IDB[:], identb_d, 'IDB')
    cload(FG[:], fg_d.partition_broadcast(128), 'FG')
    cload(GT[1][:], g1_d, 'GT1'); cload(GT['m'][:], gm_d, 'GTm'); cload(GT[2][:], g2_d, 'GT2')

    slab_ctr = [0]
    def load_slab(parts):
        i = slab_ctr[0]; slab_ctr[0] += 1
        buf = SLAB[i % 2]
        for pi, (mk_dst, src) in enumerate(parts):
            dst = mk_dst(buf)
            wk = [('slab', i % 2, q) for q in range(3)] if pi == 0 else [('slab', i % 2, pi)]
            P.add('pool', lambda e, dst=dst, src=src: e.dma_start(out=dst, in_=src),
                  w=wk, dma='slab%d_%d' % (i % 2, pi))
        return buf, i % 2

    def load_x(T):
        for t in range(TT):
            r0 = T * ST + t * 128
            P.add('sp', lambda e, t=t, r0=r0: e.dma_start(out=X[:, t, :], in_=x_d[r0:r0 + 128, :]),
                  w=[('X', t)], dma='x%d' % (t % 4))

    def rstd_batch():
        for t in range(TT):
            P.add('act', lambda e, t=t: e.activation(out=JUNK[:], in_=X[:, t, :], func=AF.Square, accum_out=SS[:, t:t + 1]),
                  r=[('X', t)], w=['JUNK', ('SS', t)])
        P.add('dve', lambda e: e.tensor_scalar(RS[:], SS[:], 1.0 / DM, EPS, op0=ALU.mult, op1=ALU.add),
              r=[('SS', t) for t in range(TT)], w=['RS'])
        P.add('act', lambda e: e.activation(out=RS[:], in_=RS[:], func=AF.Ln), r=['RS'], w=['RS'])
        P.add('act', lambda e: e.activation(out=RS[:], in_=RS[:], func=AF.Exp, scale=-0.5), r=['RS'], w=['RS'])

    def norm_to_HT(gkey):
        rstd_batch()
        G = GT[gkey]
        for t in range(TT):
            hb = HB[t % 2]
            P.add('act', lambda e, t=t, hb=hb: e.activation(out=hb[:], in_=X[:, t, :], func=AF.Copy, scale=RS[:, t:t + 1]),
                  r=[('X', t), 'RS'], w=[('HB', t % 2)])
            tp = PSB[0]
            for k in range(KC):
                P.add('pe', lambda e, k=k, hb=hb, tp=tp: e.transpose(tp[:, k * 128:(k + 1) * 128], hb[:, k * 128:(k + 1) * 128], IDB[:]),
                      r=[('HB', t % 2), 'IDB'], w=[('PSB', 0)])
            P.add('dve', lambda e, t=t, tp=tp: e.tensor_tensor(
                out=HT[:, :, t * 128:(t + 1) * 128],
                in0=tp[:, 0:KC * 128].rearrange("p (k c) -> p k c", k=KC),
                in1=G[:].unsqueeze(2).to_broadcast([128, KC, 128]), op=ALU.mult),
                r=[('PSB', 0), 'GT%s' % gkey], w=[('HT', t)])

    cnt = dict(gu=0, o=0, act=0)
    def ffn(f):
        wg, wu, wd = W['ffn%d_w_gate' % f], W['ffn%d_w_up' % f], W['ffn%d_w_down' % f]
        norm_to_HT(f)
        SG, ACTT = ffn_views()
        groups = []
        c = 0
        while c < FC:
            n = min(4, FC - c)
            groups.append((c, n)); c += n
        for (c0, n) in groups:
            WGO, WUO, WDO = 0, KC * 512, 2 * KC * 512
            parts = [
                (lambda b, n=n: b[:, WGO:WGO + KC * n * 128].rearrange("p (k c) -> p k c", k=KC),
                 wg[:, c0 * 128:(c0 + n) * 128].rearrange("(k p) c -> p k c", p=128)),
                (lambda b, n=n: b[:, WUO:WUO + KC * n * 128].rearrange("p (k c) -> p k c", k=KC),
                 wu[:, c0 * 128:(c0 + n) * 128].rearrange("(k p) c -> p k c", p=128)),
                (lambda b, n=n: b[:, WDO:WDO + n * DM].rearrange("p (c d) -> p c d", c=n),
                 wd[c0 * 128:(c0 + n) * 128, :].rearrange("(c p) d -> p c d", p=128)),
            ]
            buf, bi = load_slab(parts)
            Wg = buf[:, WGO:WGO + KC * n * 128].rearrange("p (k c) -> p k c", k=KC)
            Wu = buf[:, WUO:WUO + KC * n * 128].rearrange("p (k c) -> p k c", k=KC)
            Wd = buf[:, WDO:WDO + n * DM].rearrange("p (c d) -> p c d", c=n)
            for m in range(NM):
                ab = cnt['act'] % 2; cnt['act'] += 1
                at = ACTT[ab]
                htk = [('HT', m * TPM + j) for j in range(TPM)]
                for ci in range(n):
                    gb = cnt['gu'] % 2; cnt['gu'] += 1
                    Gp, Up = PS[2 * gb], PS[2 * gb + 1]
                    for (Wx, Pp, pi, bk) in ((Wg, Gp, 0, 2 * gb), (Wu, Up, 1, 2 * gb + 1)):
                        for k in range(KC):
                            P.add('pe', lambda e, Wx=Wx, Pp=Pp, k=k, ci=ci, m=m: e.matmul(
                                Pp[:, 0:MTOK], Wx[:, k, ci * 128:(ci + 1) * 128], HT[:, k, m * MTOK:(m + 1) * MTOK],
                                start=(k == 0), stop=(k == KC - 1)),
                                r=[('slab', bi, pi)] + htk, w=[('PS', bk)])
                    sg = SG[gb]
                    P.add('act', lambda e, sg=sg, Gp=Gp: e.activation(out=sg[:], in_=Gp[:, 0:MTOK], func=AF.Silu),
                          r=[('PS', 2 * gb)], w=[('SG', gb)])
                    P.add('dve', lambda e, sg=sg, Up=Up, at=at, ci=ci: e.tensor_tensor(out=at[:, ci, :], in0=Up[:, 0:MTOK], in1=sg[:], op=ALU.mult),
                          r=[('PS', 2 * gb + 1), ('SG', gb)], w=[('ACTT', ab, ci)])
                for j in range(TPM):
                    t = m * TPM + j
                    for h in range(NH):
                        ob = cnt['o'] % 2; cnt['o'] += 1
                        Op = PS[4 + ob]
                        for ci in range(n):
                            P.add('pe', lambda e, Op=Op, at=at, ci=ci, j=j, h=h, n=n, Wd=Wd: e.matmul(
                                Op[:, 0:HW], at[:, ci, j * 128:(j + 1) * 128], Wd[:, ci, h * HW:(h + 1) * HW],
                                start=(ci == 0), stop=(ci == n - 1)),
                                r=[('ACTT', ab, ci), ('slab', bi, 2)], w=[('PS', 4 + ob)])
                        P.add('dve', lambda e, Op=Op, t=t, h=h: e.scalar_tensor_tensor(
                            out=X[:, t, h * HW:(h + 1) * HW], in0=Op[:, 0:HW], scalar=0.5, in1=X[:, t, h * HW:(h + 1) * HW],
                            op0=ALU.mult, op1=ALU.add),
                            r=[('PS', 4 + ob), ('X', t)], w=[('X', t)])

    def final_out(T):
        rstd_batch()
        for t in range(TT):
            y = Y[t % 2]
            r0 = T * ST + t * 128
            P.add('dve', lambda e, t=t, y=y: e.scalar_tensor_tensor(out=y[:], in0=X[:, t, :], scalar=RS[:, t:t + 1], in1=FG[:],
                                                                    op0=ALU.mult, op1=ALU.mult),
                  r=[('X', t), 'RS', 'FG'], w=[('Y', t % 2)])
            P.add('sp', lambda e, y=y, r0=r0: e.dma_start(out=out_d[r0:r0 + 128, :], in_=y[:]),
                  r=[('Y', t % 2)], dma='o%d' % (t % 2))

    import math
    cload(IDF[:], identf_d, 'IDF'); cload(SEL[:], sel_d, 'SEL'); cload(MASKNEG[:], maskneg_d, 'MASKNEG')
    cload(MCUR[:], mcur_d, 'MCUR'); cload(MPREV[:], mprev_d, 'MPREV')
    cload(OG[:], og_d.partition_broadcast(128), 'OG')
    cload(ESINK[:], sink_d.partition_broadcast(128), 'ESINK')
    cload(BIS[:], bi_d, 'BIS'); cload(BFS[:], bf_d, 'BFS')
    POSI = sb('POSI', [128, NTILES], I32); POSF = sb('POSF', [128, NTILES]); INVF = sb('INVF', [128, 8])
    cload(POSI[:], pos_d, 'POSI'); cload(INVF[:], invf_d.partition_broadcast(128), 'INVF')
    P.add('dve', lambda e: e.tensor_scalar(BIS[:], BIS[:], 1.0 / 15.0, None, op0=ALU.mult), r=['BIS'], w=['BIS'])
    P.add('dve', lambda e: e.tensor_scalar(BFS[:], BFS[:], 1.0 / 15.0, None, op0=ALU.mult), r=['BFS'], w=['BFS'])
    P.add('act', lambda e: e.activation(out=ESINK[:], in_=ESINK[:], func=AF.Exp), r=['ESINK'], w=['ESINK'])
    for i in range(2):
        P.add('dve', lambda e, i=i: e.memset(V1A[i][:], 1.0), w=[('V1A', i)])
        P.add('dve', lambda e, i=i: e.memset(KTA[i][:], 0.0), w=[('KTA', i)])
    NA = NTILES * 8
    ANG = sb('ANG', [128, NTILES, 8]); RR_ = sb('RRa', [128, NA]); KI = sb('KI', [128, NA], I32); KF = sb('KF', [128, NA]); MM_ = sb('MMa', [128, NA])
    P.add('dve', lambda e: e.tensor_copy(POSF[:], POSI[:]), r=['POSI'], w=['POSF'])
    P.add('dve', lambda e: e.tensor_tensor(out=ANG[:], in0=POSF[:].unsqueeze(2).to_broadcast([128, NTILES, 8]),
                                           in1=INVF[:].unsqueeze(1).to_broadcast([128, NTILES, 8]), op=ALU.mult), r=['POSF', 'INVF'], w=['ANG'])
    TWO_PI = 2.0 * math.pi
    C1 = 6.28125; C2 = TWO_PI - C1
    def sin_of(dst, shift, tag):
        angf = ANG[:].rearrange("p a b -> p (a b)")
        P.add('dve', lambda e: e.tensor_scalar(RR_[:], angf, shift, None, op0=ALU.add), r=['ANG'], w=['RRa'])
        P.add('dve', lambda e: e.tensor_scalar(KF[:], RR_[:], 1.0 / TWO_PI, None, op0=ALU.mult), r=['RRa'], w=['KF'])
        P.add('dve', lambda e: e.tensor_copy(KI[:], KF[:]), r=['KF'], w=['KI'])
        P.add('dve', lambda e: e.tensor_copy(KF[:], KI[:]), r=['KI'], w=['KF'])
        P.add('dve', lambda e: e.scalar_tensor_tensor(out=RR_[:], in0=KF[:], scalar=-C1, in1=RR_[:], op0=ALU.mult, op1=ALU.add), r=['KF', 'RRa'], w=['RRa'])
        P.add('dve', lambda e: e.scalar_tensor_tensor(out=RR_[:], in0=KF[:], scalar=-C2, in1=RR_[:], op0=ALU.mult, op1=ALU.add), r=['KF', 'RRa'], w=['RRa'])
        P.add('dve', lambda e: e.tensor_scalar(MM_[:], RR_[:], math.pi, -TWO_PI, op0=ALU.is_gt, op1=ALU.mult), r=['RRa'], w=['MMa'])
        P.add('dve', lambda e: e.tensor_tensor(out=RR_[:], in0=RR_[:], in1=MM_[:], op=ALU.add), r=['RRa', 'MMa'], w=['RRa'])
        P.add('dve', lambda e: e.tensor_scalar(MM_[:], RR_[:], -math.pi, TWO_PI, op0=ALU.is_lt, op1=ALU.mult), r=['RRa'], w=['MMa'])
        P.add('dve', lambda e: e.tensor_tensor(out=RR_[:], in0=RR_[:], in1=MM_[:], op=ALU.add), r=['RRa', 'MMa'], w=['RRa'])
        P.add('dve', lambda e: e.tensor_scalar(RR_[:], RR_[:], math.pi, -math.pi, op0=ALU.min, op1=ALU.max), r=['RRa'], w=['RRa'])
        P.add('act', lambda e: e.activation(out=dst[:].rearrange("p a b -> p (a b)"), in_=RR_[:], func=AF.Sin), r=['RRa'], w=[tag])
    sin_of(SIN, 0.0, 'SIN')
    sin_of(COS, math.pi / 2.0, 'COS')

    W_IN = W['w_in']
    pcnt = dict(qk=0, g=0, o=0)

    def mixer(T):
        seq_first = (T % STPS == 0)
        norm_to_HT('m')
        abarrier()
        QT = av('QT', 64, [4, MTOK], BF16); KT = av('KT', 64, [4, MTOK], BF16)
        TI = av('TI', 4, [MTOK]); TF = av('TF', 4, [MTOK]); NEGB = av('NEGB', 4, [MTOK]); GG = av('GG', 4, [MTOK])
        MMs = av('MMs', 4, [MTOK]); NM_ = av('NMs', 4, [MTOK]); ONES4 = av('ONES4', 4, [MTOK]); ZEROS4 = av('ZEROS4', 4, [MTOK])
        GCOL = av('GCOL', 128, [4]); ENMs = [av('ENM', 128, [4]) for _ in range(2)]; NRs = [av('NR', 128, [4, 129]) for _ in range(2)]; RNEW = av('RNEW', 128, [4]); WCOL = av('WCOL', 128, [4])
        DEC = av('DEC', 64, [4]); DEN = av('DEN', 128, [4]); NEGD = av('NEGD', 128, [4]); RRm = av('RRm', 128, [4]); SS2 = av('SS2', 128, [4]); RS2 = av('RS2', 128, [4])
        ARG = av('ARG', 128, [4, 128]); DTm = av('DTm', 128, [4, 128]); PD = av('PD', 128, [4, 128], BF16)
        ARG2 = av('ARG2', 64, [4, 128]); EE = av('EE', 64, [4, 128]); QP = av('QP', 64, [4, 128], BF16)
        KP = av('KP', 128, [4, 64], BF16); V1 = av('V1', 128, [4, 129], BF16)
        SIGOs = [av('SIGO', 128, [512]) for _ in range(2)]; HMN = av('HMN', 128, [4, 128]); T1 = av('T1', 128, [4, 128]); HMB = av('HMB', 128, [512], BF16)
        JK2 = av('JK2', 128, [128])
        P.add('dve', lambda e: e.memset(ONES4, 1.0), w=['ONES4'])
        P.add('dve', lambda e: e.memset(ZEROS4, 0.0), w=['ZEROS4'])
        P.add('dve', lambda e: e.memset(V1, 1.0), w=['V1'])
        if seq_first:
            P.add('dve', lambda e: e.memset(SST[:], 0.0), w=['SST'])
            P.add('dve', lambda e: e.memset(SBF[:], 0.0), w=['SBF'])
            P.add('dve', lambda e: e.memset(RCOL[:], 0.0), w=['RCOL'])
            P.add('dve', lambda e: e.memset(BC[:], 0.0), w=['BC'])
            P.add('dve', lambda e: e.memset(MC[:], 0.0), w=['MC'])
        pending_back = [None]
        NW1 = 1544
        parts = [(lambda b: b[:, 0:KC * NW1].rearrange("p (k c) -> p k c", k=KC), W_IN[:, 0:NW1].rearrange("(k p) c -> p k c", p=128))]
        buf, bi = load_slab(parts)
        W1 = buf[:, 0:KC * NW1].rearrange("p (k c) -> p k c", k=KC)
        WK = [('slab', bi, 0)]
        for m in range(NM):
            mc0 = m * MTOK
            htk = [('HT', m * TPM + j) for j in range(TPM)]
            for h in range(4):
                for which in range(2):
                    coff = which * 256 + h * 64
                    b = pcnt['qk'] % 4; pcnt['qk'] += 1
                    for k in range(KC):
                        P.add('pe', lambda e, b=b, k=k, coff=coff, mc0=mc0: e.matmul(PS[b][0:64, 0:MTOK], W1[:, k, coff:coff + 64], HT[:, k, mc0:mc0 + MTOK],
                                                                             start=(k == 0), stop=(k == KC - 1)), r=WK + htk, w=[('PS', b)])
                    if which == 0:
                        P.add('act', lambda e, b=b, h=h: e.activation(out=QT[:, h, :], in_=PS[b][0:64, 0:MTOK], func=AF.Copy, scale=0.125),
                              r=[('PS', b)], w=['QT'])
                    else:
                        P.add('dve', lambda e, b=b, h=h: e.tensor_copy(KT[:, h, :], PS[b][0:64, 0:MTOK]), r=[('PS', b)], w=['KT'])
            for (b, coff) in ((4, 1536), (5, 1540)):
                for k in range(KC):
                    P.add('pe', lambda e, b=b, k=k, coff=coff, mc0=mc0: e.matmul(PS[b][0:4, 0:MTOK], W1[:, k, coff:coff + 4], HT[:, k, mc0:mc0 + MTOK],
                                                                         start=(k == 0), stop=(k == KC - 1)), r=WK + htk, w=[('PS', b)])
            P.add('act', lambda e: e.activation(out=TI, in_=PS[4][0:4, 0:MTOK], func=AF.Tanh, scale=1.0 / 15.0, bias=BIS[:]), r=[('PS', 4), 'BIS'], w=['TI'])
            P.add('act', lambda e: e.activation(out=TF, in_=PS[5][0:4, 0:MTOK], func=AF.Tanh, scale=1.0 / 15.0, bias=BFS[:]), r=[('PS', 5), 'BFS'], w=['TF'])
            P.add('act', lambda e: e.activation(out=TF, in_=TF, func=AF.Exp, scale=-15.0), r=['TF'], w=['TF'])
            P.add('act', lambda e: e.activation(out=TF, in_=TF, func=AF.Ln, bias=1.0), r=['TF'], w=['TF'])
            P.add('dve', lambda e: e.tensor_tensor_scan(NEGB, ONES4, TF, BC[:], ALU.mult, ALU.add), r=['ONES4', 'TF', 'BC'], w=['NEGB'])
            P.add('dve', lambda e: e.tensor_copy(BC[:], NEGB[:, MTOK - 1:MTOK]), r=['NEGB'], w=['BC'])
            P.add('dve', lambda e: e.scalar_tensor_tensor(out=GG, in0=TI, scalar=15.0, in1=NEGB, op0=ALU.mult, op1=ALU.add), r=['TI', 'NEGB'], w=['GG'])
            P.add('dve', lambda e: e.tensor_tensor_scan(MMs, GG, ZEROS4, MC[:], ALU.max, ALU.max), r=['GG', 'ZEROS4', 'MC'], w=['MMs'])
            P.add('dve', lambda e: e.tensor_copy(MC[:], MMs[:, MTOK - 1:MTOK]), r=['MMs'], w=['MC'])
            P.add('dve', lambda e: e.tensor_tensor(out=NM_, in0=NEGB, in1=MMs, op=ALU.subtract), r=['NEGB', 'MMs'], w=['NMs'])
            for j in range(TPM):
                t = m * TPM + j
                cj = slice(j * 128, (j + 1) * 128)
                tcols = slice(t * 128, (t + 1) * 128)
                par = t % 2
                a_start = len(P.ops)
                ENM = ENMs[par]; SIGO = SIGOs[par]; NR = NRs[par]
                ek = ('ENM', par); sk = ('SIGO', par); nk = ('NR', par)
                P.add('pe', lambda e, cj=cj: e.transpose(PS[4][:, 0:4], GG[:, cj], IDF[0:4, 0:4]), r=['GG', 'IDF'], w=[('PS', 4)])
                P.add('pe', lambda e, cj=cj: e.transpose(PS[4][:, 4:8], NM_[:, cj], IDF[0:4, 0:4]), r=['NMs', 'IDF'], w=[('PS', 4)])
                P.add('dve', lambda e: e.tensor_copy(GCOL, PS[4][:, 0:4]), r=[('PS', 4)], w=['GCOL'])
                P.add('act', lambda e, ENM=ENM: e.activation(out=ENM, in_=PS[4][:, 4:8], func=AF.Exp), r=[('PS', 4)], w=[ek])
                for h in range(4):
                    P.add('pe', lambda e, h=h, cj=cj: e.matmul(PS[3][:, h * 128:(h + 1) * 128], SEL[0:4, h * 128:(h + 1) * 128], MMs[:, cj], start=True, stop=True),
                          r=['SEL', 'MMs'], w=[('PS', 3)])
                PS3v = PS[3][:, 0:512].rearrange("p (a b) -> p a b", a=4)
                P.add('dve', lambda e: e.tensor_copy(RNEW, PS3v[:, :, 127]), r=[('PS', 3)], w=['RNEW'])
                P.add('dve', lambda e: e.scalar_tensor_tensor(out=ARG, in0=PS3v, scalar=-1.0, in1=MASKNEG[:].unsqueeze(1).to_broadcast([128, 4, 128]),
                                                              op0=ALU.mult, op1=ALU.add), r=[('PS', 3), 'MASKNEG'], w=['ARG'])
                P.add('dve', lambda e: e.tensor_tensor(out=ARG, in0=ARG, in1=GCOL.unsqueeze(2).to_broadcast([128, 4, 128]), op=ALU.add), r=['ARG', 'GCOL'], w=['ARG'])
                P.add('act', lambda e: e.activation(out=DTm, in_=ARG, func=AF.Exp), r=['ARG'], w=['DTm'])
                P.add('dve', lambda e: e.scalar_tensor_tensor(out=ARG2, in0=PS3v[0:64], scalar=-1.0, in1=RCOL[0:64, :].unsqueeze(2).to_broadcast([64, 4, 128]),
                                                              op0=ALU.mult, op1=ALU.add), r=[('PS', 3), 'RCOL'], w=['ARG2'])
                P.add('act', lambda e: e.activation(out=EE, in_=ARG2, func=AF.Exp), r=['ARG2'], w=['EE'])
                P.add('dve', lambda e, cj=cj: e.tensor_tensor(out=QP, in0=QT[:, :, cj], in1=EE, op=ALU.mult), r=['QT', 'EE'], w=['QP'])
                for (b, c0, wd_) in ((0, 256, 256), (1, 512, 512), (2, 1024, 512)):
                    for k in range(KC):
                        P.add('pe', lambda e, b=b, c0=c0, wd_=wd_, k=k, tcols=tcols: e.matmul(PS[b][:, 0:wd_], HT[:, k, tcols], W1[:, k, c0:c0 + wd_],
                                                                                          start=(k == 0), stop=(k == KC - 1)), r=WK + [('HT', t)], w=[('PS', b)])
                P.add('dve', lambda e: e.tensor_tensor(out=WCOL, in0=GCOL, in1=RNEW, op=ALU.subtract), r=['GCOL', 'RNEW'], w=['WCOL'])
                P.add('act', lambda e: e.activation(out=WCOL, in_=WCOL, func=AF.Exp), r=['WCOL'], w=['WCOL'])
                P.add('dve', lambda e: e.tensor_tensor(out=KP, in0=PS[0][:, 0:256].rearrange("p (a b) -> p a b", a=4),
                                                       in1=WCOL.unsqueeze(2).to_broadcast([128, 4, 64]), op=ALU.mult), r=[('PS', 0), 'WCOL'], w=['KP'])
                P.add('act', lambda e: e.activation(out=V1[:, :, 0:128], in_=PS[1][:, 0:512].rearrange("p (a b) -> p a b", a=4), func=AF.Copy),
                      r=[('PS', 1)], w=['V1'])
                P.add('act', lambda e, SIGO=SIGO: e.activation(out=SIGO, in_=PS[2][:, 0:512], func=AF.Sigmoid), r=[('PS', 2)], w=[sk])
                for h in range(4):
                    P.add('pe', lambda e, h=h, cj=cj: e.matmul(PS[5][:, h * 128:(h + 1) * 128], KT[:, h, cj], QT[:, h, cj], start=True, stop=True),
                          r=['KT', 'QT'], w=[('PS', 5)])
                P.add('dve', lambda e: e.tensor_tensor(out=PD, in0=PS[5][:, 0:512].rearrange("p (a b) -> p a b", a=4), in1=DTm, op=ALU.mult),
                      r=[('PS', 5), 'DTm'], w=['PD'])
                for h in range(4):
                    b = h // 2; o0 = (h % 2) * 129
                    P.add('pe', lambda e, h=h, b=b, o0=o0: e.matmul(PS[b][:, o0:o0 + 129], PD[:, h, :], V1[:, h, :], start=True, stop=False),
                          r=['PD', 'V1'], w=[('PS', b)])
                    P.add('pe', lambda e, h=h, b=b, o0=o0: e.matmul(PS[b][:, o0:o0 + 129], QP[:, h, :], SBF[:, h, :], start=False, stop=True),
                          r=['QP', 'SBF'], w=[('PS', b)])
                for h in range(4):
                    b = (2, 4)[h // 2]; o0 = (h % 2) * 129
                    P.add('pe', lambda e, h=h, b=b, o0=o0: e.matmul(PS[b][0:64, o0:o0 + 129], KP[:, h, :], V1[:, h, :], start=True, stop=True),
                          r=['KP', 'V1'], w=[('PS', b)])
                P.add('dve', lambda e: e.tensor_tensor(out=DEC, in0=RCOL[0:64, :], in1=RNEW[0:64, :], op=ALU.subtract), r=['RCOL', 'RNEW'], w=['DEC'])
                P.add('act', lambda e: e.activation(out=DEC, in_=DEC, func=AF.Exp), r=['DEC'], w=['DEC'])
                for h in range(4):
                    b = (2, 4)[h // 2]; o0 = (h % 2) * 129
                    P.add('dve', lambda e, h=h, b=b, o0=o0: e.scalar_tensor_tensor(out=SST[:, h, :], in0=SST[:, h, :], scalar=DEC[:, h:h + 1], in1=PS[b][0:64, o0:o0 + 129],
                                                                                   op0=ALU.mult, op1=ALU.add), r=['SST', 'DEC', ('PS', b), 'SBF'], w=['SST'])
                P.add('act', lambda e: e.activation(out=SBF[:], in_=SST[:], func=AF.Copy), r=['SST'], w=['SBF'])
                P.add('dve', lambda e: e.tensor_copy(RCOL[:], RNEW), r=['RNEW'], w=['RCOL'])
                for b in range(2):
                    pv = PS[b][:, 0:258].rearrange("p (a b) -> p a b", a=2)
                    P.add('dve', lambda e, b=b, pv=pv, NR=NR: e.tensor_copy(NR[:, 2 * b:2 * b + 2, :], pv), r=[('PS', b)], w=[nk])

                def back(t=t, tcols=tcols, ENM=ENM, SIGO=SIGO, NR=NR, ek=ek, sk=sk, nk=nk):
                    P.add('dve', lambda e: e.tensor_copy(DEN, NR[:, :, 128]), r=[nk], w=['DEN'])
                    P.add('dve', lambda e: e.tensor_scalar(NEGD, DEN, -1.0, None, op0=ALU.mult), r=['DEN'], w=['NEGD'])
                    P.add('dve', lambda e: e.tensor_tensor(out=DEN, in0=DEN, in1=NEGD, op=ALU.max), r=['DEN', 'NEGD'], w=['DEN'])
                    P.add('dve', lambda e: e.tensor_tensor(out=DEN, in0=DEN, in1=ENM, op=ALU.max), r=['DEN', ek], w=['DEN'])
                    P.add('dve', lambda e: e.reciprocal(RRm, DEN), r=['DEN'], w=['RRm'])
                    P.add('dve', lambda e: e.tensor_tensor(out=HMN, in0=NR[:, :, 0:128], in1=RRm.unsqueeze(2).to_broadcast([128, 4, 128]), op=ALU.mult),
                          r=[nk, 'RRm'], w=['HMN'])
                    for h in range(4):
                        P.add('act', lambda e, h=h: e.activation(out=JK2, in_=HMN[:, h, :], func=AF.Square, accum_out=SS2[:, h:h + 1]), r=['HMN'], w=['JK2', 'SS2'])
                    P.add('dve', lambda e: e.tensor_scalar(RS2, SS2, 1.0 / 128.0, EPS, op0=ALU.mult, op1=ALU.add), r=['SS2'], w=['RS2'])
                    P.add('act', lambda e: e.activation(out=RS2, in_=RS2, func=AF.Ln), r=['RS2'], w=['RS2'])
                    P.add('act', lambda e: e.activation(out=RS2, in_=RS2, func=AF.Exp, scale=-0.5), r=['RS2'], w=['RS2'])
                    P.add('dve', lambda e: e.tensor_tensor(out=T1, in0=HMN, in1=RS2.unsqueeze(2).to_broadcast([128, 4, 128]), op=ALU.mult), r=['HMN', 'RS2'], w=['T1'])
                    P.add('dve', lambda e: e.tensor_tensor(out=SIGO, in0=SIGO, in1=OG[:], op=ALU.mult), r=[sk, 'OG'], w=[sk])
                    P.add('dve', lambda e: e.tensor_tensor(out=HMB, in0=T1.rearrange("p a b -> p (a b)"), in1=SIGO, op=ALU.mult), r=['T1', sk], w=['HMB'])
                    for kc in range(4):
                        P.add('pe', lambda e, kc=kc: e.transpose(PSB[1][:, kc * 128:(kc + 1) * 128], HMB[:, kc * 128:(kc + 1) * 128], IDB[:]), r=['HMB', 'IDB'], w=[('PSB', 1)])
                    P.add('dve', lambda e: e.tensor_copy(HM[:, :, tcols], PSB[1][:, 0:512].rearrange("p (a b) -> p a b", a=4)), r=[('PSB', 1)], w=[('HM', t)])
                if pending_back[0] is not None:
                    A = P.ops[a_start:]; del P.ops[a_start:]
                    b_start = len(P.ops)
                    pending_back[0]()
                    B = P.ops[b_start:]; del P.ops[b_start:]
                    lead = 12
                    step = max(1, (len(A) - lead) // (len(B) + 1))
                    merged = []; bi_ = 0
                    for ai, op in enumerate(A):
                        merged.append(op)
                        if ai >= lead and (ai - lead) % step == step - 1 and bi_ < len(B):
                            merged.append(B[bi_]); bi_ += 1
                    merged.extend(B[bi_:])
                    P.ops.extend(merged)
                pending_back[0] = back
        if pending_back[0] is not None:
            pending_back[0]()
            pending_back[0] = None

        abarrier()
        QKVS = av('QKVS', 128, [768]); QKB = av('QKB', 128, [10, 64], BF16)
        TA = av('TA', 128, [10, 8]); TB = av('TB', 128, [10, 8])
        QTA = av('QTA', 64, [8, 128], BF16)
        EB = [[av('EB', 128, [4, 128], BF16) for _ in range(2)] for _ in range(2)]
        PT = [[av('PT', 128, [4, 128], BF16) for _ in range(2)] for _ in range(2)]
        OAB = av('OAB', 128, [8, 64], BF16); DENA = av('DENA', 128, [8]); RRA = av('RRA', 128, [8])
        NW2 = 768
        parts = [(lambda b: b[:, 0:KC * NW2].rearrange("p (k c) -> p k c", k=KC), W_IN[:, 1544:1544 + NW2].rearrange("(k p) c -> p k c", p=128))]
        buf, bi = load_slab(parts)
        W2 = buf[:, 0:KC * NW2].rearrange("p (k c) -> p k c", k=KC)
        WK = [('slab', bi, 0)]
        XR = QKVS[:, 0:640].rearrange("p (a b) -> p a b", a=10)
        for t in range(TT):
            gt = T * TT + t
            lt = (T % STPS) * TT + t
            par = gt % 2
            tcols = slice(t * 128, (t + 1) * 128)
            for (b, c0, wd_) in ((0, 0, 512), (1, 512, 256)):
                for k in range(KC):
                    P.add('pe', lambda e, b=b, c0=c0, wd_=wd_, k=k, tcols=tcols: e.matmul(PS[b][:, 0:wd_], HT[:, k, tcols], W2[:, k, c0:c0 + wd_],
                                                                                      start=(k == 0), stop=(k == KC - 1)), r=WK + [('HT', t)], w=[('PS', b)])
            P.add('act', lambda e: e.activation(out=QKVS[:, 0:512], in_=PS[0][:, 0:512], func=AF.Copy, scale=0.125), r=[('PS', 0)], w=['QKVS'])
            P.add('act', lambda e: e.activation(out=QKVS[:, 512:768], in_=PS[1][:, 0:256], func=AF.Copy), r=[('PS', 1)], w=['QKVS'])
            cosb = COS[:, gt, :].unsqueeze(1).to_broadcast([128, 10, 8]); sinb = SIN[:, gt, :].unsqueeze(1).to_broadcast([128, 10, 8])
            x1 = XR[:, :, 0:8]; x2 = XR[:, :, 8:16]
            P.add('dve', lambda e, x1=x1, cosb=cosb: e.tensor_tensor(out=TA, in0=x1, in1=cosb, op=ALU.mult), r=['QKVS', 'COS'], w=['TA'])
            P.add('dve', lambda e, x2=x2, sinb=sinb: e.tensor_tensor(out=TB, in0=x2, in1=sinb, op=ALU.mult), r=['QKVS', 'SIN'], w=['TB'])
            P.add('dve', lambda e: e.tensor_tensor(out=QKB[:, :, 0:8], in0=TA, in1=TB, op=ALU.subtract), r=['TA', 'TB'], w=['QKB'])
            P.add('dve', lambda e, x2=x2, cosb=cosb: e.tensor_tensor(out=TA, in0=x2, in1=cosb, op=ALU.mult), r=['QKVS', 'COS', 'QKB'], w=['TA'])
            P.add('dve', lambda e, x1=x1, sinb=sinb: e.tensor_tensor(out=TB, in0=x1, in1=sinb, op=ALU.mult), r=['QKVS', 'SIN', 'QKB'], w=['TB'])
            P.add('dve', lambda e: e.tensor_tensor(out=QKB[:, :, 8:16], in0=TA, in1=TB, op=ALU.add), r=['TA', 'TB'], w=['QKB'])
            P.add('dve', lambda e: e.tensor_copy(QKB[:, :, 16:64], XR[:, :, 16:64]), r=['QKVS'], w=['QKB'])
            P.add('act', lambda e, par=par: e.activation(out=V1A[par][:, :, 0:64], in_=QKVS[:, 640:768].rearrange("p (a b) -> p a b", a=2), func=AF.Copy),
                  r=['QKVS'], w=[('V1A', par)])
            for h in range(8):
                P.add('pe', lambda e, h=h: e.transpose(PSB[0][0:64, h * 128:(h + 1) * 128], QKB[:, h, :], IDB[:]), r=['QKB', 'IDB'], w=[('PSB', 0)])
            for kv in range(2):
                P.add('pe', lambda e, kv=kv: e.transpose(PSB[1][0:64, kv * 128:(kv + 1) * 128], QKB[:, 8 + kv, :], IDB[:]), r=['QKB', 'IDB'], w=[('PSB', 1)])
            P.add('dve', lambda e: e.tensor_copy(QTA, PSB[0][0:64, 0:1024].rearrange("p (a b) -> p a b", a=8)), r=[('PSB', 0)], w=['QTA'])
            P.add('dve', lambda e, par=par: e.tensor_copy(KTA[par][:], PSB[1][0:64, 0:256].rearrange("p (a b) -> p a b", a=2)), r=[('PSB', 1)], w=[('KTA', par)])
            has_prev = lt > 0
            for kv in range(2):
                for pc in range(2):
                    if pc == 1 and not has_prev:
                        continue
                    b = 2 + 2 * kv + pc
                    src = par if pc == 0 else 1 - par
                    P.add('pe', lambda e, b=b, kv=kv, src=src: e.matmul(PS[b][:, 0:512], KTA[src][:, kv, :], QTA[:, 4 * kv:4 * kv + 4, :], start=True, stop=True),
                          r=[('KTA', src), 'QTA'], w=[('PS', b)])
                    eb = EB[kv][pc]; pt = PT[kv][pc]
                    mk = MCUR if pc == 0 else MPREV
                    P.add('act', lambda e, b=b, eb=eb: e.activation(out=eb, in_=PS[b][:, 0:512].rearrange("p (a b) -> p a b", a=4), func=AF.Exp), r=[('PS', b)], w=['EB'])
                    P.add('dve', lambda e, eb=eb, pt=pt, mk=mk: e.tensor_tensor(out=pt, in0=eb, in1=mk[:].unsqueeze(1).to_broadcast([128, 4, 128]), op=ALU.mult),
                          r=['EB', 'MCUR', 'MPREV'], w=['PT'])
            for h in range(8):
                kv = h // 4; hh = h % 4; b = h // 4; o0 = hh * 65
                if has_prev:
                    P.add('pe', lambda e, b=b, o0=o0, kv=kv, hh=hh, par=par: e.matmul(PS[b][:, o0:o0 + 65], PT[kv][1][:, hh, :], V1A[1 - par][:, kv, :], start=True, stop=False),
                          r=['PT', ('V1A', 1 - par)], w=[('PS', b)])
                P.add('pe', lambda e, b=b, o0=o0, kv=kv, hh=hh, par=par, has_prev=has_prev: e.matmul(PS[b][:, o0:o0 + 65], PT[kv][0][:, hh, :], V1A[par][:, kv, :],
                                                                                              start=(not has_prev), stop=True),
                      r=['PT', ('V1A', par)], w=[('PS', b)])
            for b in range(2):
                pv = PS[b][:, 0:260].rearrange("p (a b) -> p a b", a=4)
                P.add('dve', lambda e, b=b, pv=pv: e.tensor_tensor(out=DENA[:, 4 * b:4 * b + 4], in0=pv[:, :, 64], in1=ESINK[:, 4 * b:4 * b + 4], op=ALU.add),
                      r=[('PS', b), 'ESINK'], w=['DENA'])
            P.add('dve', lambda e: e.reciprocal(RRA, DENA), r=['DENA'], w=['RRA'])
            for b in range(2):
                pv = PS[b][:, 0:260].rearrange("p (a b) -> p a b", a=4)
                P.add('dve', lambda e, b=b, pv=pv: e.tensor_tensor(out=OAB[:, 4 * b:4 * b + 4, :], in0=pv[:, :, 0:64],
                                                                   in1=RRA[:, 4 * b:4 * b + 4].unsqueeze(2).to_broadcast([128, 4, 64]), op=ALU.mult),
                      r=[('PS', b), 'RRA'], w=['OAB'])
            OABf = OAB.rearrange("p a b -> p (a b)")
            for kc in range(4):
                P.add('pe', lambda e, kc=kc, OABf=OABf: e.transpose(PSB[1][:, kc * 128:(kc + 1) * 128], OABf[:, kc * 128:(kc + 1) * 128], IDB[:]), r=['OAB', 'IDB'], w=[('PSB', 1)])
            P.add('dve', lambda e, tcols=tcols: e.tensor_copy(OA[:, :, tcols], PSB[1][:, 0:512].rearrange("p (a b) -> p a b", a=4)), r=[('PSB', 1)], w=[('OA', t)])

        abarrier()
        MG = av('MG', 128, [KC, ST], BF16)
        SGG = [av('SGG', 128, [MTOK]) for _ in range(2)]
        TMPG = [av('TMPG', 128, [MTOK]) for _ in range(2)]
        for (br, gc0, wbr, SRC, skey) in (('m', 2312, W['w_branch_mlstm'], HM, 'HM'), ('a', 2312 + DM, W['w_branch_attn'], OA, 'OA')):
            GO, BO = 0, KC * DM
            parts = [(lambda b: b[:, GO:GO + KC * DM].rearrange("p (k c) -> p k c", k=KC), W_IN[:, gc0:gc0 + DM].rearrange("(k p) c -> p k c", p=128)),
                     (lambda b: b[:, BO:BO + 4 * DM].rearrange("p (k c) -> p k c", k=4), wbr.rearrange("(k p) c -> p k c", p=128))]
            buf, bi = load_slab(parts)
            W3 = buf[:, GO:GO + KC * DM].rearrange("p (k c) -> p k c", k=KC)
            WB = buf[:, BO:BO + 4 * DM].rearrange("p (k c) -> p k c", k=4)
            for m in range(NM):
                mc0 = m * MTOK
                htk = [('HT', m * TPM + j) for j in range(TPM)]
                srk = [(skey, m * TPM + j) for j in range(TPM)]
                for d in range(KC):
                    gb = pcnt['g'] % 2; pcnt['g'] += 1
                    Gp, Yp = PS[2 * gb], PS[2 * gb + 1]
                    for k in range(KC):
                        P.add('pe', lambda e, Gp=Gp, k=k, d=d, mc0=mc0, W3=W3: e.matmul(Gp[:, 0:MTOK], W3[:, k, d * 128:(d + 1) * 128], HT[:, k, mc0:mc0 + MTOK],
                                                                                start=(k == 0), stop=(k == KC - 1)), r=[('slab', bi, 0)] + htk, w=[('PS', 2 * gb)])
                    for k in range(4):
                        P.add('pe', lambda e, Yp=Yp, k=k, d=d, mc0=mc0, WB=WB, SRC=SRC: e.matmul(Yp[:, 0:MTOK], WB[:, k, d * 128:(d + 1) * 128], SRC[:, k, mc0:mc0 + MTOK],
                                                                                         start=(k == 0), stop=(k == 3)), r=[('slab', bi, 1)] + srk, w=[('PS', 2 * gb + 1)])
                    sgg = SGG[gb]
                    P.add('act', lambda e, sgg=sgg, Gp=Gp: e.activation(out=sgg, in_=Gp[:, 0:MTOK], func=AF.Sigmoid), r=[('PS', 2 * gb)], w=[('SGG', gb)])
                    if br == 'm':
                        P.add('dve', lambda e, sgg=sgg, Yp=Yp, d=d, mc0=mc0: e.tensor_tensor(out=MG[:, d, mc0:mc0 + MTOK], in0=Yp[:, 0:MTOK], in1=sgg, op=ALU.mult),
                              r=[('PS', 2 * gb + 1), ('SGG', gb)], w=[('MG', d, m)])
                    else:
                        tg = TMPG[gb]
                        P.add('dve', lambda e, sgg=sgg, Yp=Yp, tg=tg: e.tensor_tensor(out=tg, in0=Yp[:, 0:MTOK], in1=sgg, op=ALU.mult),
                              r=[('PS', 2 * gb + 1), ('SGG', gb)], w=[('TMPG', gb)])
                        P.add('dve', lambda e, tg=tg, d=d, mc0=mc0: e.tensor_tensor(out=MG[:, d, mc0:mc0 + MTOK], in0=MG[:, d, mc0:mc0 + MTOK], in1=tg, op=ALU.add),
                              r=[('TMPG', gb), ('MG', d, m)], w=[('MG', d, m)])
        parts = [(lambda b: b[:, 0:KC * DM].rearrange("p (k c) -> p k c", k=KC), W['w_out'].rearrange("(k p) c -> p k c", p=128))]
        buf, bi = load_slab(parts)
        WO = buf[:, 0:KC * DM].rearrange("p (k c) -> p k c", k=KC)
        for t in range(TT):
            m = t // TPM
            tcols = slice(t * 128, (t + 1) * 128)
            for h in range(NH):
                ob = pcnt['o'] % 2; pcnt['o'] += 1
                Op = PS[4 + ob]
                for k in range(KC):
                    P.add('pe', lambda e, Op=Op, k=k, h=h, tcols=tcols: e.matmul(Op[:, 0:HW], MG[:, k, tcols], WO[:, k, h * HW:(h + 1) * HW], start=(k == 0), stop=(k == KC - 1)),
                          r=[('slab', bi, 0)] + [('MG', d, m) for d in range(KC)], w=[('PS', 4 + ob)])
                P.add('dve', lambda e, Op=Op, t=t, h=h: e.tensor_tensor(out=X[:, t, h * HW:(h + 1) * HW], in0=Op[:, 0:HW], in1=X[:, t, h * HW:(h + 1) * HW], op=ALU.add),
                      r=[('PS', 4 + ob), ('X', t)], w=[('X', t)])


    for T in range(NST):
        load_x(T)
        if 'ffn1' in phases:
            ffn(1)
        if 'mix' in phases:
            mixer(T)
        if 'ffn2' in phases:
            ffn(2)
        final_out(T)
    P.add('sp', None, r=[('dmasem', 'o0'), ('dmasem', 'o1')])
    P.build(st)
    st.close()
    return nc, P


def make_consts():
    s = np.arange(128)[:, None]; j = np.arange(128)[None, :]
    c = {}
    c['ident_bf'] = np.eye(128).astype(ml_dtypes.bfloat16)
    c['ident_f'] = np.eye(128).astype(np.float32)
    sel = np.zeros((4, 4, 128), np.float32)
    for h in range(4):
        sel[h, h, :] = 1.0
    c['sel'] = sel.reshape(4, 512)
    c['maskneg'] = np.where(s <= j, 0.0, -1e30).astype(np.float32)
    c['mask_cur'] = (s <= j).astype(np.float32).astype(ml_dtypes.bfloat16)
    c['mask_prev'] = (s > j).astype(np.float32).astype(ml_dtypes.bfloat16)
    c['inv_freq'] = (500000.0 ** (-np.arange(8, dtype=np.float32) * 2.0 / 16)).astype(np.float32)
    return c


def make_in_maps(cfg, inputs):
    DM, S, NSEQ, NC_ = cfg['DM'], cfg['S'], cfg['NSEQ'], cfg['NCORES']
    KC = DM // 128
    consts = make_consts()
    shared = dict(consts)
    f32 = lambda a: np.ascontiguousarray(np.asarray(a, dtype=np.float32))
    for n in WNAMES:
        shared[n] = f32(inputs[n][0])
    for n, k in (('ffn1_norm_g', 'ffn1_norm_gT'), ('mix_norm_g', 'mix_norm_gT'), ('ffn2_norm_g', 'ffn2_norm_gT')):
        shared[k] = f32(np.asarray(inputs[n][0]).reshape(KC, 128).T)
    shared['final_norm_g'] = f32(inputs['final_norm_g'])
    shared['mlstm_b_i'] = f32(np.asarray(inputs['mlstm_b_i'][0]).reshape(4, 1))
    shared['mlstm_b_f'] = f32(np.asarray(inputs['mlstm_b_f'][0]).reshape(4, 1))
    shared['mlstm_out_norm_g'] = f32(inputs['mlstm_out_norm_g'][0])
    shared['attn_sinks'] = f32(inputs['attn_sinks'][0])
    x = np.asarray(inputs['x'], dtype=np.float32)
    pos = np.asarray(inputs['positions'], dtype=np.int32)
    maps = []
    for c in range(NC_):
        m = dict(shared)
        m['x'] = np.ascontiguousarray(x[c * NSEQ:(c + 1) * NSEQ].reshape(NSEQ * S, DM))
        m['posT'] = np.ascontiguousarray(pos[c * NSEQ:(c + 1) * NSEQ].reshape(-1, 128).T)
        maps.append(m)
    return maps


def run(cfg, inputs, phases=('ffn1', 'mix', 'ffn2'), sim=False, trace=False):
    nc, P = build_program(cfg, phases)
    maps = make_in_maps(cfg, inputs)
    res = run_bass_kernel_spmd(nc, maps, core_ids=list(range(cfg['NCORES']))).results
    out = np.concatenate([r['out'] for r in res], axis=0)
    B = cfg['NSEQ'] * cfg['NCORES']
    return out.reshape(B, cfg['S'], cfg['DM'])


def kernel(**inputs):
    cfg = FULL
    out = run(cfg, inputs)
    return np.ascontiguousarray(out.astype(np.float32))
```

```python
import numpy as np
import concourse.bass as bass
import concourse.mybir as mybir
from contextlib import ExitStack

F32 = mybir.dt.float32; BF16 = mybir.dt.bfloat16; I32 = mybir.dt.int32
AF = mybir.ActivationFunctionType; ALU = mybir.AluOpType; AX = mybir.AxisListType

import os
STRICT = os.environ.get('PROG_STRICT', '0') == '1'

class Prog:
    ENGS = ('pe', 'act', 'dve', 'pool', 'sp')
    def __init__(self, nc):
        self.nc = nc
        self.ops = []
    def add(self, eng, fn, r=(), w=(), dma=None):
        self.ops.append([eng, fn, tuple(r), tuple(w), dma])
    def barrier(self, tag):
        for e in self.ENGS:
            self.add(e, lambda eng: eng.nop(), r=(), w=[('bar', tag, e)])
        for e in self.ENGS:
            self.add(e, lambda eng: eng.nop(), r=[('bar', tag, e2) for e2 in self.ENGS if e2 != e], w=())
    def build(self, stack):
        nc = self.nc
        ops = self.ops
        n = len(ops)
        last_w = {}
        readers = {}
        deps = [None] * n
        needs_sig = [False] * n
        for i, (eng, fn, r, w, dma) in enumerate(ops):
            d = {}
            def add_dep(j, kind):
                if j == i:
                    return
                e2, _, _, _, dma2 = ops[j]
                if e2 == eng and dma2 is None and dma is None:
                    if eng == 'pe' or (kind != 'raw' and not STRICT):
                        return
                d[j] = True
            rk = list(r)
            wk = list(w)
            if dma is not None:
                wk.append(('dmasem', dma))
            for k in rk:
                if k in last_w:
                    add_dep(last_w[k], 'raw')
            for k in wk:
                if k in last_w:
                    add_dep(last_w[k], 'waw')
                for j in readers.get(k, ()):
                    add_dep(j, 'war')
            for k in rk:
                readers.setdefault(k, []).append(i)
            for k in wk:
                last_w[k] = i
                readers[k] = []
            deps[i] = list(d.keys())
            for j in deps[i]:
                needs_sig[j] = True
        sig = [None] * n
        cnt = {}
        semnames = set()
        for i, (eng, fn, r, w, dma) in enumerate(ops):
            if not needs_sig[i] and dma is None:
                continue
            name = ('dma_' + dma) if dma is not None else ('eng_' + eng)
            inc = 16 if dma is not None else 1
            cnt[name] = cnt.get(name, 0) + inc
            sig[i] = (name, cnt[name], inc)
            semnames.add(name)
        sems = {}
        for name in sorted(semnames):
            sems[name] = stack.enter_context(nc.semaphore(name))
        self.sem_counts = cnt
        per_eng = {e: [] for e in self.ENGS}
        waited = {e: {} for e in self.ENGS}
        nwaits = 0
        for i, (eng, fn, r, w, dma) in enumerate(ops):
            need = {}
            for j in deps[i]:
                name, val, _ = sig[j]
                if need.get(name, 0) < val:
                    need[name] = val
            ws = []
            for name, val in need.items():
                if waited[eng].get(name, 0) < val:
                    waited[eng][name] = val
                    ws.append((sems[name], val))
                    nwaits += 1
            s = None
            if sig[i] is not None:
                s = (sems[sig[i][0]], sig[i][2])
            per_eng[eng].append((ws, fn, s))
        self.nwaits = nwaits
        block = stack.enter_context(nc.Block())
        def mk(lst):
            def body(eng):
                for ws, fn, s in lst:
                    for sem, val in ws:
                        eng.wait_ge(sem, val)
                    if fn is None:
                        assert s is None
                        continue
                    ins = fn(eng)
                    if s is not None:
                        ins.then_inc(s[0], s[1])
            return body
        block.tensor(mk(per_eng['pe']))
        block.scalar(mk(per_eng['act']))
        block.vector(mk(per_eng['dve']))
        block.gpsimd(mk(per_eng['pool']))
        block.sync(mk(per_eng['sp']))


import numpy as np
import ml_dtypes
import concourse.bass as bass
import concourse.mybir as mybir
from contextlib import ExitStack
from concourse.bass_utils import run_bass_kernel_spmd

F32 = mybir.dt.float32; BF16 = mybir.dt.bfloat16; I32 = mybir.dt.int32
AF = mybir.ActivationFunctionType; ALU = mybir.AluOpType; AX = mybir.AxisListType
EPS = 1e-6

FULL = dict(DM=1024, DFF=2816, S=2048, NSEQ=2, ST=1024, NCORES=8)

WNAMES = ['ffn1_w_gate', 'ffn1_w_up', 'ffn1_w_down', 'w_in', 'w_branch_mlstm', 'w_branch_attn', 'w_out',
          'ffn2_w_gate', 'ffn2_w_up', 'ffn2_w_down']


def build_program(cfg, phases=('ffn1', 'mix', 'ffn2')):
    DM, DFF, S, NSEQ, ST = cfg['DM'], cfg['DFF'], cfg['S'], cfg['NSEQ'], cfg['ST']
    KC = DM // 128
    FC = DFF // 128
    TT = ST // 128
    MTOK = min(512, ST)
    NM = ST // MTOK
    TPM = MTOK // 128
    NTOK = NSEQ * S
    NST = NTOK // ST
    STPS = S // ST
    NTILES = NTOK // 128
    INW = 2312 + 2 * DM
    HW = min(512, DM)
    NH = DM // HW

    nc = bass.Bass("TRN2", target_bir_lowering=False)
    st = ExitStack()
    P = Prog(nc)

    def din(name, shape, dt=F32):
        return nc.dram_tensor(name, list(shape), dt, kind="ExternalInput").ap()
    x_d = din('x', [NTOK, DM])
    pos_d = din('posT', [128, NTILES], I32)
    W = {}
    W['ffn1_w_gate'] = din('ffn1_w_gate', [DM, DFF]); W['ffn1_w_up'] = din('ffn1_w_up', [DM, DFF]); W['ffn1_w_down'] = din('ffn1_w_down', [DFF, DM])
    W['ffn2_w_gate'] = din('ffn2_w_gate', [DM, DFF]); W['ffn2_w_up'] = din('ffn2_w_up', [DM, DFF]); W['ffn2_w_down'] = din('ffn2_w_down', [DFF, DM])
    W['w_in'] = din('w_in', [DM, INW]); W['w_branch_mlstm'] = din('w_branch_mlstm', [512, DM]); W['w_branch_attn'] = din('w_branch_attn', [512, DM])
    W['w_out'] = din('w_out', [DM, DM])
    g1_d = din('ffn1_norm_gT', [128, KC]); gm_d = din('mix_norm_gT', [128, KC]); g2_d = din('ffn2_norm_gT', [128, KC])
    fg_d = din('final_norm_g', [DM])
    bi_d = din('mlstm_b_i', [4, 1]); bf_d = din('mlstm_b_f', [4, 1])
    og_d = din('mlstm_out_norm_g', [512]); sink_d = din('attn_sinks', [8])
    identb_d = din('ident_bf', [128, 128], BF16); identf_d = din('ident_f', [128, 128])
    sel_d = din('sel', [4, 512]); maskneg_d = din('maskneg', [128, 128])
    mcur_d = din('mask_cur', [128, 128], BF16); mprev_d = din('mask_prev', [128, 128], BF16)
    invf_d = din('inv_freq', [8])
    out_d = nc.dram_tensor('out', [NTOK, DM], F32, kind="ExternalOutput").ap()

    def sb(name, shape, dt=F32):
        return st.enter_context(nc.sbuf_tensor(name, list(shape), dt))
    XSL = TT + TT // 2
    X = sb('X', [128, XSL, DM])
    xbase = [0]
    def xs(t):
        return (xbase[0] + t) % XSL
    HT = sb('HT', [128, KC, ST], BF16)
    SLAB_E = 12352
    SLAB = [sb('SLAB%d' % i, [128, SLAB_E], BF16) for i in range(2)]
    Y = [sb('Y0', [128, DM])]
    FG = sb('FG', [128, DM])
    GT = {1: sb('G1', [128, KC]), 'm': sb('GM', [128, KC]), 2: sb('G2', [128, KC])}
    IDB = sb('IDB', [128, 128], BF16)
    SS = sb('SS', [128, TT]); RS = sb('RS', [128, TT])
    HB = [sb('HB%d' % i, [128, DM], BF16) for i in range(2)]
    HM = sb('HM', [128, 4, ST], BF16)
    OA = sb('OA', [128, 4, ST], BF16)
    IDF = sb('IDF', [128, 128])
    SEL = sb('SEL', [4, 512])
    MASKNEG = sb('MASKNEG', [128, 128])
    MCUR = sb('MCUR', [128, 128], BF16); MPREV = sb('MPREV', [128, 128], BF16)
    OG = sb('OG', [128, 512])
    ESINK = sb('ESINK', [128, 8])
    BIS = sb('BIS', [4, 1]); BFS = sb('BFS', [4, 1])
    COS = sb('COS', [128, NTILES, 8]); SIN = sb('SIN', [128, NTILES, 8])
    SST = sb('SST', [64, 4, 129]); SBF = sb('SBF', [64, 4, 129], BF16)
    RCOL = sb('RCOL', [128, 4]); BC = sb('BC', [4, 1]); MC = sb('MC', [4, 1])
    V1A = [sb('V1A%d' % i, [128, 2, 65], BF16) for i in range(3)]
    KTA = [sb('KTA%d' % i, [64, 2, 128], BF16) for i in range(3)]
    DUMMY = sb('DUMMY', [128, 8])
    AE = 25600
    ARENA = sb('ARENA', [128, AE], BF16)
    arena_names = set()
    aoff = [0]
    def areset():
        aoff[0] = 0
    def av(name, parts, free, dt=F32):
        n = int(np.prod(free))
        four = dt in (F32, I32)
        ne = n * (2 if four else 1)
        ne = (ne + 15) // 16 * 16
        a = aoff[0]; aoff[0] += ne
        assert aoff[0] <= AE, (name, aoff[0])
        v = ARENA[0:parts, a:a + n * (2 if four else 1)]
        if four:
            v = v.bitcast(dt)
        if len(free) == 2:
            v = v.rearrange("p (a b) -> p a b", a=free[0])
        arena_names.add(name)
        return v
    _add = P.add
    def padd(eng, fn, r=(), w=(), dma=None):
        r = list(r)
        for k in list(r) + list(w):
            b = k[0] if isinstance(k, tuple) else k
            if b in arena_names:
                r.append('A'); break
        _add(eng, fn, r, w, dma)
    P.add = padd
    bar_ctr = [0]
    def abarrier():
        P.add('dve', lambda e: e.memset(DUMMY[:], 0.0), w=['A', 'DUMMY'])
        areset()
    def ffn_views():
        abarrier()
        SGv = [av('SG', 128, [MTOK]) for i in range(2)]
        ACv = [av('ACTT', 128, [4, MTOK], BF16) for i in range(2)]
        return SGv, ACv

    PS = [st.enter_context(nc.psum_tensor('PS%d' % i, [128, 512], F32)) for i in range(6)]
    PSB = [st.enter_context(nc.psum_tensor('PSB%d' % i, [128, 1024], BF16)) for i in range(2)]

    cl = 0
    def cload(dst, src, key):
        nonlocal cl
        P.add('sp', lambda e: e.dma_start(out=dst, in_=src), w=[key], dma='c%d' % cl)
        cl += 1
    cload(IDB[:], identb_d, 'IDB')
    cload(FG[:], fg_d.partition_broadcast(128), 'FG')
    cload(GT[1][:], g1_d, 'GT1'); cload(GT['m'][:], gm_d, 'GTm'); cload(GT[2][:], g2_d, 'GT2')

    slab_ctr = [0]
    def load_slab(parts):
        i = slab_ctr[0]; slab_ctr[0] += 1
        buf = SLAB[i % 2]
        for pi, (mk_dst, src) in enumerate(parts):
            dst = mk_dst(buf)
            wk = [('slab', i % 2, q) for q in range(3)] if pi == 0 else [('slab', i % 2, pi)]
            P.add('pool', lambda e, dst=dst, src=src: e.dma_start(out=dst, in_=src),
                  w=wk, dma='slab%d_%d' % (i % 2, pi))
        return buf, i % 2

    def load_x(T, tiles):
        for t in tiles:
            r0 = T * ST + t * 128
            sl = (T * TT + t) % XSL
            P.add('sp', lambda e, sl=sl, r0=r0: e.dma_start(out=X[:, sl, :], in_=x_d[r0:r0 + 128, :]),
                  w=[('X', sl)], dma='x%d' % (t % 4))

    load_x(0, range(TT))

    def rstd_batch(tiles):
        t0_, t1_ = tiles[0], tiles[-1] + 1
        for t in tiles:
            sl = xs(t)
            P.add('act', lambda e, t=t, sl=sl: e.activation(out=HB[0][:], in_=X[:, sl, :], func=AF.Square, accum_out=SS[:, t:t + 1]),
                  r=[('X', sl)], w=[('HB', 0), ('SS', t)])
        P.add('dve', lambda e: e.tensor_scalar(RS[:, t0_:t1_], SS[:, t0_:t1_], 1.0 / DM, EPS, op0=ALU.mult, op1=ALU.add),
              r=[('SS', t) for t in tiles], w=[('RS', t0_)])
        P.add('act', lambda e: e.activation(out=RS[:, t0_:t1_], in_=RS[:, t0_:t1_], func=AF.Ln), r=[('RS', t0_)], w=[('RS', t0_)])
        P.add('act', lambda e: e.activation(out=RS[:, t0_:t1_], in_=RS[:, t0_:t1_], func=AF.Exp, scale=-0.5), r=[('RS', t0_)], w=[('RS', t0_)])

    def norm_to_HT(gkey):
        G = GT[gkey]
        halves = [list(range(0, TT // 2)), list(range(TT // 2, TT))]
        for tiles in halves:
            rstd_batch(tiles)
            rk = ('RS', tiles[0])
            for t in tiles:
                hb = HB[t % 2]
                sl = xs(t)
                P.add('act', lambda e, t=t, hb=hb, sl=sl: e.activation(out=hb[:], in_=X[:, sl, :], func=AF.Copy, scale=RS[:, t:t + 1]),
                      r=[('X', sl), rk], w=[('HB', t % 2)])
                tp = PSB[0]
                for k in range(KC):
                    P.add('pe', lambda e, k=k, hb=hb, tp=tp: e.transpose(tp[:, k * 128:(k + 1) * 128], hb[:, k * 128:(k + 1) * 128], IDB[:]),
                          r=[('HB', t % 2), 'IDB'], w=[('PSB', 0)])
                P.add('dve', lambda e, t=t, tp=tp: e.tensor_tensor(
                    out=HT[:, :, t * 128:(t + 1) * 128],
                    in0=tp[:, 0:KC * 128].rearrange("p (k c) -> p k c", k=KC),
                    in1=G[:].unsqueeze(2).to_broadcast([128, KC, 128]), op=ALU.mult),
                    r=[('PSB', 0), 'GT%s' % gkey], w=[('HT', t)])

    cnt = dict(gu=0, o=0, act=0)
    def ffn(f):
        wg, wu, wd = W['ffn%d_w_gate' % f], W['ffn%d_w_up' % f], W['ffn%d_w_down' % f]
        norm_to_HT(f)
        SG, ACTT = ffn_views()
        groups = []
        c = 0
        while c < FC:
            n = min(4, FC - c)
            groups.append((c, n)); c += n
        for (c0, n) in groups:
            WGO, WUO, WDO = 0, KC * 512, 2 * KC * 512
            parts = [
                (lambda b, n=n: b[:, WGO:WGO + KC * n * 128].rearrange("p (k c) -> p k c", k=KC),
                 wg[:, c0 * 128:(c0 + n) * 128].rearrange("(k p) c -> p k c", p=128)),
                (lambda b, n=n: b[:, WUO:WUO + KC * n * 128].rearrange("p (k c) -> p k c", k=KC),
                 wu[:, c0 * 128:(c0 + n) * 128].rearrange("(k p) c -> p k c", p=128)),
                (lambda b, n=n: b[:, WDO:WDO + n * DM].rearrange("p (c d) -> p c d", c=n),
                 wd[c0 * 128:(c0 + n) * 128, :].rearrange("(c p) d -> p c d", p=128)),
            ]
            buf, bi = load_slab(parts)
            Wg = buf[:, WGO:WGO + KC * n * 128].rearrange("p (k c) -> p k c", k=KC)
            Wu = buf[:, WUO:WUO + KC * n * 128].rearrange("p (k c) -> p k c", k=KC)
            Wd = buf[:, WDO:WDO + n * DM].rearrange("p (c d) -> p c d", c=n)
            for m in range(NM):
                ab = cnt['act'] % 2; cnt['act'] += 1
                at = ACTT[ab]
                htk = [('HT', m * TPM + j) for j in range(TPM)]
                for ci in range(n):
                    gb = cnt['gu'] % 2; cnt['gu'] += 1
                    Gp, Up = PS[2 * gb], PS[2 * gb + 1]
                    for (Wx, Pp, pi, bk) in ((Wg, Gp, 0, 2 * gb), (Wu, Up, 1, 2 * gb + 1)):
                        for k in range(KC):
                            P.add('pe', lambda e, Wx=Wx, Pp=Pp, k=k, ci=ci, m=m: e.matmul(
                                Pp[:, 0:MTOK], Wx[:, k, ci * 128:(ci + 1) * 128], HT[:, k, m * MTOK:(m + 1) * MTOK],
                                start=(k == 0), stop=(k == KC - 1)),
                                r=[('slab', bi, pi)] + htk, w=[('PS', bk)])
                    sg = SG[gb]
                    P.add('act', lambda e, sg=sg, Gp=Gp: e.activation(out=sg[:], in_=Gp[:, 0:MTOK], func=AF.Silu),
                          r=[('PS', 2 * gb)], w=[('SG', gb)])
                    P.add('dve', lambda e, sg=sg, Up=Up, at=at, ci=ci: e.tensor_tensor(out=at[:, ci, :], in0=Up[:, 0:MTOK], in1=sg[:], op=ALU.mult),
                          r=[('PS', 2 * gb + 1), ('SG', gb)], w=[('ACTT', ab, ci)])
                for j in range(TPM):
                    t = m * TPM + j
                    for h in range(NH):
                        ob = cnt['o'] % 2; cnt['o'] += 1
                        Op = PS[4 + ob]
                        for ci in range(n):
                            P.add('pe', lambda e, Op=Op, at=at, ci=ci, j=j, h=h, n=n, Wd=Wd: e.matmul(
                                Op[:, 0:HW], at[:, ci, j * 128:(j + 1) * 128], Wd[:, ci, h * HW:(h + 1) * HW],
                                start=(ci == 0), stop=(ci == n - 1)),
                                r=[('ACTT', ab, ci), ('slab', bi, 2)], w=[('PS', 4 + ob)])
                        sl = xs(t)
                        P.add('dve', lambda e, Op=Op, sl=sl, h=h: e.scalar_tensor_tensor(
                            out=X[:, sl, h * HW:(h + 1) * HW], in0=Op[:, 0:HW], scalar=0.5, in1=X[:, sl, h * HW:(h + 1) * HW],
                            op0=ALU.mult, op1=ALU.add),
                            r=[('PS', 4 + ob), ('X', sl)], w=[('X', sl)])

    def final_out(T):
        for tiles in (list(range(0, TT // 2)), list(range(TT // 2, TT))):
            rstd_batch(tiles)
            rk = ('RS', tiles[0])
            for t in tiles:
                y = Y[0]
                sl = xs(t)
                r0 = T * ST + t * 128
                P.add('dve', lambda e, t=t, y=y, sl=sl: e.scalar_tensor_tensor(out=y[:], in0=X[:, sl, :], scalar=RS[:, t:t + 1], in1=FG[:],
                                                                        op0=ALU.mult, op1=ALU.mult),
                      r=[('X', sl), rk, 'FG'], w=[('Y', 0)])
                P.add('sp', lambda e, y=y, r0=r0: e.dma_start(out=out_d[r0:r0 + 128, :], in_=y[:]),
                      r=[('Y', 0)], dma='o0')

    import math
    cload(IDF[:], identf_d, 'IDF'); cload(SEL[:], sel_d, 'SEL'); cload(MASKNEG[:], maskneg_d, 'MASKNEG')
    cload(MCUR[:], mcur_d, 'MCUR'); cload(MPREV[:], mprev_d, 'MPREV')
    cload(OG[:], og_d.partition_broadcast(128), 'OG')
    cload(ESINK[:], sink_d.partition_broadcast(128), 'ESINK')
    cload(BIS[:], bi_d, 'BIS'); cload(BFS[:], bf_d, 'BFS')
    POSI = sb('POSI', [128, NTILES], I32); POSF = sb('POSF', [128, NTILES]); INVF = sb('INVF', [128, 8])
    cload(POSI[:], pos_d, 'POSI'); cload(INVF[:], invf_d.partition_broadcast(128), 'INVF')
    P.add('dve', lambda e: e.tensor_scalar(BIS[:], BIS[:], 1.0 / 15.0, None, op0=ALU.mult), r=['BIS'], w=['BIS'])
    P.add('dve', lambda e: e.tensor_scalar(BFS[:], BFS[:], 1.0 / 15.0, None, op0=ALU.mult), r=['BFS'], w=['BFS'])
    P.add('act', lambda e: e.activation(out=ESINK[:], in_=ESINK[:], func=AF.Exp), r=['ESINK'], w=['ESINK'])
    for i in range(3):
        P.add('dve', lambda e, i=i: e.memset(V1A[i][:], 1.0), w=[('V1A', i)])
        P.add('dve', lambda e, i=i: e.memset(KTA[i][:], 0.0), w=[('KTA', i)])
    NA = NTILES * 8
    areset()
    ANG = av('ANG', 128, [NTILES, 8]); RR_ = av('RRa', 128, [NA]); KI = av('KI', 128, [NA], I32); KF = av('KF', 128, [NA]); MM_ = av('MMa', 128, [NA])
    P.add('dve', lambda e: e.tensor_copy(POSF[:], POSI[:]), r=['POSI'], w=['POSF'])
    P.add('dve', lambda e: e.tensor_tensor(out=ANG[:], in0=POSF[:].unsqueeze(2).to_broadcast([128, NTILES, 8]),
                                           in1=INVF[:].unsqueeze(1).to_broadcast([128, NTILES, 8]), op=ALU.mult), r=['POSF', 'INVF'], w=['ANG'])
    TWO_PI = 2.0 * math.pi
    C1 = 6.28125; C2 = TWO_PI - C1
    def sin_of(dst, shift, tag):
        angf = ANG[:].rearrange("p a b -> p (a b)")
        P.add('dve', lambda e: e.tensor_scalar(RR_[:], angf, shift, None, op0=ALU.add), r=['ANG'], w=['RRa'])
        P.add('dve', lambda e: e.tensor_scalar(KF[:], RR_[:], 1.0 / TWO_PI, None, op0=ALU.mult), r=['RRa'], w=['KF'])
        P.add('dve', lambda e: e.tensor_copy(KI[:], KF[:]), r=['KF'], w=['KI'])
        P.add('dve', lambda e: e.tensor_copy(KF[:], KI[:]), r=['KI'], w=['KF'])
        P.add('dve', lambda e: e.scalar_tensor_tensor(out=RR_[:], in0=KF[:], scalar=-C1, in1=RR_[:], op0=ALU.mult, op1=ALU.add), r=['KF', 'RRa'], w=['RRa'])
        P.add('dve', lambda e: e.scalar_tensor_tensor(out=RR_[:], in0=KF[:], scalar=-C2, in1=RR_[:], op0=ALU.mult, op1=ALU.add), r=['KF', 'RRa'], w=['RRa'])
        P.add('dve', lambda e: e.tensor_scalar(MM_[:], RR_[:], math.pi, -TWO_PI, op0=ALU.is_gt, op1=ALU.mult), r=['RRa'], w=['MMa'])
        P.add('dve', lambda e: e.tensor_tensor(out=RR_[:], in0=RR_[:], in1=MM_[:], op=ALU.add), r=['RRa', 'MMa'], w=['RRa'])
        P.add('dve', lambda e: e.tensor_scalar(MM_[:], RR_[:], -math.pi, TWO_PI, op0=ALU.is_lt, op1=ALU.mult), r=['RRa'], w=['MMa'])
        P.add('dve', lambda e: e.tensor_tensor(out=RR_[:], in0=RR_[:], in1=MM_[:], op=ALU.add), r=['RRa', 'MMa'], w=['RRa'])
        P.add('dve', lambda e: e.tensor_scalar(RR_[:], RR_[:], math.pi, -math.pi, op0=ALU.min, op1=ALU.max), r=['RRa'], w=['RRa'])
        P.add('act', lambda e: e.activation(out=dst[:].rearrange("p a b -> p (a b)"), in_=RR_[:], func=AF.Sin), r=['RRa'], w=[tag])
    sin_of(SIN, 0.0, 'SIN')
    sin_of(COS, math.pi / 2.0, 'COS')

    W_IN = W['w_in']
    pcnt = dict(qk=0, g=0, o=0)

    def mixer(T):
        seq_first = (T % STPS == 0)
        norm_to_HT('m')
        abarrier()
        QT = av('QT', 64, [4, MTOK], BF16); KT = av('KT', 64, [4, MTOK], BF16)
        TI = av('TI', 4, [MTOK]); TF = av('TF', 4, [MTOK]); NEGB = av('NEGB', 4, [MTOK]); GG = av('GG', 4, [MTOK])
        MMs = av('MMs', 4, [MTOK]); NM_ = av('NMs', 4, [MTOK]); ONES4 = av('ONES4', 4, [MTOK]); ZEROS4 = av('ZEROS4', 4, [MTOK])
        GCOL = av('GCOL', 128, [4]); ENMs = [av('ENM', 128, [4]) for _ in range(2)]; NRs = [av('NR', 128, [4, 129]) for _ in range(2)]; RNEW = av('RNEW', 128, [4]); WCOL = av('WCOL', 128, [4])
        DEC = av('DEC', 64, [4]); DEN = av('DEN', 128, [4]); NEGD = av('NEGD', 128, [4]); RRm = av('RRm', 128, [4]); SS2 = av('SS2', 128, [4]); RS2 = av('RS2', 128, [4])
        ARG = av('ARG', 128, [4, 128]); DTm = av('DTm', 128, [4, 128]); PD = av('PD', 128, [4, 128], BF16)
        ARG2 = av('ARG2', 64, [4, 128]); EE = av('EE', 64, [4, 128]); QP = av('QP', 64, [4, 128], BF16)
        KP = av('KP', 128, [4, 64], BF16); V1 = av('V1', 128, [4, 129], BF16)
        SIGOs = [av('SIGO', 128, [512]) for _ in range(2)]; HMN = av('HMN', 128, [4, 128]); T1 = av('T1', 128, [4, 128]); HMB = av('HMB', 128, [512], BF16)
        JK2 = av('JK2', 128, [128])
        P.add('dve', lambda e: e.memset(ONES4, 1.0), w=['ONES4'])
        P.add('dve', lambda e: e.memset(ZEROS4, 0.0), w=['ZEROS4'])
        P.add('dve', lambda e: e.memset(V1, 1.0), w=['V1'])
        if seq_first:
            P.add('dve', lambda e: e.memset(SST[:], 0.0), w=['SST'])
            P.add('dve', lambda e: e.memset(SBF[:], 0.0), w=['SBF'])
            P.add('dve', lambda e: e.memset(RCOL[:], 0.0), w=['RCOL'])
            P.add('dve', lambda e: e.memset(BC[:], 0.0), w=['BC'])
            P.add('dve', lambda e: e.memset(MC[:], 0.0), w=['MC'])
        pending_back = [None]
        NW1 = 1544
        parts = [(lambda b: b[:, 0:KC * NW1].rearrange("p (k c) -> p k c", k=KC), W_IN[:, 0:NW1].rearrange("(k p) c -> p k c", p=128))]
        buf, bi = load_slab(parts)
        W1 = buf[:, 0:KC * NW1].rearrange("p (k c) -> p k c", k=KC)
        WK = [('slab', bi, 0)]
        for m in range(NM):
            mc0 = m * MTOK
            htk = [('HT', m * TPM + j) for j in range(TPM)]
            for h in range(4):
                for which in range(2):
                    coff = which * 256 + h * 64
                    b = pcnt['qk'] % 4; pcnt['qk'] += 1
                    for k in range(KC):
                        P.add('pe', lambda e, b=b, k=k, coff=coff, mc0=mc0: e.matmul(PS[b][0:64, 0:MTOK], W1[:, k, coff:coff + 64], HT[:, k, mc0:mc0 + MTOK],
                                                                             start=(k == 0), stop=(k == KC - 1)), r=WK + htk, w=[('PS', b)])
                    if which == 0:
                        P.add('act', lambda e, b=b, h=h: e.activation(out=QT[:, h, :], in_=PS[b][0:64, 0:MTOK], func=AF.Copy, scale=0.125),
                              r=[('PS', b)], w=['QT'])
                    else:
                        P.add('dve', lambda e, b=b, h=h: e.tensor_copy(KT[:, h, :], PS[b][0:64, 0:MTOK]), r=[('PS', b)], w=['KT'])
            for (b, coff) in ((4, 1536), (5, 1540)):
                for k in range(KC):
                    P.add('pe', lambda e, b=b, k=k, coff=coff, mc0=mc0: e.matmul(PS[b][0:4, 0:MTOK], W1[:, k, coff:coff + 4], HT[:, k, mc0:mc0 + MTOK],
                                                                         start=(k == 0), stop=(k == KC - 1)), r=WK + htk, w=[('PS', b)])
            P.add('act', lambda e: e.activation(out=TI, in_=PS[4][0:4, 0:MTOK], func=AF.Tanh, scale=1.0 / 15.0, bias=BIS[:]), r=[('PS', 4), 'BIS'], w=['TI'])
            P.add('act', lambda e: e.activation(out=TF, in_=PS[5][0:4, 0:MTOK], func=AF.Tanh, scale=1.0 / 15.0, bias=BFS[:]), r=[('PS', 5), 'BFS'], w=['TF'])
            P.add('act', lambda e: e.activation(out=TF, in_=TF, func=AF.Exp, scale=-15.0), r=['TF'], w=['TF'])
            P.add('act', lambda e: e.activation(out=TF, in_=TF, func=AF.Ln, bias=1.0), r=['TF'], w=['TF'])
            P.add('dve', lambda e: e.tensor_tensor_scan(NEGB, ONES4, TF, BC[:], ALU.mult, ALU.add), r=['ONES4', 'TF', 'BC'], w=['NEGB'])
            P.add('dve', lambda e: e.tensor_copy(BC[:], NEGB[:, MTOK - 1:MTOK]), r=['NEGB'], w=['BC'])
            P.add('dve', lambda e: e.scalar_tensor_tensor(out=GG, in0=TI, scalar=15.0, in1=NEGB, op0=ALU.mult, op1=ALU.add), r=['TI', 'NEGB'], w=['GG'])
            P.add('dve', lambda e: e.tensor_tensor_scan(MMs, GG, ZEROS4, MC[:], ALU.max, ALU.max), r=['GG', 'ZEROS4', 'MC'], w=['MMs'])
            P.add('dve', lambda e: e.tensor_copy(MC[:], MMs[:, MTOK - 1:MTOK]), r=['MMs'], w=['MC'])
            P.add('dve', lambda e: e.tensor_tensor(out=NM_, in0=NEGB, in1=MMs, op=ALU.subtract), r=['NEGB', 'MMs'], w=['NMs'])
            for j in range(TPM):
                t = m * TPM + j
                cj = slice(j * 128, (j + 1) * 128)
                tcols = slice(t * 128, (t + 1) * 128)
                par = t % 2
                a_start = len(P.ops)
                ENM = ENMs[par]; SIGO = SIGOs[par]; NR = NRs[par]
                ek = ('ENM', par); sk = ('SIGO', par); nk = ('NR', par)
                P.add('pe', lambda e, cj=cj: e.transpose(PS[4][:, 0:4], GG[:, cj], IDF[0:4, 0:4]), r=['GG', 'IDF'], w=[('PS', 4)])
                P.add('pe', lambda e, cj=cj: e.transpose(PS[4][:, 4:8], NM_[:, cj], IDF[0:4, 0:4]), r=['NMs', 'IDF'], w=[('PS', 4)])
                P.add('dve', lambda e: e.tensor_copy(GCOL, PS[4][:, 0:4]), r=[('PS', 4)], w=['GCOL'])
                P.add('act', lambda e, ENM=ENM: e.activation(out=ENM, in_=PS[4][:, 4:8], func=AF.Exp), r=[('PS', 4)], w=[ek])
                for h in range(4):
                    P.add('pe', lambda e, h=h, cj=cj: e.matmul(PS[3][:, h * 128:(h + 1) * 128], SEL[0:4, h * 128:(h + 1) * 128], MMs[:, cj], start=True, stop=True),
                          r=['SEL', 'MMs'], w=[('PS', 3)])
                PS3v = PS[3][:, 0:512].rearrange("p (a b) -> p a b", a=4)
                P.add('dve', lambda e: e.tensor_copy(RNEW, PS3v[:, :, 127]), r=[('PS', 3)], w=['RNEW'])
                P.add('dve', lambda e: e.scalar_tensor_tensor(out=ARG, in0=PS3v, scalar=-1.0, in1=MASKNEG[:].unsqueeze(1).to_broadcast([128, 4, 128]),
                                                              op0=ALU.mult, op1=ALU.add), r=[('PS', 3), 'MASKNEG'], w=['ARG'])
                P.add('dve', lambda e: e.tensor_tensor(out=ARG, in0=ARG, in1=GCOL.unsqueeze(2).to_broadcast([128, 4, 128]), op=ALU.add), r=['ARG', 'GCOL'], w=['ARG'])
                P.add('act', lambda e: e.activation(out=DTm, in_=ARG, func=AF.Exp), r=['ARG'], w=['DTm'])
                P.add('dve', lambda e: e.scalar_tensor_tensor(out=ARG2, in0=PS3v[0:64], scalar=-1.0, in1=RCOL[0:64, :].unsqueeze(2).to_broadcast([64, 4, 128]),
                                                              op0=ALU.mult, op1=ALU.add), r=[('PS', 3), 'RCOL'], w=['ARG2'])
                P.add('act', lambda e: e.activation(out=EE, in_=ARG2, func=AF.Exp), r=['ARG2'], w=['EE'])
                P.add('dve', lambda e, cj=cj: e.tensor_tensor(out=QP, in0=QT[:, :, cj], in1=EE, op=ALU.mult), r=['QT', 'EE'], w=['QP'])
                for (b, c0, wd_) in ((0, 256, 256), (1, 512, 512), (2, 1024, 512)):
                    for k in range(KC):
                        P.add('pe', lambda e, b=b, c0=c0, wd_=wd_, k=k, tcols=tcols: e.matmul(PS[b][:, 0:wd_], HT[:, k, tcols], W1[:, k, c0:c0 + wd_],
                                                                                          start=(k == 0), stop=(k == KC - 1)), r=WK + [('HT', t)], w=[('PS', b)])
                P.add('dve', lambda e: e.tensor_tensor(out=WCOL, in0=GCOL, in1=RNEW, op=ALU.subtract), r=['GCOL', 'RNEW'], w=['WCOL'])
                P.add('act', lambda e: e.activation(out=WCOL, in_=WCOL, func=AF.Exp), r=['WCOL'], w=['WCOL'])
                P.add('dve', lambda e: e.tensor_tensor(out=KP, in0=PS[0][:, 0:256].rearrange("p (a b) -> p a b", a=4),
                                                       in1=WCOL.unsqueeze(2).to_broadcast([128, 4, 64]), op=ALU.mult), r=[('PS', 0), 'WCOL'], w=['KP'])
                P.add('act', lambda e: e.activation(out=V1[:, :, 0:128], in_=PS[1][:, 0:512].rearrange("p (a b) -> p a b", a=4), func=AF.Copy),
                      r=[('PS', 1)], w=['V1'])
                P.add('act', lambda e, SIGO=SIGO: e.activation(out=SIGO, in_=PS[2][:, 0:512], func=AF.Sigmoid), r=[('PS', 2)], w=[sk])
                for h in range(4):
                    P.add('pe', lambda e, h=h, cj=cj: e.matmul(PS[5][:, h * 128:(h + 1) * 128], KT[:, h, cj], QT[:, h, cj], start=True, stop=True),
                          r=['KT', 'QT'], w=[('PS', 5)])
                P.add('dve', lambda e: e.tensor_tensor(out=PD, in0=PS[5][:, 0:512].rearrange("p (a b) -> p a b", a=4), in1=DTm, op=ALU.mult),
                      r=[('PS', 5), 'DTm'], w=['PD'])
                for h in range(4):
                    b = h // 2; o0 = (h % 2) * 129
                    P.add('pe', lambda e, h=h, b=b, o0=o0: e.matmul(PS[b][:, o0:o0 + 129], PD[:, h, :], V1[:, h, :], start=True, stop=False),
                          r=['PD', 'V1'], w=[('PS', b)])
                    P.add('pe', lambda e, h=h, b=b, o0=o0: e.matmul(PS[b][:, o0:o0 + 129], QP[:, h, :], SBF[:, h, :], start=False, stop=True),
                          r=['QP', 'SBF'], w=[('PS', b)])
                for h in range(4):
                    b = (2, 4)[h // 2]; o0 = (h % 2) * 129
                    P.add('pe', lambda e, h=h, b=b, o0=o0: e.matmul(PS[b][0:64, o0:o0 + 129], KP[:, h, :], V1[:, h, :], start=True, stop=True),
                          r=['KP', 'V1'], w=[('PS', b)])
                P.add('dve', lambda e: e.tensor_tensor(out=DEC, in0=RCOL[0:64, :], in1=RNEW[0:64, :], op=ALU.subtract), r=['RCOL', 'RNEW'], w=['DEC'])
                P.add('act', lambda e: e.activation(out=DEC, in_=DEC, func=AF.Exp), r=['DEC'], w=['DEC'])
                for h in range(4):
                    b = (2, 4)[h // 2]; o0 = (h % 2) * 129
                    P.add('dve', lambda e, h=h, b=b, o0=o0: e.scalar_tensor_tensor(out=SST[:, h, :], in0=SST[:, h, :], scalar=DEC[:, h:h + 1], in1=PS[b][0:64, o0:o0 + 129],
                                                                                   op0=ALU.mult, op1=ALU.add), r=['SST', 'DEC', ('PS', b), 'SBF'], w=['SST'])
                P.add('act', lambda e: e.activation(out=SBF[:], in_=SST[:], func=AF.Copy), r=['SST'], w=['SBF'])
                P.add('dve', lambda e: e.tensor_copy(RCOL[:], RNEW), r=['RNEW'], w=['RCOL'])
                for b in range(2):
                    pv = PS[b][:, 0:258].rearrange("p (a b) -> p a b", a=2)
                    P.add('dve', lambda e, b=b, pv=pv, NR=NR: e.tensor_copy(NR[:, 2 * b:2 * b + 2, :], pv), r=[('PS', b)], w=[nk])

                def back(t=t, tcols=tcols, ENM=ENM, SIGO=SIGO, NR=NR, ek=ek, sk=sk, nk=nk):
                    P.add('dve', lambda e: e.tensor_copy(DEN, NR[:, :, 128]), r=[nk], w=['DEN'])
                    P.add('dve', lambda e: e.tensor_scalar(NEGD, DEN, -1.0, None, op0=ALU.mult), r=['DEN'], w=['NEGD'])
                    P.add('dve', lambda e: e.tensor_tensor(out=DEN, in0=DEN, in1=NEGD, op=ALU.max), r=['DEN', 'NEGD'], w=['DEN'])
                    P.add('dve', lambda e: e.tensor_tensor(out=DEN, in0=DEN, in1=ENM, op=ALU.max), r=['DEN', ek], w=['DEN'])
                    P.add('dve', lambda e: e.reciprocal(RRm, DEN), r=['DEN'], w=['RRm'])
                    P.add('dve', lambda e: e.tensor_tensor(out=HMN, in0=NR[:, :, 0:128], in1=RRm.unsqueeze(2).to_broadcast([128, 4, 128]), op=ALU.mult),
                          r=[nk, 'RRm'], w=['HMN'])
                    for h in range(4):
                        P.add('act', lambda e, h=h: e.activation(out=JK2, in_=HMN[:, h, :], func=AF.Square, accum_out=SS2[:, h:h + 1]), r=['HMN'], w=['JK2', 'SS2'])
                    P.add('dve', lambda e: e.tensor_scalar(RS2, SS2, 1.0 / 128.0, EPS, op0=ALU.mult, op1=ALU.add), r=['SS2'], w=['RS2'])
                    P.add('act', lambda e: e.activation(out=RS2, in_=RS2, func=AF.Ln), r=['RS2'], w=['RS2'])
                    P.add('act', lambda e: e.activation(out=RS2, in_=RS2, func=AF.Exp, scale=-0.5), r=['RS2'], w=['RS2'])
                    P.add('dve', lambda e: e.tensor_tensor(out=T1, in0=HMN, in1=RS2.unsqueeze(2).to_broadcast([128, 4, 128]), op=ALU.mult), r=['HMN', 'RS2'], w=['T1'])
                    P.add('dve', lambda e: e.tensor_tensor(out=SIGO, in0=SIGO, in1=OG[:], op=ALU.mult), r=[sk, 'OG'], w=[sk])
                    P.add('dve', lambda e: e.tensor_tensor(out=HMB, in0=T1.rearrange("p a b -> p (a b)"), in1=SIGO, op=ALU.mult), r=['T1', sk], w=['HMB'])
                    for kc in range(4):
                        P.add('pe', lambda e, kc=kc: e.transpose(PSB[1][:, kc * 128:(kc + 1) * 128], HMB[:, kc * 128:(kc + 1) * 128], IDB[:]), r=['HMB', 'IDB'], w=[('PSB', 1)])
                    P.add('dve', lambda e: e.tensor_copy(HM[:, :, tcols], PSB[1][:, 0:512].rearrange("p (a b) -> p a b", a=4)), r=[('PSB', 1)], w=[('HM', t)])
                if pending_back[0] is not None:
                    A = P.ops[a_start:]; del P.ops[a_start:]
                    b_start = len(P.ops)
                    pending_back[0]()
                    B = P.ops[b_start:]; del P.ops[b_start:]
                    lead = 12
                    step = max(1, (len(A) - lead) // (len(B) + 1))
                    merged = []; bi_ = 0
                    for ai, op in enumerate(A):
                        merged.append(op)
                        if ai >= lead and (ai - lead) % step == step - 1 and bi_ < len(B):
                            merged.append(B[bi_]); bi_ += 1
                    merged.extend(B[bi_:])
                    P.ops.extend(merged)
                pending_back[0] = back
        if pending_back[0] is not None:
            pending_back[0]()
            pending_back[0] = None

        abarrier()
        QKVS = av('QKVS', 128, [768]); QKB = av('QKB', 128, [10, 64], BF16)
        TA = av('TA', 128, [10, 8]); TB = av('TB', 128, [10, 8])
        QTAs = [av('QTA', 64, [8, 128], BF16) for _ in range(2)]
        EB = [[av('EB', 128, [4, 128], BF16) for _ in range(2)] for _ in range(2)]
        PT = [[av('PT', 128, [4, 128], BF16) for _ in range(2)] for _ in range(2)]
        OAB = av('OAB', 128, [8, 64], BF16); DENA = av('DENA', 128, [8]); RRA = av('RRA', 128, [8])
        NW2 = 768
        parts = [(lambda b: b[:, 0:KC * NW2].rearrange("p (k c) -> p k c", k=KC), W_IN[:, 1544:1544 + NW2].rearrange("(k p) c -> p k c", p=128))]
        buf, bi = load_slab(parts)
        W2 = buf[:, 0:KC * NW2].rearrange("p (k c) -> p k c", k=KC)
        WK = [('slab', bi, 0)]
        XR = QKVS[:, 0:640].rearrange("p (a b) -> p a b", a=10)

        def p2_stage1(t):
            gt = T * TT + t
            par = gt % 3
            qp = t % 2
            QTA = QTAs[qp]
            tcols = slice(t * 128, (t + 1) * 128)
            for (b, c0, wd_) in ((0, 0, 512), (1, 512, 256)):
                for k in range(KC):
                    P.add('pe', lambda e, b=b, c0=c0, wd_=wd_, k=k, tcols=tcols: e.matmul(PS[b][:, 0:wd_], HT[:, k, tcols], W2[:, k, c0:c0 + wd_],
                                                                                      start=(k == 0), stop=(k == KC - 1)), r=WK + [('HT', t)], w=[('PS', b)])
            P.add('act', lambda e: e.activation(out=QKVS[:, 0:512], in_=PS[0][:, 0:512], func=AF.Copy, scale=0.125), r=[('PS', 0)], w=['QKVS'])
            P.add('act', lambda e: e.activation(out=QKVS[:, 512:768], in_=PS[1][:, 0:256], func=AF.Copy), r=[('PS', 1)], w=['QKVS'])
            cosb = COS[:, gt, :].unsqueeze(1).to_broadcast([128, 10, 8]); sinb = SIN[:, gt, :].unsqueeze(1).to_broadcast([128, 10, 8])
            x1 = XR[:, :, 0:8]; x2 = XR[:, :, 8:16]
            P.add('dve', lambda e: e.tensor_tensor(out=TA, in0=x1, in1=cosb, op=ALU.mult), r=['QKVS', 'COS'], w=['TA'])
            P.add('dve', lambda e: e.tensor_tensor(out=TB, in0=x2, in1=sinb, op=ALU.mult), r=['QKVS', 'SIN'], w=['TB'])
            P.add('dve', lambda e: e.tensor_tensor(out=QKB[:, :, 0:8], in0=TA, in1=TB, op=ALU.subtract), r=['TA', 'TB'], w=['QKB'])
            P.add('dve', lambda e: e.tensor_tensor(out=TA, in0=x2, in1=cosb, op=ALU.mult), r=['QKVS', 'COS', 'QKB'], w=['TA'])
            P.add('dve', lambda e: e.tensor_tensor(out=TB, in0=x1, in1=sinb, op=ALU.mult), r=['QKVS', 'SIN', 'QKB'], w=['TB'])
            P.add('dve', lambda e: e.tensor_tensor(out=QKB[:, :, 8:16], in0=TA, in1=TB, op=ALU.add), r=['TA', 'TB'], w=['QKB'])
            P.add('dve', lambda e: e.tensor_copy(QKB[:, :, 16:64], XR[:, :, 16:64]), r=['QKVS'], w=['QKB'])
            P.add('act', lambda e: e.activation(out=V1A[par][:, :, 0:64], in_=QKVS[:, 640:768].rearrange("p (a b) -> p a b", a=2), func=AF.Copy),
                  r=['QKVS'], w=[('V1A', par)])
            for h in range(8):
                P.add('pe', lambda e, h=h: e.transpose(PSB[0][0:64, h * 128:(h + 1) * 128], QKB[:, h, :], IDB[:]), r=['QKB', 'IDB'], w=[('PSB', 0)])
            for kv in range(2):
                P.add('pe', lambda e, kv=kv: e.transpose(PSB[1][0:64, kv * 128:(kv + 1) * 128], QKB[:, 8 + kv, :], IDB[:]), r=['QKB', 'IDB'], w=[('PSB', 1)])
            P.add('dve', lambda e: e.tensor_copy(QTA, PSB[0][0:64, 0:1024].rearrange("p (a b) -> p a b", a=8)), r=[('PSB', 0)], w=[('QTA', qp)])
            P.add('dve', lambda e: e.tensor_copy(KTA[par][:], PSB[1][0:64, 0:256].rearrange("p (a b) -> p a b", a=2)), r=[('PSB', 1)], w=[('KTA', par)])

        def p2_stage2(t):
            gt = T * TT + t
            lt = (T % STPS) * TT + t
            par = gt % 3
            ppar = (gt - 1) % 3
            qp = t % 2
            QTA = QTAs[qp]
            tcols = slice(t * 128, (t + 1) * 128)
            has_prev = lt > 0
            for kv in range(2):
                for pc in range(2):
                    if pc == 1 and not has_prev:
                        continue
                    b = 2 + 2 * kv + pc
                    src = par if pc == 0 else ppar
                    P.add('pe', lambda e, b=b, kv=kv, src=src: e.matmul(PS[b][:, 0:512], KTA[src][:, kv, :], QTA[:, 4 * kv:4 * kv + 4, :], start=True, stop=True),
                          r=[('KTA', src), ('QTA', qp)], w=[('PS', b)])
                    eb = EB[kv][pc]; pt = PT[kv][pc]
                    mk = MCUR if pc == 0 else MPREV
                    P.add('act', lambda e, b=b, eb=eb: e.activation(out=eb, in_=PS[b][:, 0:512].rearrange("p (a b) -> p a b", a=4), func=AF.Exp), r=[('PS', b)], w=[('EB', kv, pc)])
                    P.add('dve', lambda e, eb=eb, pt=pt, mk=mk: e.tensor_tensor(out=pt, in0=eb, in1=mk[:].unsqueeze(1).to_broadcast([128, 4, 128]), op=ALU.mult),
                          r=[('EB', kv, pc), 'MCUR', 'MPREV'], w=[('PT', kv, pc)])
            for h in range(8):
                kv = h // 4; hh = h % 4; b = 2 + 2 * kv; o0 = hh * 65
                if has_prev:
                    P.add('pe', lambda e, b=b, o0=o0, kv=kv, hh=hh: e.matmul(PS[b][:, o0:o0 + 65], PT[kv][1][:, hh, :], V1A[ppar][:, kv, :], start=True, stop=False),
                          r=[('PT', kv, 1), ('V1A', ppar)], w=[('PS', b)])
                P.add('pe', lambda e, b=b, o0=o0, kv=kv, hh=hh: e.matmul(PS[b][:, o0:o0 + 65], PT[kv][0][:, hh, :], V1A[par][:, kv, :],
                                                                     start=(not has_prev), stop=True),
                      r=[('PT', kv, 0), ('V1A', par)], w=[('PS', b)])
            for kv in range(2):
                b = 2 + 2 * kv
                pv = PS[b][:, 0:260].rearrange("p (a b) -> p a b", a=4)
                P.add('dve', lambda e, kv=kv, pv=pv: e.tensor_tensor(out=DENA[:, 4 * kv:4 * kv + 4], in0=pv[:, :, 64], in1=ESINK[:, 4 * kv:4 * kv + 4], op=ALU.add),
                      r=[('PS', b), 'ESINK'], w=['DENA'])
            P.add('dve', lambda e: e.reciprocal(RRA, DENA), r=['DENA'], w=['RRA'])
            for kv in range(2):
                b = 2 + 2 * kv
                pv = PS[b][:, 0:260].rearrange("p (a b) -> p a b", a=4)
                P.add('dve', lambda e, kv=kv, pv=pv: e.tensor_tensor(out=OAB[:, 4 * kv:4 * kv + 4, :], in0=pv[:, :, 0:64],
                                                                     in1=RRA[:, 4 * kv:4 * kv + 4].unsqueeze(2).to_broadcast([128, 4, 64]), op=ALU.mult),
                      r=[('PS', b), 'RRA'], w=['OAB'])
            OABf = OAB.rearrange("p a b -> p (a b)")
            for kc in range(4):
                P.add('pe', lambda e, kc=kc: e.transpose(PSB[1][:, 512 + kc * 128:512 + (kc + 1) * 128], OABf[:, kc * 128:(kc + 1) * 128], IDB[:]), r=['OAB', 'IDB'], w=[('PSB', 1)])
            P.add('dve', lambda e: e.tensor_copy(OA[:, :, tcols], PSB[1][:, 512:1024].rearrange("p (a b) -> p a b", a=4)), r=[('PSB', 1)], w=[('OA', t)])

        def capture(fn, *a):
            s0 = len(P.ops)
            fn(*a)
            ops_ = P.ops[s0:]; del P.ops[s0:]
            return ops_

        def interleave(A, B):
            out = []
            na, nb = len(A), len(B)
            ia = ib = 0
            while ia < na or ib < nb:
                if ib >= nb or (ia < na and ia * nb <= ib * na):
                    out.append(A[ia]); ia += 1
                else:
                    out.append(B[ib]); ib += 1
            return out

        p2_stage1(0)
        for t in range(1, TT):
            A = capture(p2_stage1, t)
            B = capture(p2_stage2, t - 1)
            P.ops.extend(interleave(A, B))
        p2_stage2(TT - 1)

        abarrier()
        MG = av('MG', 128, [KC, ST], BF16)
        SGG = [av('SGG', 128, [MTOK]) for _ in range(2)]
        TMPG = [av('TMPG', 128, [MTOK]) for _ in range(2)]
        for (br, gc0, wbr, SRC, skey) in (('m', 2312, W['w_branch_mlstm'], HM, 'HM'), ('a', 2312 + DM, W['w_branch_attn'], OA, 'OA')):
            GO, BO = 0, KC * DM
            parts = [(lambda b: b[:, GO:GO + KC * DM].rearrange("p (k c) -> p k c", k=KC), W_IN[:, gc0:gc0 + DM].rearrange("(k p) c -> p k c", p=128)),
                     (lambda b: b[:, BO:BO + 4 * DM].rearrange("p (k c) -> p k c", k=4), wbr.rearrange("(k p) c -> p k c", p=128))]
            buf, bi = load_slab(parts)
            W3 = buf[:, GO:GO + KC * DM].rearrange("p (k c) -> p k c", k=KC)
            WB = buf[:, BO:BO + 4 * DM].rearrange("p (k c) -> p k c", k=4)
            for m in range(NM):
                mc0 = m * MTOK
                htk = [('HT', m * TPM + j) for j in range(TPM)]
                srk = [(skey, m * TPM + j) for j in range(TPM)]
                for d in range(KC):
                    gb = pcnt['g'] % 2; pcnt['g'] += 1
                    Gp, Yp = PS[2 * gb], PS[2 * gb + 1]
                    for k in range(KC):
                        P.add('pe', lambda e, Gp=Gp, k=k, d=d, mc0=mc0, W3=W3: e.matmul(Gp[:, 0:MTOK], W3[:, k, d * 128:(d + 1) * 128], HT[:, k, mc0:mc0 + MTOK],
                                                                                start=(k == 0), stop=(k == KC - 1)), r=[('slab', bi, 0)] + htk, w=[('PS', 2 * gb)])
                    for k in range(4):
                        P.add('pe', lambda e, Yp=Yp, k=k, d=d, mc0=mc0, WB=WB, SRC=SRC: e.matmul(Yp[:, 0:MTOK], WB[:, k, d * 128:(d + 1) * 128], SRC[:, k, mc0:mc0 + MTOK],
                                                                                         start=(k == 0), stop=(k == 3)), r=[('slab', bi, 1)] + srk, w=[('PS', 2 * gb + 1)])
                    sgg = SGG[gb]
                    P.add('act', lambda e, sgg=sgg, Gp=Gp: e.activation(out=sgg, in_=Gp[:, 0:MTOK], func=AF.Sigmoid), r=[('PS', 2 * gb)], w=[('SGG', gb)])
                    if br == 'm':
                        P.add('dve', lambda e, sgg=sgg, Yp=Yp, d=d, mc0=mc0: e.tensor_tensor(out=MG[:, d, mc0:mc0 + MTOK], in0=Yp[:, 0:MTOK], in1=sgg, op=ALU.mult),
                              r=[('PS', 2 * gb + 1), ('SGG', gb)], w=[('MG', d, m)])
                    else:
                        tg = TMPG[gb]
                        P.add('dve', lambda e, sgg=sgg, Yp=Yp, tg=tg: e.tensor_tensor(out=tg, in0=Yp[:, 0:MTOK], in1=sgg, op=ALU.mult),
                              r=[('PS', 2 * gb + 1), ('SGG', gb)], w=[('TMPG', gb)])
                        P.add('dve', lambda e, tg=tg, d=d, mc0=mc0: e.tensor_tensor(out=MG[:, d, mc0:mc0 + MTOK], in0=MG[:, d, mc0:mc0 + MTOK], in1=tg, op=ALU.add),
                              r=[('TMPG', gb), ('MG', d, m)], w=[('MG', d, m)])
        parts = [(lambda b: b[:, 0:KC * DM].rearrange("p (k c) -> p k c", k=KC), W['w_out'].rearrange("(k p) c -> p k c", p=128))]
        buf, bi = load_slab(parts)
        WO = buf[:, 0:KC * DM].rearrange("p (k c) -> p k c", k=KC)
        for t in range(TT):
            m = t // TPM
            tcols = slice(t * 128, (t + 1) * 128)
            for h in range(NH):
                ob = pcnt['o'] % 2; pcnt['o'] += 1
                Op = PS[4 + ob]
                for k in range(KC):
                    P.add('pe', lambda e, Op=Op, k=k, h=h, tcols=tcols: e.matmul(Op[:, 0:HW], MG[:, k, tcols], WO[:, k, h * HW:(h + 1) * HW], start=(k == 0), stop=(k == KC - 1)),
                          r=[('slab', bi, 0)] + [('MG', d, m) for d in range(KC)], w=[('PS', 4 + ob)])
                sl = xs(t)
                P.add('dve', lambda e, Op=Op, sl=sl, h=h: e.tensor_tensor(out=X[:, sl, h * HW:(h + 1) * HW], in0=Op[:, 0:HW], in1=X[:, sl, h * HW:(h + 1) * HW], op=ALU.add),
                      r=[('PS', 4 + ob), ('X', sl)], w=[('X', sl)])


    for T in range(NST):
        xbase[0] = (T * TT) % XSL
        if T + 1 < NST:
            load_x(T + 1, range(0, TT // 2))
        if 'ffn1' in phases:
            ffn(1)
        if 'mix' in phases:
            mixer(T)
        if 'ffn2' in phases:
            ffn(2)
        final_out(T)
        if T + 1 < NST:
            load_x(T + 1, range(TT // 2, TT))
    P.add('sp', None, r=[('dmasem', 'o0')])
    P.build(st)
    st.close()
    return nc, P


def make_consts():
    s = np.arange(128)[:, None]; j = np.arange(128)[None, :]
    c = {}
    c['ident_bf'] = np.eye(128).astype(ml_dtypes.bfloat16)
    c['ident_f'] = np.eye(128).astype(np.float32)
    sel = np.zeros((4, 4, 128), np.float32)
    for h in range(4):
        sel[h, h, :] = 1.0
    c['sel'] = sel.reshape(4, 512)
    c['maskneg'] = np.where(s <= j, 0.0, -1e30).astype(np.float32)
    c['mask_cur'] = (s <= j).astype(np.float32).astype(ml_dtypes.bfloat16)
    c['mask_prev'] = (s > j).astype(np.float32).astype(ml_dtypes.bfloat16)
    c['inv_freq'] = (500000.0 ** (-np.arange(8, dtype=np.float32) * 2.0 / 16)).astype(np.float32)
    return c


def make_in_maps(cfg, inputs):
    DM, S, NSEQ, NC_ = cfg['DM'], cfg['S'], cfg['NSEQ'], cfg['NCORES']
    KC = DM // 128
    consts = make_consts()
    shared = dict(consts)
    f32 = lambda a: np.ascontiguousarray(np.asarray(a, dtype=np.float32))
    for n in WNAMES:
        shared[n] = f32(inputs[n][0])
    for n, k in (('ffn1_norm_g', 'ffn1_norm_gT'), ('mix_norm_g', 'mix_norm_gT'), ('ffn2_norm_g', 'ffn2_norm_gT')):
        shared[k] = f32(np.asarray(inputs[n][0]).reshape(KC, 128).T)
    shared['final_norm_g'] = f32(inputs['final_norm_g'])
    shared['mlstm_b_i'] = f32(np.asarray(inputs['mlstm_b_i'][0]).reshape(4, 1))
    shared['mlstm_b_f'] = f32(np.asarray(inputs['mlstm_b_f'][0]).reshape(4, 1))
    shared['mlstm_out_norm_g'] = f32(inputs['mlstm_out_norm_g'][0])
    shared['attn_sinks'] = f32(inputs['attn_sinks'][0])
    x = np.asarray(inputs['x'], dtype=np.float32)
    pos = np.asarray(inputs['positions'], dtype=np.int32)
    maps = []
    for c in range(NC_):
        m = dict(shared)
        m['x'] = np.ascontiguousarray(x[c * NSEQ:(c + 1) * NSEQ].reshape(NSEQ * S, DM))
        m['posT'] = np.ascontiguousarray(pos[c * NSEQ:(c + 1) * NSEQ].reshape(-1, 128).T)
        maps.append(m)
    return maps


def run(cfg, inputs, phases=('ffn1', 'mix', 'ffn2'), sim=False, trace=False):
    nc, P = build_program(cfg, phases)
    maps = make_in_maps(cfg, inputs)
    res = run_bass_kernel_spmd(nc, maps, core_ids=list(range(cfg['NCORES']))).results
    out = np.concatenate([r['out'] for r in res], axis=0)
    B = cfg['NSEQ'] * cfg['NCORES']
    return out.reshape(B, cfg['S'], cfg['DM'])


def kernel(**inputs):
    cfg = FULL
    out = run(cfg, inputs)
    return np.ascontiguousarray(out.astype(np.float32))
```

```python
import numpy as np
import concourse.bass as bass
import concourse.mybir as mybir
from contextlib import ExitStack

F32 = mybir.dt.float32; BF16 = mybir.dt.bfloat16; I32 = mybir.dt.int32
AF = mybir.ActivationFunctionType; ALU = mybir.AluOpType; AX = mybir.AxisListType

import os
STRICT = os.environ.get('PROG_STRICT', '0') == '1'

class Prog:
    ENGS = ('pe', 'act', 'dve', 'pool', 'sp')
    def __init__(self, nc):
        self.nc = nc
        self.ops = []
    def add(self, eng, fn, r=(), w=(), dma=None):
        self.ops.append([eng, fn, tuple(r), tuple(w), dma])
    def barrier(self, tag):
        for e in self.ENGS:
            self.add(e, lambda eng: eng.nop(), r=(), w=[('bar', tag, e)])
        for e in self.ENGS:
            self.add(e, lambda eng: eng.nop(), r=[('bar', tag, e2) for e2 in self.ENGS if e2 != e], w=())
    def build(self, stack):
        nc = self.nc
        ops = self.ops
        n = len(ops)
        last_w = {}
        readers = {}
        deps = [None] * n
        needs_sig = [False] * n
        for i, (eng, fn, r, w, dma) in enumerate(ops):
            d = {}
            def add_dep(j, kind):
                if j == i:
                    return
                e2, _, _, _, dma2 = ops[j]
                if e2 == eng and dma2 is None and dma is None:
                    if eng == 'pe' or (kind != 'raw' and not STRICT):
                        return
                d[j] = True
            rk = list(r)
            wk = list(w)
            if dma is not None:
                wk.append(('dmasem', dma))
            for k in rk:
                if k in last_w:
                    add_dep(last_w[k], 'raw')
            for k in wk:
                if k in last_w:
                    add_dep(last_w[k], 'waw')
                for j in readers.get(k, ()):
                    add_dep(j, 'war')
            for k in rk:
                readers.setdefault(k, []).append(i)
            for k in wk:
                last_w[k] = i
                readers[k] = []
            deps[i] = list(d.keys())
            for j in deps[i]:
                needs_sig[j] = True
        sig = [None] * n
        cnt = {}
        semnames = set()
        for i, (eng, fn, r, w, dma) in enumerate(ops):
            if not needs_sig[i] and dma is None:
                continue
            name = ('dma_' + dma) if dma is not None else ('eng_' + eng)
            inc = 16 if dma is not None else 1
            cnt[name] = cnt.get(name, 0) + inc
            sig[i] = (name, cnt[name], inc)
            semnames.add(name)
        sems = {}
        for name in sorted(semnames):
            sems[name] = stack.enter_context(nc.semaphore(name))
        self.sem_counts = cnt
        per_eng = {e: [] for e in self.ENGS}
        waited = {e: {} for e in self.ENGS}
        nwaits = 0
        for i, (eng, fn, r, w, dma) in enumerate(ops):
            need = {}
            for j in deps[i]:
                name, val, _ = sig[j]
                if need.get(name, 0) < val:
                    need[name] = val
            ws = []
            for name, val in need.items():
                if waited[eng].get(name, 0) < val:
                    waited[eng][name] = val
                    ws.append((sems[name], val))
                    nwaits += 1
            s = None
            if sig[i] is not None:
                s = (sems[sig[i][0]], sig[i][2])
            per_eng[eng].append((ws, fn, s))
        self.nwaits = nwaits
        block = stack.enter_context(nc.Block())
        def mk(lst):
            def body(eng):
                for ws, fn, s in lst:
                    for sem, val in ws:
                        eng.wait_ge(sem, val)
                    if fn is None:
                        assert s is None
                        continue
                    ins = fn(eng)
                    if s is not None:
                        ins.then_inc(s[0], s[1])
            return body
        block.tensor(mk(per_eng['pe']))
        block.scalar(mk(per_eng['act']))
        block.vector(mk(per_eng['dve']))
        block.gpsimd(mk(per_eng['pool']))
        block.sync(mk(per_eng['sp']))


import numpy as np
import ml_dtypes
import concourse.bass as bass
import concourse.mybir as mybir
from contextlib import ExitStack
from concourse.bass_utils import run_bass_kernel_spmd

F32 = mybir.dt.float32; BF16 = mybir.dt.bfloat16; I32 = mybir.dt.int32
AF = mybir.ActivationFunctionType; ALU = mybir.AluOpType; AX = mybir.AxisListType
EPS = 1e-6

FULL = dict(DM=1024, DFF=2816, S=2048, NSEQ=2, ST=1024, NCORES=8)

WNAMES = ['ffn1_w_gate', 'ffn1_w_up', 'ffn1_w_down', 'w_in', 'w_branch_mlstm', 'w_branch_attn', 'w_out',
          'ffn2_w_gate', 'ffn2_w_up', 'ffn2_w_down']


def build_program(cfg, phases=('ffn1', 'mix', 'ffn2')):
    DM, DFF, S, NSEQ, ST = cfg['DM'], cfg['DFF'], cfg['S'], cfg['NSEQ'], cfg['ST']
    KC = DM // 128
    FC = DFF // 128
    TT = ST // 128
    MTOK = min(512, ST)
    NM = ST // MTOK
    TPM = MTOK // 128
    NTOK = NSEQ * S
    NST = NTOK // ST
    STPS = S // ST
    NTILES = NTOK // 128
    INW = 2312 + 2 * DM
    HW = min(512, DM)
    NH = DM // HW

    nc = bass.Bass("TRN2", target_bir_lowering=False)
    st = ExitStack()
    P = Prog(nc)

    def din(name, shape, dt=F32):
        return nc.dram_tensor(name, list(shape), dt, kind="ExternalInput").ap()
    x_d = din('x', [NTOK, DM])
    pos_d = din('posT', [128, NTILES], I32)
    W = {}
    W['ffn1_w_gate'] = din('ffn1_w_gate', [DM, DFF]); W['ffn1_w_up'] = din('ffn1_w_up', [DM, DFF]); W['ffn1_w_down'] = din('ffn1_w_down', [DFF, DM])
    W['ffn2_w_gate'] = din('ffn2_w_gate', [DM, DFF]); W['ffn2_w_up'] = din('ffn2_w_up', [DM, DFF]); W['ffn2_w_down'] = din('ffn2_w_down', [DFF, DM])
    W['w_in'] = din('w_in', [DM, INW]); W['w_branch_mlstm'] = din('w_branch_mlstm', [512, DM]); W['w_branch_attn'] = din('w_branch_attn', [512, DM])
    W['w_out'] = din('w_out', [DM, DM])
    g1_d = din('ffn1_norm_gT', [128, KC]); gm_d = din('mix_norm_gT', [128, KC]); g2_d = din('ffn2_norm_gT', [128, KC])
    fg_d = din('final_norm_g', [DM])
    bi_d = din('mlstm_b_i', [4, 1]); bf_d = din('mlstm_b_f', [4, 1])
    og_d = din('mlstm_out_norm_g', [512]); sink_d = din('attn_sinks', [8])
    identb_d = din('ident_bf', [128, 128], BF16); identf_d = din('ident_f', [128, 128])
    sel_d = din('sel', [4, 512]); maskneg_d = din('maskneg', [128, 128])
    mcur_d = din('mask_cur', [128, 128], BF16); mprev_d = din('mask_prev', [128, 128], BF16)
    invf_d = din('inv_freq', [8])
    out_d = nc.dram_tensor('out', [NTOK, DM], F32, kind="ExternalOutput").ap()

    def sb(name, shape, dt=F32):
        return st.enter_context(nc.sbuf_tensor(name, list(shape), dt))
    XSL = TT + TT // 2
    X = sb('X', [128, XSL, DM])
    xbase = [0]
    def xs(t):
        return (xbase[0] + t) % XSL
    HT = sb('HT', [128, KC, ST], BF16)
    SLAB_E = 12352
    SLAB = [sb('SLAB%d' % i, [128, SLAB_E], BF16) for i in range(2)]
    Y = [sb('Y%d' % i, [128, DM]) for i in range(2)]
    FG = sb('FG', [128, DM])
    GT = {1: sb('G1', [128, KC]), 'm': sb('GM', [128, KC]), 2: sb('G2', [128, KC])}
    IDB = sb('IDB', [128, 128], BF16)
    SS = sb('SS', [128, TT]); RS = sb('RS', [128, TT])
    HB = [sb('HB%d' % i, [128, DM], BF16) for i in range(2)]
    HM = sb('HM', [128, 4, ST], BF16)
    OA = sb('OA', [128, 4, ST], BF16)
    IDF = sb('IDF', [128, 128])
    SEL = sb('SEL', [4, 512])
    MASKNEG = sb('MASKNEG', [128, 128])
    MCUR = sb('MCUR', [128, 128], BF16); MPREV = sb('MPREV', [128, 128], BF16)
    OG = sb('OG', [128, 512])
    ESINK = sb('ESINK', [128, 8])
    BIS = sb('BIS', [4, 1]); BFS = sb('BFS', [4, 1])
    COS = sb('COS', [128, NTILES, 8]); SIN = sb('SIN', [128, NTILES, 8])
    SST = sb('SST', [64, 4, 129]); SBF = sb('SBF', [64, 4, 129], BF16)
    RCOL = sb('RCOL', [128, 4]); BC = sb('BC', [4, 1]); MC = sb('MC', [4, 1])
    V1A = [sb('V1A%d' % i, [128, 2, 65], BF16) for i in range(3)]
    KTA = [sb('KTA%d' % i, [64, 2, 128], BF16) for i in range(3)]
    DUMMY = sb('DUMMY', [128, 8])
    AE = 25600
    ARENA = sb('ARENA', [128, AE], BF16)
    arena_names = set()
    aoff = [0]
    def areset():
        aoff[0] = 0
    def av(name, parts, free, dt=F32, register=True):
        n = int(np.prod(free))
        four = dt in (F32, I32)
        ne = n * (2 if four else 1)
        ne = (ne + 15) // 16 * 16
        a = aoff[0]; aoff[0] += ne
        assert aoff[0] <= AE, (name, aoff[0])
        v = ARENA[0:parts, a:a + n * (2 if four else 1)]
        if four:
            v = v.bitcast(dt)
        if len(free) == 2:
            v = v.rearrange("p (a b) -> p a b", a=free[0])
        if register:
            arena_names.add(name)
        return v
    _add = P.add
    def padd(eng, fn, r=(), w=(), dma=None):
        r = list(r)
        for k in list(r) + list(w):
            b = k[0] if isinstance(k, tuple) else k
            if b in arena_names:
                r.append('A'); break
        _add(eng, fn, r, w, dma)
    P.add = padd
    bar_ctr = [0]
    SETUP_KEYS = []
    def abarrier():
        P.add('dve', lambda e: e.memset(DUMMY[:], 0.0), r=list(SETUP_KEYS), w=['A', 'DUMMY'])
        areset()
    def ffn_views():
        abarrier()
        SGv = [av('SG', 128, [MTOK]) for i in range(2)]
        ACv = [av('ACTT', 128, [4, MTOK], BF16) for i in range(2)]
        return SGv, ACv

    PS = [st.enter_context(nc.psum_tensor('PS%d' % i, [128, 512], F32)) for i in range(6)]
    PSB = [st.enter_context(nc.psum_tensor('PSB%d' % i, [128, 1024], BF16)) for i in range(2)]

    cl = 0
    def cload(dst, src, key):
        nonlocal cl
        P.add('sp', lambda e: e.dma_start(out=dst, in_=src), w=[key], dma='c%d' % cl)
        cl += 1
    cload(IDB[:], identb_d, 'IDB')
    cload(FG[:], fg_d.partition_broadcast(128), 'FG')
    cload(GT[1][:], g1_d, 'GT1'); cload(GT['m'][:], gm_d, 'GTm'); cload(GT[2][:], g2_d, 'GT2')

    slab_ctr = [0]
    def load_slab(parts):
        i = slab_ctr[0]; slab_ctr[0] += 1
        buf = SLAB[i % 2]
        for pi, (mk_dst, src) in enumerate(parts):
            dst = mk_dst(buf)
            wk = [('slab', i % 2, q) for q in range(3)] if pi == 0 else [('slab', i % 2, pi)]
            P.add('pool', lambda e, dst=dst, src=src: e.dma_start(out=dst, in_=src),
                  w=wk, dma='slab%d_%d' % (i % 2, pi))
        return buf, i % 2

    def load_x(T, tiles):
        for t in tiles:
            r0 = T * ST + t * 128
            sl = (T * TT + t) % XSL
            P.add('sp', lambda e, sl=sl, r0=r0: e.dma_start(out=X[:, sl, :], in_=x_d[r0:r0 + 128, :]),
                  w=[('X', sl)], dma='x%d' % (t % 4))

    load_x(0, range(TT))

    def rstd_batch(tiles):
        t0_, t1_ = tiles[0], tiles[-1] + 1
        for t in tiles:
            sl = xs(t)
            P.add('act', lambda e, t=t, sl=sl: e.activation(out=HB[0][:], in_=X[:, sl, :], func=AF.Square, accum_out=SS[:, t:t + 1]),
                  r=[('X', sl)], w=[('HB', 0), ('SS', t)])
        P.add('dve', lambda e: e.tensor_scalar(RS[:, t0_:t1_], SS[:, t0_:t1_], 1.0 / DM, EPS, op0=ALU.mult, op1=ALU.add),
              r=[('SS', t) for t in tiles], w=[('RS', t0_)])
        P.add('act', lambda e: e.activation(out=RS[:, t0_:t1_], in_=RS[:, t0_:t1_], func=AF.Ln), r=[('RS', t0_)], w=[('RS', t0_)])
        P.add('act', lambda e: e.activation(out=RS[:, t0_:t1_], in_=RS[:, t0_:t1_], func=AF.Exp, scale=-0.5), r=[('RS', t0_)], w=[('RS', t0_)])

    def norm_to_HT(gkey):
        G = GT[gkey]
        halves = [list(range(0, TT // 2)), list(range(TT // 2, TT))]
        for tiles in halves:
            rstd_batch(tiles)
            rk = ('RS', tiles[0])
            for t in tiles:
                hb = HB[t % 2]
                sl = xs(t)
                P.add('act', lambda e, t=t, hb=hb, sl=sl: e.activation(out=hb[:], in_=X[:, sl, :], func=AF.Copy, scale=RS[:, t:t + 1]),
                      r=[('X', sl), rk], w=[('HB', t % 2)])
                tp = PSB[0]
                for k in range(KC):
                    P.add('pe', lambda e, k=k, hb=hb, tp=tp: e.transpose(tp[:, k * 128:(k + 1) * 128], hb[:, k * 128:(k + 1) * 128], IDB[:]),
                          r=[('HB', t % 2), 'IDB'], w=[('PSB', 0)])
                P.add('dve', lambda e, t=t, tp=tp: e.tensor_tensor(
                    out=HT[:, :, t * 128:(t + 1) * 128],
                    in0=tp[:, 0:KC * 128].rearrange("p (k c) -> p k c", k=KC),
                    in1=G[:].unsqueeze(2).to_broadcast([128, KC, 128]), op=ALU.mult),
                    r=[('PSB', 0), 'GT%s' % gkey], w=[('HT', t)])

    cnt = dict(gu=0, o=0, act=0)
    def ffn(f):
        wg, wu, wd = W['ffn%d_w_gate' % f], W['ffn%d_w_up' % f], W['ffn%d_w_down' % f]
        norm_to_HT(f)
        SG, ACTT = ffn_views()
        groups = []
        c = 0
        while c < FC:
            n = min(4, FC - c)
            groups.append((c, n)); c += n
        for (c0, n) in groups:
            WGO, WUO, WDO = 0, KC * 512, 2 * KC * 512
            parts = [
                (lambda b, n=n: b[:, WGO:WGO + KC * n * 128].rearrange("p (k c) -> p k c", k=KC),
                 wg[:, c0 * 128:(c0 + n) * 128].rearrange("(k p) c -> p k c", p=128)),
                (lambda b, n=n: b[:, WUO:WUO + KC * n * 128].rearrange("p (k c) -> p k c", k=KC),
                 wu[:, c0 * 128:(c0 + n) * 128].rearrange("(k p) c -> p k c", p=128)),
                (lambda b, n=n: b[:, WDO:WDO + n * DM].rearrange("p (c d) -> p c d", c=n),
                 wd[c0 * 128:(c0 + n) * 128, :].rearrange("(c p) d -> p c d", p=128)),
            ]
            buf, bi = load_slab(parts)
            Wg = buf[:, WGO:WGO + KC * n * 128].rearrange("p (k c) -> p k c", k=KC)
            Wu = buf[:, WUO:WUO + KC * n * 128].rearrange("p (k c) -> p k c", k=KC)
            Wd = buf[:, WDO:WDO + n * DM].rearrange("p (c d) -> p c d", c=n)
            for m in range(NM):
                ab = cnt['act'] % 2; cnt['act'] += 1
                at = ACTT[ab]
                htk = [('HT', m * TPM + j) for j in range(TPM)]
                for ci in range(n):
                    gb = cnt['gu'] % 2; cnt['gu'] += 1
                    Gp, Up = PS[2 * gb], PS[2 * gb + 1]
                    for (Wx, Pp, pi, bk) in ((Wg, Gp, 0, 2 * gb), (Wu, Up, 1, 2 * gb + 1)):
                        for k in range(KC):
                            P.add('pe', lambda e, Wx=Wx, Pp=Pp, k=k, ci=ci, m=m: e.matmul(
                                Pp[:, 0:MTOK], Wx[:, k, ci * 128:(ci + 1) * 128], HT[:, k, m * MTOK:(m + 1) * MTOK],
                                start=(k == 0), stop=(k == KC - 1)),
                                r=[('slab', bi, pi)] + htk, w=[('PS', bk)])
                    sg = SG[gb]
                    P.add('act', lambda e, sg=sg, Gp=Gp: e.activation(out=sg[:], in_=Gp[:, 0:MTOK], func=AF.Silu),
                          r=[('PS', 2 * gb)], w=[('SG', gb)])
                    P.add('dve', lambda e, sg=sg, Up=Up, at=at, ci=ci: e.tensor_tensor(out=at[:, ci, :], in0=Up[:, 0:MTOK], in1=sg[:], op=ALU.mult),
                          r=[('PS', 2 * gb + 1), ('SG', gb)], w=[('ACTT', ab, ci)])
                for j in range(TPM):
                    t = m * TPM + j
                    for h in range(NH):
                        ob = cnt['o'] % 2; cnt['o'] += 1
                        Op = PS[4 + ob]
                        for ci in range(n):
                            P.add('pe', lambda e, Op=Op, at=at, ci=ci, j=j, h=h, n=n, Wd=Wd: e.matmul(
                                Op[:, 0:HW], at[:, ci, j * 128:(j + 1) * 128], Wd[:, ci, h * HW:(h + 1) * HW],
                                start=(ci == 0), stop=(ci == n - 1)),
                                r=[('ACTT', ab, ci), ('slab', bi, 2)], w=[('PS', 4 + ob)])
                        sl = xs(t)
                        P.add('dve', lambda e, Op=Op, sl=sl, h=h: e.scalar_tensor_tensor(
                            out=X[:, sl, h * HW:(h + 1) * HW], in0=Op[:, 0:HW], scalar=0.5, in1=X[:, sl, h * HW:(h + 1) * HW],
                            op0=ALU.mult, op1=ALU.add),
                            r=[('PS', 4 + ob), ('X', sl)], w=[('X', sl)])

    def final_out(T):
        for tiles in (list(range(0, TT // 2)), list(range(TT // 2, TT))):
            rstd_batch(tiles)
            rk = ('RS', tiles[0])
            for t in tiles:
                y = Y[t % 2]
                sl = xs(t)
                r0 = T * ST + t * 128
                P.add('dve', lambda e, t=t, y=y, sl=sl: e.scalar_tensor_tensor(out=y[:], in0=X[:, sl, :], scalar=RS[:, t:t + 1], in1=FG[:],
                                                                        op0=ALU.mult, op1=ALU.mult),
                      r=[('X', sl), rk, 'FG'], w=[('Y', t % 2)])
                P.add('sp', lambda e, y=y, r0=r0: e.dma_start(out=out_d[r0:r0 + 128, :], in_=y[:]),
                      r=[('Y', t % 2)], dma='o%d' % (t % 2))

    import math
    cload(IDF[:], identf_d, 'IDF'); cload(SEL[:], sel_d, 'SEL'); cload(MASKNEG[:], maskneg_d, 'MASKNEG')
    cload(MCUR[:], mcur_d, 'MCUR'); cload(MPREV[:], mprev_d, 'MPREV')
    cload(OG[:], og_d.partition_broadcast(128), 'OG')
    cload(ESINK[:], sink_d.partition_broadcast(128), 'ESINK')
    cload(BIS[:], bi_d, 'BIS'); cload(BFS[:], bf_d, 'BFS')
    aoff[0] = 8192
    POSI = av('POSI', 128, [NTILES], I32, register=False); POSF = av('POSF', 128, [NTILES], register=False); INVF = av('INVF', 128, [8], register=False)
    SETUP_KEYS.extend(['POSI', 'POSF', 'INVF', 'ANG', 'RRa', 'KI', 'KF', 'MMa'])
    cload(POSI[:], pos_d, 'POSI'); cload(INVF[:], invf_d.partition_broadcast(128), 'INVF')
    P.add('dve', lambda e: e.tensor_scalar(BIS[:], BIS[:], 1.0 / 15.0, None, op0=ALU.mult), r=['BIS'], w=['BIS'])
    P.add('dve', lambda e: e.tensor_scalar(BFS[:], BFS[:], 1.0 / 15.0, None, op0=ALU.mult), r=['BFS'], w=['BFS'])
    P.add('act', lambda e: e.activation(out=ESINK[:], in_=ESINK[:], func=AF.Exp), r=['ESINK'], w=['ESINK'])
    for i in range(3):
        P.add('dve', lambda e, i=i: e.memset(V1A[i][:], 1.0), w=[('V1A', i)])
        P.add('dve', lambda e, i=i: e.memset(KTA[i][:], 0.0), w=[('KTA', i)])
    NA = NTILES * 8
    ANG = av('ANG', 128, [NTILES, 8], register=False); RR_ = av('RRa', 128, [NA], register=False); KI = av('KI', 128, [NA], I32, register=False); KF = av('KF', 128, [NA], register=False); MM_ = av('MMa', 128, [NA], register=False)
    P.add('dve', lambda e: e.tensor_copy(POSF[:], POSI[:]), r=['POSI'], w=['POSF'])
    P.add('dve', lambda e: e.tensor_tensor(out=ANG[:], in0=POSF[:].unsqueeze(2).to_broadcast([128, NTILES, 8]),
                                           in1=INVF[:].unsqueeze(1).to_broadcast([128, NTILES, 8]), op=ALU.mult), r=['POSF', 'INVF'], w=['ANG'])
    TWO_PI = 2.0 * math.pi
    C1 = 6.28125; C2 = TWO_PI - C1
    def sin_of(dst, shift, tag):
        angf = ANG[:].rearrange("p a b -> p (a b)")
        P.add('dve', lambda e: e.tensor_scalar(RR_[:], angf, shift, None, op0=ALU.add), r=['ANG'], w=['RRa'])
        P.add('dve', lambda e: e.tensor_scalar(KF[:], RR_[:], 1.0 / TWO_PI, None, op0=ALU.mult), r=['RRa'], w=['KF'])
        P.add('dve', lambda e: e.tensor_copy(KI[:], KF[:]), r=['KF'], w=['KI'])
        P.add('dve', lambda e: e.tensor_copy(KF[:], KI[:]), r=['KI'], w=['KF'])
        P.add('dve', lambda e: e.scalar_tensor_tensor(out=RR_[:], in0=KF[:], scalar=-C1, in1=RR_[:], op0=ALU.mult, op1=ALU.add), r=['KF', 'RRa'], w=['RRa'])
        P.add('dve', lambda e: e.scalar_tensor_tensor(out=RR_[:], in0=KF[:], scalar=-C2, in1=RR_[:], op0=ALU.mult, op1=ALU.add), r=['KF', 'RRa'], w=['RRa'])
        P.add('dve', lambda e: e.tensor_scalar(MM_[:], RR_[:], math.pi, -TWO_PI, op0=ALU.is_gt, op1=ALU.mult), r=['RRa'], w=['MMa'])
        P.add('dve', lambda e: e.tensor_tensor(out=RR_[:], in0=RR_[:], in1=MM_[:], op=ALU.add), r=['RRa', 'MMa'], w=['RRa'])
        P.add('dve', lambda e: e.tensor_scalar(MM_[:], RR_[:], -math.pi, TWO_PI, op0=ALU.is_lt, op1=ALU.mult), r=['RRa'], w=['MMa'])
        P.add('dve', lambda e: e.tensor_tensor(out=RR_[:], in0=RR_[:], in1=MM_[:], op=ALU.add), r=['RRa', 'MMa'], w=['RRa'])
        P.add('dve', lambda e: e.tensor_scalar(RR_[:], RR_[:], math.pi, -math.pi, op0=ALU.min, op1=ALU.max), r=['RRa'], w=['RRa'])
        P.add('act', lambda e: e.activation(out=dst[:].rearrange("p a b -> p (a b)"), in_=RR_[:], func=AF.Sin), r=['RRa'], w=[tag])
    sin_of(SIN, 0.0, 'SIN')
    sin_of(COS, math.pi / 2.0, 'COS')

    W_IN = W['w_in']
    pcnt = dict(qk=0, g=0, o=0)

    def mixer(T):
        seq_first = (T % STPS == 0)
        norm_to_HT('m')
        abarrier()
        QT = av('QT', 64, [4, MTOK], BF16); KT = av('KT', 64, [4, MTOK], BF16)
        TI = av('TI', 4, [MTOK]); TF = av('TF', 4, [MTOK]); NEGB = av('NEGB', 4, [MTOK]); GG = av('GG', 4, [MTOK])
        MMs = av('MMs', 4, [MTOK]); NM_ = av('NMs', 4, [MTOK]); ONES4 = av('ONES4', 4, [MTOK]); ZEROS4 = av('ZEROS4', 4, [MTOK])
        GCOL = av('GCOL', 128, [4]); ENMs = [av('ENM', 128, [4]) for _ in range(2)]; NRs = [av('NR', 128, [4, 129]) for _ in range(2)]; RNEW = av('RNEW', 128, [4]); WCOL = av('WCOL', 128, [4])
        DEC = av('DEC', 64, [4]); DEN = av('DEN', 128, [4]); NEGD = av('NEGD', 128, [4]); RRm = av('RRm', 128, [4]); SS2 = av('SS2', 128, [4]); RS2 = av('RS2', 128, [4])
        ARG = av('ARG', 128, [4, 128]); DTm = av('DTm', 128, [4, 128]); PD = av('PD', 128, [4, 128], BF16)
        ARG2 = av('ARG2', 64, [4, 128]); EE = av('EE', 64, [4, 128]); QP = av('QP', 64, [4, 128], BF16)
        KP = av('KP', 128, [4, 64], BF16); V1 = av('V1', 128, [4, 129], BF16)
        SIGOs = [av('SIGO', 128, [512]) for _ in range(2)]; HMN = av('HMN', 128, [4, 128]); T1 = av('T1', 128, [4, 128]); HMB = av('HMB', 128, [512], BF16)
        JK2 = av('JK2', 128, [128])
        P.add('dve', lambda e: e.memset(ONES4, 1.0), w=['ONES4'])
        P.add('dve', lambda e: e.memset(ZEROS4, 0.0), w=['ZEROS4'])
        P.add('dve', lambda e: e.memset(V1, 1.0), w=['V1'])
        if seq_first:
            P.add('dve', lambda e: e.memset(SST[:], 0.0), w=['SST'])
            P.add('dve', lambda e: e.memset(SBF[:], 0.0), w=['SBF'])
            P.add('dve', lambda e: e.memset(RCOL[:], 0.0), w=['RCOL'])
            P.add('dve', lambda e: e.memset(BC[:], 0.0), w=['BC'])
            P.add('dve', lambda e: e.memset(MC[:], 0.0), w=['MC'])
        pending_back = [None]
        NW1 = 1544
        parts = [(lambda b: b[:, 0:KC * NW1].rearrange("p (k c) -> p k c", k=KC), W_IN[:, 0:NW1].rearrange("(k p) c -> p k c", p=128))]
        buf, bi = load_slab(parts)
        W1 = buf[:, 0:KC * NW1].rearrange("p (k c) -> p k c", k=KC)
        WK = [('slab', bi, 0)]
        for m in range(NM):
            mc0 = m * MTOK
            htk = [('HT', m * TPM + j) for j in range(TPM)]
            for h in range(4):
                for which in range(2):
                    coff = which * 256 + h * 64
                    b = pcnt['qk'] % 4; pcnt['qk'] += 1
                    for k in range(KC):
                        P.add('pe', lambda e, b=b, k=k, coff=coff, mc0=mc0: e.matmul(PS[b][0:64, 0:MTOK], W1[:, k, coff:coff + 64], HT[:, k, mc0:mc0 + MTOK],
                                                                             start=(k == 0), stop=(k == KC - 1)), r=WK + htk, w=[('PS', b)])
                    if which == 0:
                        P.add('act', lambda e, b=b, h=h: e.activation(out=QT[:, h, :], in_=PS[b][0:64, 0:MTOK], func=AF.Copy, scale=0.125),
                              r=[('PS', b)], w=['QT'])
                    else:
                        P.add('dve', lambda e, b=b, h=h: e.tensor_copy(KT[:, h, :], PS[b][0:64, 0:MTOK]), r=[('PS', b)], w=['KT'])
            for (b, coff) in ((4, 1536), (5, 1540)):
                for k in range(KC):
                    P.add('pe', lambda e, b=b, k=k, coff=coff, mc0=mc0: e.matmul(PS[b][0:4, 0:MTOK], W1[:, k, coff:coff + 4], HT[:, k, mc0:mc0 + MTOK],
                                                                         start=(k == 0), stop=(k == KC - 1)), r=WK + htk, w=[('PS', b)])
            P.add('act', lambda e: e.activation(out=TI, in_=PS[4][0:4, 0:MTOK], func=AF.Tanh, scale=1.0 / 15.0, bias=BIS[:]), r=[('PS', 4), 'BIS'], w=['TI'])
            P.add('act', lambda e: e.activation(out=TF, in_=PS[5][0:4, 0:MTOK], func=AF.Tanh, scale=1.0 / 15.0, bias=BFS[:]), r=[('PS', 5), 'BFS'], w=['TF'])
            P.add('act', lambda e: e.activation(out=TF, in_=TF, func=AF.Exp, scale=-15.0), r=['TF'], w=['TF'])
            P.add('act', lambda e: e.activation(out=TF, in_=TF, func=AF.Ln, bias=1.0), r=['TF'], w=['TF'])
            P.add('dve', lambda e: e.tensor_tensor_scan(NEGB, ONES4, TF, BC[:], ALU.mult, ALU.add), r=['ONES4', 'TF', 'BC'], w=['NEGB'])
            P.add('dve', lambda e: e.tensor_copy(BC[:], NEGB[:, MTOK - 1:MTOK]), r=['NEGB'], w=['BC'])
            P.add('dve', lambda e: e.scalar_tensor_tensor(out=GG, in0=TI, scalar=15.0, in1=NEGB, op0=ALU.mult, op1=ALU.add), r=['TI', 'NEGB'], w=['GG'])
            P.add('dve', lambda e: e.tensor_tensor_scan(MMs, GG, ZEROS4, MC[:], ALU.max, ALU.max), r=['GG', 'ZEROS4', 'MC'], w=['MMs'])
            P.add('dve', lambda e: e.tensor_copy(MC[:], MMs[:, MTOK - 1:MTOK]), r=['MMs'], w=['MC'])
            P.add('dve', lambda e: e.tensor_tensor(out=NM_, in0=NEGB, in1=MMs, op=ALU.subtract), r=['NEGB', 'MMs'], w=['NMs'])
            for j in range(TPM):
                t = m * TPM + j
                cj = slice(j * 128, (j + 1) * 128)
                tcols = slice(t * 128, (t + 1) * 128)
                par = t % 2
                a_start = len(P.ops)
                ENM = ENMs[par]; SIGO = SIGOs[par]; NR = NRs[par]
                ek = ('ENM', par); sk = ('SIGO', par); nk = ('NR', par)
                P.add('pe', lambda e, cj=cj: e.transpose(PS[4][:, 0:4], GG[:, cj], IDF[0:4, 0:4]), r=['GG', 'IDF'], w=[('PS', 4)])
                P.add('pe', lambda e, cj=cj: e.transpose(PS[4][:, 4:8], NM_[:, cj], IDF[0:4, 0:4]), r=['NMs', 'IDF'], w=[('PS', 4)])
                P.add('dve', lambda e: e.tensor_copy(GCOL, PS[4][:, 0:4]), r=[('PS', 4)], w=['GCOL'])
                P.add('act', lambda e, ENM=ENM: e.activation(out=ENM, in_=PS[4][:, 4:8], func=AF.Exp), r=[('PS', 4)], w=[ek])
                for h in range(4):
                    P.add('pe', lambda e, h=h, cj=cj: e.matmul(PS[3][:, h * 128:(h + 1) * 128], SEL[0:4, h * 128:(h + 1) * 128], MMs[:, cj], start=True, stop=True),
                          r=['SEL', 'MMs'], w=[('PS', 3)])
                PS3v = PS[3][:, 0:512].rearrange("p (a b) -> p a b", a=4)
                P.add('dve', lambda e: e.tensor_copy(RNEW, PS3v[:, :, 127]), r=[('PS', 3)], w=['RNEW'])
                P.add('dve', lambda e: e.scalar_tensor_tensor(out=ARG, in0=PS3v, scalar=-1.0, in1=MASKNEG[:].unsqueeze(1).to_broadcast([128, 4, 128]),
                                                              op0=ALU.mult, op1=ALU.add), r=[('PS', 3), 'MASKNEG'], w=['ARG'])
                P.add('dve', lambda e: e.tensor_tensor(out=ARG, in0=ARG, in1=GCOL.unsqueeze(2).to_broadcast([128, 4, 128]), op=ALU.add), r=['ARG', 'GCOL'], w=['ARG'])
                P.add('act', lambda e: e.activation(out=DTm, in_=ARG, func=AF.Exp), r=['ARG'], w=['DTm'])
                P.add('dve', lambda e: e.scalar_tensor_tensor(out=ARG2, in0=PS3v[0:64], scalar=-1.0, in1=RCOL[0:64, :].unsqueeze(2).to_broadcast([64, 4, 128]),
                                                              op0=ALU.mult, op1=ALU.add), r=[('PS', 3), 'RCOL'], w=['ARG2'])
                P.add('act', lambda e: e.activation(out=EE, in_=ARG2, func=AF.Exp), r=['ARG2'], w=['EE'])
                P.add('dve', lambda e, cj=cj: e.tensor_tensor(out=QP, in0=QT[:, :, cj], in1=EE, op=ALU.mult), r=['QT', 'EE'], w=['QP'])
                for (b, c0, wd_) in ((0, 256, 256), (1, 512, 512), (2, 1024, 512)):
                    for k in range(KC):
                        P.add('pe', lambda e, b=b, c0=c0, wd_=wd_, k=k, tcols=tcols: e.matmul(PS[b][:, 0:wd_], HT[:, k, tcols], W1[:, k, c0:c0 + wd_],
                                                                                          start=(k == 0), stop=(k == KC - 1)), r=WK + [('HT', t)], w=[('PS', b)])
                P.add('dve', lambda e: e.tensor_tensor(out=WCOL, in0=GCOL, in1=RNEW, op=ALU.subtract), r=['GCOL', 'RNEW'], w=['WCOL'])
                P.add('act', lambda e: e.activation(out=WCOL, in_=WCOL, func=AF.Exp), r=['WCOL'], w=['WCOL'])
                P.add('dve', lambda e: e.tensor_tensor(out=KP, in0=PS[0][:, 0:256].rearrange("p (a b) -> p a b", a=4),
                                                       in1=WCOL.unsqueeze(2).to_broadcast([128, 4, 64]), op=ALU.mult), r=[('PS', 0), 'WCOL'], w=['KP'])
                P.add('act', lambda e: e.activation(out=V1[:, :, 0:128], in_=PS[1][:, 0:512].rearrange("p (a b) -> p a b", a=4), func=AF.Copy),
                      r=[('PS', 1)], w=['V1'])
                P.add('act', lambda e, SIGO=SIGO: e.activation(out=SIGO, in_=PS[2][:, 0:512], func=AF.Sigmoid), r=[('PS', 2)], w=[sk])
                for h in range(4):
                    P.add('pe', lambda e, h=h, cj=cj: e.matmul(PS[5][:, h * 128:(h + 1) * 128], KT[:, h, cj], QT[:, h, cj], start=True, stop=True),
                          r=['KT', 'QT'], w=[('PS', 5)])
                P.add('dve', lambda e: e.tensor_tensor(out=PD, in0=PS[5][:, 0:512].rearrange("p (a b) -> p a b", a=4), in1=DTm, op=ALU.mult),
                      r=[('PS', 5), 'DTm'], w=['PD'])
                for h in range(4):
                    b = h // 2; o0 = (h % 2) * 129
                    P.add('pe', lambda e, h=h, b=b, o0=o0: e.matmul(PS[b][:, o0:o0 + 129], PD[:, h, :], V1[:, h, :], start=True, stop=False),
                          r=['PD', 'V1'], w=[('PS', b)])
                    P.add('pe', lambda e, h=h, b=b, o0=o0: e.matmul(PS[b][:, o0:o0 + 129], QP[:, h, :], SBF[:, h, :], start=False, stop=True),
                          r=['QP', 'SBF'], w=[('PS', b)])
                for h in range(4):
                    b = (2, 4)[h // 2]; o0 = (h % 2) * 129
                    P.add('pe', lambda e, h=h, b=b, o0=o0: e.matmul(PS[b][0:64, o0:o0 + 129], KP[:, h, :], V1[:, h, :], start=True, stop=True),
                          r=['KP', 'V1'], w=[('PS', b)])
                P.add('dve', lambda e: e.tensor_tensor(out=DEC, in0=RCOL[0:64, :], in1=RNEW[0:64, :], op=ALU.subtract), r=['RCOL', 'RNEW'], w=['DEC'])
                P.add('act', lambda e: e.activation(out=DEC, in_=DEC, func=AF.Exp), r=['DEC'], w=['DEC'])
                for h in range(4):
                    b = (2, 4)[h // 2]; o0 = (h % 2) * 129
                    P.add('dve', lambda e, h=h, b=b, o0=o0: e.scalar_tensor_tensor(out=SST[:, h, :], in0=SST[:, h, :], scalar=DEC[:, h:h + 1], in1=PS[b][0:64, o0:o0 + 129],
                                                                                   op0=ALU.mult, op1=ALU.add), r=['SST', 'DEC', ('PS', b), 'SBF'], w=['SST'])
                P.add('act', lambda e: e.activation(out=SBF[:], in_=SST[:], func=AF.Copy), r=['SST'], w=['SBF'])
                P.add('dve', lambda e: e.tensor_copy(RCOL[:], RNEW), r=['RNEW'], w=['RCOL'])
                for b in range(2):
                    pv = PS[b][:, 0:258].rearrange("p (a b) -> p a b", a=2)
                    P.add('dve', lambda e, b=b, pv=pv, NR=NR: e.tensor_copy(NR[:, 2 * b:2 * b + 2, :], pv), r=[('PS', b)], w=[nk])

                def back(t=t, tcols=tcols, ENM=ENM, SIGO=SIGO, NR=NR, ek=ek, sk=sk, nk=nk):
                    P.add('dve', lambda e: e.tensor_copy(DEN, NR[:, :, 128]), r=[nk], w=['DEN'])
                    P.add('dve', lambda e: e.tensor_scalar(NEGD, DEN, -1.0, None, op0=ALU.mult), r=['DEN'], w=['NEGD'])
                    P.add('dve', lambda e: e.tensor_tensor(out=DEN, in0=DEN, in1=NEGD, op=ALU.max), r=['DEN', 'NEGD'], w=['DEN'])
                    P.add('dve', lambda e: e.tensor_tensor(out=DEN, in0=DEN, in1=ENM, op=ALU.max), r=['DEN', ek], w=['DEN'])
                    P.add('dve', lambda e: e.reciprocal(RRm, DEN), r=['DEN'], w=['RRm'])
                    P.add('dve', lambda e: e.tensor_tensor(out=HMN, in0=NR[:, :, 0:128], in1=RRm.unsqueeze(2).to_broadcast([128, 4, 128]), op=ALU.mult),
                          r=[nk, 'RRm'], w=['HMN'])
                    for h in range(4):
                        P.add('act', lambda e, h=h: e.activation(out=JK2, in_=HMN[:, h, :], func=AF.Square, accum_out=SS2[:, h:h + 1]), r=['HMN'], w=['JK2', 'SS2'])
                    P.add('dve', lambda e: e.tensor_scalar(RS2, SS2, 1.0 / 128.0, EPS, op0=ALU.mult, op1=ALU.add), r=['SS2'], w=['RS2'])
                    P.add('act', lambda e: e.activation(out=RS2, in_=RS2, func=AF.Ln), r=['RS2'], w=['RS2'])
                    P.add('act', lambda e: e.activation(out=RS2, in_=RS2, func=AF.Exp, scale=-0.5), r=['RS2'], w=['RS2'])
                    P.add('dve', lambda e: e.tensor_tensor(out=T1, in0=HMN, in1=RS2.unsqueeze(2).to_broadcast([128, 4, 128]), op=ALU.mult), r=['HMN', 'RS2'], w=['T1'])
                    P.add('dve', lambda e: e.tensor_tensor(out=SIGO, in0=SIGO, in1=OG[:], op=ALU.mult), r=[sk, 'OG'], w=[sk])
                    P.add('dve', lambda e: e.tensor_tensor(out=HMB, in0=T1.rearrange("p a b -> p (a b)"), in1=SIGO, op=ALU.mult), r=['T1', sk], w=['HMB'])
                    for kc in range(4):
                        P.add('pe', lambda e, kc=kc: e.transpose(PSB[1][:, kc * 128:(kc + 1) * 128], HMB[:, kc * 128:(kc + 1) * 128], IDB[:]), r=['HMB', 'IDB'], w=[('PSB', 1)])
                    P.add('dve', lambda e: e.tensor_copy(HM[:, :, tcols], PSB[1][:, 0:512].rearrange("p (a b) -> p a b", a=4)), r=[('PSB', 1)], w=[('HM', t)])
                if pending_back[0] is not None:
                    A = P.ops[a_start:]; del P.ops[a_start:]
                    b_start = len(P.ops)
                    pending_back[0]()
                    B = P.ops[b_start:]; del P.ops[b_start:]
                    lead = 12
                    step = max(1, (len(A) - lead) // (len(B) + 1))
                    merged = []; bi_ = 0
                    for ai, op in enumerate(A):
                        merged.append(op)
                        if ai >= lead and (ai - lead) % step == step - 1 and bi_ < len(B):
                            merged.append(B[bi_]); bi_ += 1
                    merged.extend(B[bi_:])
                    P.ops.extend(merged)
                pending_back[0] = back
        if pending_back[0] is not None:
            pending_back[0]()
            pending_back[0] = None

        abarrier()
        QKVS = av('QKVS', 128, [768]); QKB = av('QKB', 128, [10, 64], BF16)
        TA = av('TA', 128, [10, 8]); TB = av('TB', 128, [10, 8])
        QTAs = [av('QTA', 64, [8, 128], BF16) for _ in range(2)]
        EB = [[av('EB', 128, [4, 128], BF16) for _ in range(2)] for _ in range(2)]
        PT = [[av('PT', 128, [4, 128], BF16) for _ in range(2)] for _ in range(2)]
        OAB = av('OAB', 128, [8, 64], BF16); DENA = av('DENA', 128, [8]); RRA = av('RRA', 128, [8])
        NW2 = 768
        parts = [(lambda b: b[:, 0:KC * NW2].rearrange("p (k c) -> p k c", k=KC), W_IN[:, 1544:1544 + NW2].rearrange("(k p) c -> p k c", p=128))]
        buf, bi = load_slab(parts)
        W2 = buf[:, 0:KC * NW2].rearrange("p (k c) -> p k c", k=KC)
        WK = [('slab', bi, 0)]
        XR = QKVS[:, 0:640].rearrange("p (a b) -> p a b", a=10)

        def p2_stage1(t):
            gt = T * TT + t
            par = gt % 3
            qp = t % 2
            QTA = QTAs[qp]
            tcols = slice(t * 128, (t + 1) * 128)
            for (b, c0, wd_) in ((0, 0, 512), (1, 512, 256)):
                for k in range(KC):
                    P.add('pe', lambda e, b=b, c0=c0, wd_=wd_, k=k, tcols=tcols: e.matmul(PS[b][:, 0:wd_], HT[:, k, tcols], W2[:, k, c0:c0 + wd_],
                                                                                      start=(k == 0), stop=(k == KC - 1)), r=WK + [('HT', t)], w=[('PS', b)])
            P.add('act', lambda e: e.activation(out=QKVS[:, 0:512], in_=PS[0][:, 0:512], func=AF.Copy, scale=0.125), r=[('PS', 0)], w=['QKVS'])
            P.add('act', lambda e: e.activation(out=QKVS[:, 512:768], in_=PS[1][:, 0:256], func=AF.Copy), r=[('PS', 1)], w=['QKVS'])
            cosb = COS[:, gt, :].unsqueeze(1).to_broadcast([128, 10, 8]); sinb = SIN[:, gt, :].unsqueeze(1).to_broadcast([128, 10, 8])
            x1 = XR[:, :, 0:8]; x2 = XR[:, :, 8:16]
            P.add('dve', lambda e: e.tensor_tensor(out=TA, in0=x1, in1=cosb, op=ALU.mult), r=['QKVS', 'COS'], w=['TA'])
            P.add('dve', lambda e: e.tensor_tensor(out=TB, in0=x2, in1=sinb, op=ALU.mult), r=['QKVS', 'SIN'], w=['TB'])
            P.add('dve', lambda e: e.tensor_tensor(out=QKB[:, :, 0:8], in0=TA, in1=TB, op=ALU.subtract), r=['TA', 'TB'], w=['QKB'])
            P.add('dve', lambda e: e.tensor_tensor(out=TA, in0=x2, in1=cosb, op=ALU.mult), r=['QKVS', 'COS', 'QKB'], w=['TA'])
            P.add('dve', lambda e: e.tensor_tensor(out=TB, in0=x1, in1=sinb, op=ALU.mult), r=['QKVS', 'SIN', 'QKB'], w=['TB'])
            P.add('dve', lambda e: e.tensor_tensor(out=QKB[:, :, 8:16], in0=TA, in1=TB, op=ALU.add), r=['TA', 'TB'], w=['QKB'])
            P.add('dve', lambda e: e.tensor_copy(QKB[:, :, 16:64], XR[:, :, 16:64]), r=['QKVS'], w=['QKB'])
            P.add('act', lambda e: e.activation(out=V1A[par][:, :, 0:64], in_=QKVS[:, 640:768].rearrange("p (a b) -> p a b", a=2), func=AF.Copy),
                  r=['QKVS'], w=[('V1A', par)])
            for h in range(8):
                P.add('pe', lambda e, h=h: e.transpose(PSB[0][0:64, h * 128:(h + 1) * 128], QKB[:, h, :], IDB[:]), r=['QKB', 'IDB'], w=[('PSB', 0)])
            for kv in range(2):
                P.add('pe', lambda e, kv=kv: e.transpose(PSB[1][0:64, kv * 128:(kv + 1) * 128], QKB[:, 8 + kv, :], IDB[:]), r=['QKB', 'IDB'], w=[('PSB', 1)])
            P.add('dve', lambda e: e.tensor_copy(QTA, PSB[0][0:64, 0:1024].rearrange("p (a b) -> p a b", a=8)), r=[('PSB', 0)], w=[('QTA', qp)])
            P.add('dve', lambda e: e.tensor_copy(KTA[par][:], PSB[1][0:64, 0:256].rearrange("p (a b) -> p a b", a=2)), r=[('PSB', 1)], w=[('KTA', par)])

        def p2_stage2(t):
            gt = T * TT + t
            lt = (T % STPS) * TT + t
            par = gt % 3
            ppar = (gt - 1) % 3
            qp = t % 2
            QTA = QTAs[qp]
            tcols = slice(t * 128, (t + 1) * 128)
            has_prev = lt > 0
            for kv in range(2):
                for pc in range(2):
                    if pc == 1 and not has_prev:
                        continue
                    b = 2 + 2 * kv + pc
                    src = par if pc == 0 else ppar
                    P.add('pe', lambda e, b=b, kv=kv, src=src: e.matmul(PS[b][:, 0:512], KTA[src][:, kv, :], QTA[:, 4 * kv:4 * kv + 4, :], start=True, stop=True),
                          r=[('KTA', src), ('QTA', qp)], w=[('PS', b)])
                    eb = EB[kv][pc]; pt = PT[kv][pc]
                    mk = MCUR if pc == 0 else MPREV
                    P.add('act', lambda e, b=b, eb=eb: e.activation(out=eb, in_=PS[b][:, 0:512].rearrange("p (a b) -> p a b", a=4), func=AF.Exp), r=[('PS', b)], w=[('EB', kv, pc)])
                    P.add('dve', lambda e, eb=eb, pt=pt, mk=mk: e.tensor_tensor(out=pt, in0=eb, in1=mk[:].unsqueeze(1).to_broadcast([128, 4, 128]), op=ALU.mult),
                          r=[('EB', kv, pc), 'MCUR', 'MPREV'], w=[('PT', kv, pc)])
            for h in range(8):
                kv = h // 4; hh = h % 4; b = 2 + 2 * kv; o0 = hh * 65
                if has_prev:
                    P.add('pe', lambda e, b=b, o0=o0, kv=kv, hh=hh: e.matmul(PS[b][:, o0:o0 + 65], PT[kv][1][:, hh, :], V1A[ppar][:, kv, :], start=True, stop=False),
                          r=[('PT', kv, 1), ('V1A', ppar)], w=[('PS', b)])
                P.add('pe', lambda e, b=b, o0=o0, kv=kv, hh=hh: e.matmul(PS[b][:, o0:o0 + 65], PT[kv][0][:, hh, :], V1A[par][:, kv, :],
                                                                     start=(not has_prev), stop=True),
                      r=[('PT', kv, 0), ('V1A', par)], w=[('PS', b)])
            for kv in range(2):
                b = 2 + 2 * kv
                pv = PS[b][:, 0:260].rearrange("p (a b) -> p a b", a=4)
                P.add('dve', lambda e, kv=kv, pv=pv: e.tensor_tensor(out=DENA[:, 4 * kv:4 * kv + 4], in0=pv[:, :, 64], in1=ESINK[:, 4 * kv:4 * kv + 4], op=ALU.add),
                      r=[('PS', b), 'ESINK'], w=['DENA'])
            P.add('dve', lambda e: e.reciprocal(RRA, DENA), r=['DENA'], w=['RRA'])
            for kv in range(2):
                b = 2 + 2 * kv
                pv = PS[b][:, 0:260].rearrange("p (a b) -> p a b", a=4)
                P.add('dve', lambda e, kv=kv, pv=pv: e.tensor_tensor(out=OAB[:, 4 * kv:4 * kv + 4, :], in0=pv[:, :, 0:64],
                                                                     in1=RRA[:, 4 * kv:4 * kv + 4].unsqueeze(2).to_broadcast([128, 4, 64]), op=ALU.mult),
                      r=[('PS', b), 'RRA'], w=['OAB'])
            OABf = OAB.rearrange("p a b -> p (a b)")
            for kc in range(4):
                P.add('pe', lambda e, kc=kc: e.transpose(PSB[1][:, 512 + kc * 128:512 + (kc + 1) * 128], OABf[:, kc * 128:(kc + 1) * 128], IDB[:]), r=['OAB', 'IDB'], w=[('PSB', 1)])
            P.add('dve', lambda e: e.tensor_copy(OA[:, :, tcols], PSB[1][:, 512:1024].rearrange("p (a b) -> p a b", a=4)), r=[('PSB', 1)], w=[('OA', t)])

        def capture(fn, *a):
            s0 = len(P.ops)
            fn(*a)
            ops_ = P.ops[s0:]; del P.ops[s0:]
            return ops_

        def interleave(A, B):
            out = []
            na, nb = len(A), len(B)
            ia = ib = 0
            while ia < na or ib < nb:
                if ib >= nb or (ia < na and ia * nb <= ib * na):
                    out.append(A[ia]); ia += 1
                else:
                    out.append(B[ib]); ib += 1
            return out

        p2_stage1(0)
        for t in range(1, TT):
            A = capture(p2_stage1, t)
            B = capture(p2_stage2, t - 1)
            P.ops.extend(interleave(A, B))
        p2_stage2(TT - 1)

        abarrier()
        MG = av('MG', 128, [KC, ST], BF16)
        SGG = [av('SGG', 128, [MTOK]) for _ in range(2)]
        TMPG = [av('TMPG', 128, [MTOK]) for _ in range(2)]
        for (br, gc0, wbr, SRC, skey) in (('m', 2312, W['w_branch_mlstm'], HM, 'HM'), ('a', 2312 + DM, W['w_branch_attn'], OA, 'OA')):
            GO, BO = 0, KC * DM
            parts = [(lambda b: b[:, GO:GO + KC * DM].rearrange("p (k c) -> p k c", k=KC), W_IN[:, gc0:gc0 + DM].rearrange("(k p) c -> p k c", p=128)),
                     (lambda b: b[:, BO:BO + 4 * DM].rearrange("p (k c) -> p k c", k=4), wbr.rearrange("(k p) c -> p k c", p=128))]
            buf, bi = load_slab(parts)
            W3 = buf[:, GO:GO + KC * DM].rearrange("p (k c) -> p k c", k=KC)
            WB = buf[:, BO:BO + 4 * DM].rearrange("p (k c) -> p k c", k=4)
            for m in range(NM):
                mc0 = m * MTOK
                htk = [('HT', m * TPM + j) for j in range(TPM)]
                srk = [(skey, m * TPM + j) for j in range(TPM)]
                for d in range(KC):
                    gb = pcnt['g'] % 2; pcnt['g'] += 1
                    Gp, Yp = PS[2 * gb], PS[2 * gb + 1]
                    for k in range(KC):
                        P.add('pe', lambda e, Gp=Gp, k=k, d=d, mc0=mc0, W3=W3: e.matmul(Gp[:, 0:MTOK], W3[:, k, d * 128:(d + 1) * 128], HT[:, k, mc0:mc0 + MTOK],
                                                                                start=(k == 0), stop=(k == KC - 1)), r=[('slab', bi, 0)] + htk, w=[('PS', 2 * gb)])
                    for k in range(4):
                        P.add('pe', lambda e, Yp=Yp, k=k, d=d, mc0=mc0, WB=WB, SRC=SRC: e.matmul(Yp[:, 0:MTOK], WB[:, k, d * 128:(d + 1) * 128], SRC[:, k, mc0:mc0 + MTOK],
                                                                                         start=(k == 0), stop=(k == 3)), r=[('slab', bi, 1)] + srk, w=[('PS', 2 * gb + 1)])
                    sgg = SGG[gb]
                    P.add('act', lambda e, sgg=sgg, Gp=Gp: e.activation(out=sgg, in_=Gp[:, 0:MTOK], func=AF.Sigmoid), r=[('PS', 2 * gb)], w=[('SGG', gb)])
                    if br == 'm':
                        P.add('dve', lambda e, sgg=sgg, Yp=Yp, d=d, mc0=mc0: e.tensor_tensor(out=MG[:, d, mc0:mc0 + MTOK], in0=Yp[:, 0:MTOK], in1=sgg, op=ALU.mult),
                              r=[('PS', 2 * gb + 1), ('SGG', gb)], w=[('MG', d, m)])
                    else:
                        tg = TMPG[gb]
                        P.add('dve', lambda e, sgg=sgg, Yp=Yp, tg=tg: e.tensor_tensor(out=tg, in0=Yp[:, 0:MTOK], in1=sgg, op=ALU.mult),
                              r=[('PS', 2 * gb + 1), ('SGG', gb)], w=[('TMPG', gb)])
                        P.add('dve', lambda e, tg=tg, d=d, mc0=mc0: e.tensor_tensor(out=MG[:, d, mc0:mc0 + MTOK], in0=MG[:, d, mc0:mc0 + MTOK], in1=tg, op=ALU.add),
                              r=[('TMPG', gb), ('MG', d, m)], w=[('MG', d, m)])
        parts = [(lambda b: b[:, 0:KC * DM].rearrange("p (k c) -> p k c", k=KC), W['w_out'].rearrange("(k p) c -> p k c", p=128))]
        buf, bi = load_slab(parts)
        WO = buf[:, 0:KC * DM].rearrange("p (k c) -> p k c", k=KC)
        for t in range(TT):
            m = t // TPM
            tcols = slice(t * 128, (t + 1) * 128)
            for h in range(NH):
                ob = pcnt['o'] % 2; pcnt['o'] += 1
                Op = PS[4 + ob]
                for k in range(KC):
                    P.add('pe', lambda e, Op=Op, k=k, h=h, tcols=tcols: e.matmul(Op[:, 0:HW], MG[:, k, tcols], WO[:, k, h * HW:(h + 1) * HW], start=(k == 0), stop=(k == KC - 1)),
                          r=[('slab', bi, 0)] + [('MG', d, m) for d in range(KC)], w=[('PS', 4 + ob)])
                sl = xs(t)
                P.add('dve', lambda e, Op=Op, sl=sl, h=h: e.tensor_tensor(out=X[:, sl, h * HW:(h + 1) * HW], in0=Op[:, 0:HW], in1=X[:, sl, h * HW:(h + 1) * HW], op=ALU.add),
                      r=[('PS', 4 + ob), ('X', sl)], w=[('X', sl)])


    for T in range(NST):
        xbase[0] = (T * TT) % XSL
        if T + 1 < NST:
            load_x(T + 1, range(0, TT // 2))
        if 'ffn1' in phases:
            ffn(1)
        if 'mix' in phases:
            mixer(T)
        if 'ffn2' in phases:
            ffn(2)
        final_out(T)
        if T + 1 < NST:
            load_x(T + 1, range(TT // 2, TT))
    P.add('sp', None, r=[('dmasem', 'o0'), ('dmasem', 'o1')])
    P.build(st)
    st.close()
    return nc, P


def make_consts():
    s = np.arange(128)[:, None]; j = np.arange(128)[None, :]
    c = {}
    c['ident_bf'] = np.eye(128).astype(ml_dtypes.bfloat16)
    c['ident_f'] = np.eye(128).astype(np.float32)
    sel = np.zeros((4, 4, 128), np.float32)
    for h in range(4):
        sel[h, h, :] = 1.0
    c['sel'] = sel.reshape(4, 512)
    c['maskneg'] = np.where(s <= j, 0.0, -1e30).astype(np.float32)
    c['mask_cur'] = (s <= j).astype(np.float32).astype(ml_dtypes.bfloat16)
    c['mask_prev'] = (s > j).astype(np.float32).astype(ml_dtypes.bfloat16)
    c['inv_freq'] = (500000.0 ** (-np.arange(8, dtype=np.float32) * 2.0 / 16)).astype(np.float32)
    return c


def make_in_maps(cfg, inputs):
    DM, S, NSEQ, NC_ = cfg['DM'], cfg['S'], cfg['NSEQ'], cfg['NCORES']
    KC = DM // 128
    consts = make_consts()
    shared = dict(consts)
    f32 = lambda a: np.ascontiguousarray(np.asarray(a, dtype=np.float32))
    for n in WNAMES:
        shared[n] = f32(inputs[n][0])
    for n, k in (('ffn1_norm_g', 'ffn1_norm_gT'), ('mix_norm_g', 'mix_norm_gT'), ('ffn2_norm_g', 'ffn2_norm_gT')):
        shared[k] = f32(np.asarray(inputs[n][0]).reshape(KC, 128).T)
    shared['final_norm_g'] = f32(inputs['final_norm_g'])
    shared['mlstm_b_i'] = f32(np.asarray(inputs['mlstm_b_i'][0]).reshape(4, 1))
    shared['mlstm_b_f'] = f32(np.asarray(inputs['mlstm_b_f'][0]).reshape(4, 1))
    shared['mlstm_out_norm_g'] = f32(inputs['mlstm_out_norm_g'][0])
    shared['attn_sinks'] = f32(inputs['attn_sinks'][0])
    x = np.asarray(inputs['x'], dtype=np.float32)
    pos = np.asarray(inputs['positions'], dtype=np.int32)
    maps = []
    for c in range(NC_):
        m = dict(shared)
        m['x'] = np.ascontiguousarray(x[c * NSEQ:(c + 1) * NSEQ].reshape(NSEQ * S, DM))
        m['posT'] = np.ascontiguousarray(pos[c * NSEQ:(c + 1) * NSEQ].reshape(-1, 128).T)
        maps.append(m)
    return maps


def run(cfg, inputs, phases=('ffn1', 'mix', 'ffn2'), sim=False, trace=False):
    nc, P = build_program(cfg, phases)
    maps = make_in_maps(cfg, inputs)
    res = run_bass_kernel_spmd(nc, maps, core_ids=list(range(cfg['NCORES']))).results
    out = np.concatenate([r['out'] for r in res], axis=0)
    B = cfg['NSEQ'] * cfg['NCORES']
    return out.reshape(B, cfg['S'], cfg['DM'])


def kernel(**inputs):
    cfg = FULL
    out = run(cfg, inputs)
    return np.ascontiguousarray(out.astype(np.float32))
```
